# Optimizing a Trainium2 kernel written in Bass

```python
import jax, jax.numpy as jnp
from jax import lax
import numpy as np

D_MODEL = 1024
BATCH = 8
SEQ = 4096
DEPTH = 1
DEC_BATCH = 8
DEC_SEQ = 32
PAST_LEN = 4096

CHUNK = 64
HEAD_DIM = 64
A_HEADS = 8
A_WIDTH = A_HEADS * HEAD_DIM
Q_LORA = 256
KV_LORA = 128
NOPE_DIM = HEAD_DIM
ROPE_DIM = 32
ROPE_BASE = 10000.0
MLA_SCALE = (NOPE_DIM + ROPE_DIM) ** -0.5
B_HEADS = 8
B_WIDTH = B_HEADS * HEAD_DIM
BAND_CHUNKS = 8
MAX_REL = 128
N_REL = 2 * MAX_REL + 1
B_SCALE = HEAD_DIM ** -0.5
MIX_WIDTH = A_WIDTH + B_WIDTH
Q_BLOCK = 128
EPS = 1e-6
NEG = -1e30

OFF_CQ = 0
OFF_CKV = OFF_CQ + Q_LORA
OFF_KR = OFF_CKV + KV_LORA
OFF_GA = OFF_KR + ROPE_DIM
OFF_QB = OFF_GA + A_WIDTH
OFF_KB = OFF_QB + B_WIDTH
OFF_VB = OFF_KB + B_WIDTH
OFF_GB = OFF_VB + B_WIDTH
IN_WIDTH = OFF_GB + B_WIDTH

kernel_name = "hybrid_mla_chunkband_stream_step"


def rmsnorm(x, g):
    x32 = x.astype(jnp.float32)
    r = x32 * lax.rsqrt(jnp.mean(x32 * x32, axis=-1, keepdims=True) + EPS)
    return (r * g.astype(jnp.float32)).astype(x.dtype)


def rope(x, pos):
    half = ROPE_DIM // 2
    inv = ROPE_BASE ** (-jnp.arange(half, dtype=jnp.float32) / half)
    ang = pos.astype(jnp.float32)[:, None] * inv
    ang = ang.reshape(ang.shape[0], *([1] * (x.ndim - 3)), half)
    cos, sin = jnp.cos(ang), jnp.sin(ang)
    x1 = x[..., :half].astype(jnp.float32)
    x2 = x[..., half:].astype(jnp.float32)
    return jnp.concatenate([x1 * cos - x2 * sin, x1 * sin + x2 * cos], axis=-1).astype(x.dtype)


def mixer_inputs(xn, pos, w_in, g_cq, w_uq, g_ckv, w_uk):
    b, s, _ = xn.shape
    z = xn @ w_in
    c_q = rmsnorm(z[..., OFF_CQ:OFF_CKV], g_cq)
    q = jnp.einsum('bsr,rhd->bshd', c_q, w_uq)
    q_lat = jnp.einsum('bshd,chd->bshc', q[..., :NOPE_DIM], w_uk)
    q_pe = rope(q[..., NOPE_DIM:], pos)
    ckv = rmsnorm(z[..., OFF_CKV:OFF_KR], g_ckv)
    kpe = rope(z[..., OFF_KR:OFF_GA], pos)
    g_a = z[..., OFF_GA:OFF_QB]
    qb = z[..., OFF_QB:OFF_KB].reshape(b, s, B_HEADS, HEAD_DIM)
    kb = z[..., OFF_KB:OFF_VB].reshape(b, s, B_HEADS, HEAD_DIM)
    vb = z[..., OFF_VB:OFF_GB].reshape(b, s, B_HEADS, HEAD_DIM)
    g_b = z[..., OFF_GB:]
    return q_lat, q_pe, ckv, kpe, g_a, qb, kb, vb, g_b


def mla_attend(q_lat, q_pe, ckv, kpe, mask):
    sc = (jnp.einsum('bqhc,bkc->bhqk', q_lat, ckv)
          + jnp.einsum('bqhr,bkr->bhqk', q_pe, kpe)).astype(jnp.float32) * MLA_SCALE
    if mask is not None:
        sc = jnp.where(mask, sc, NEG)
    p = jax.nn.softmax(sc, axis=-1).astype(ckv.dtype)
    return jnp.einsum('bhqk,bkc->bqhc', p, ckv)


def mla_prompt(q_lat, q_pe, ckv, kpe):
    b, s = q_lat.shape[:2]
    nqb = s // Q_BLOCK
    kchunk = jnp.arange(s) // CHUNK

    def blk(args):
        ql, qp, i = args
        qchunk = (i * Q_BLOCK + jnp.arange(Q_BLOCK)) // CHUNK
        mask = kchunk[None, :] <= qchunk[:, None]
        return mla_attend(ql, qp, ckv, kpe, mask)

    qlb = q_lat.reshape(b, nqb, Q_BLOCK, *q_lat.shape[2:]).swapaxes(0, 1)
    qpb = q_pe.reshape(b, nqb, Q_BLOCK, *q_pe.shape[2:]).swapaxes(0, 1)
    out = lax.map(blk, (qlb, qpb, jnp.arange(nqb)))
    return out.swapaxes(0, 1).reshape(b, s, *out.shape[3:])


def rel_bias_lookup(rel_bias, dist):
    idx = jnp.clip(dist, -MAX_REL, MAX_REL) + MAX_REL
    return rel_bias[:, idx].astype(jnp.float32)


def band_prompt(q, k, v, rel_bias):
    b, s, h, d = q.shape
    nc = s // CHUNK
    w = (BAND_CHUNKS + 1) * CHUNK
    qc = q.reshape(b, nc, CHUNK, h, d)
    pad = ((0, 0), (BAND_CHUNKS, 0), (0, 0), (0, 0), (0, 0))
    kp = jnp.pad(k.reshape(b, nc, CHUNK, h, d), pad)
    vp = jnp.pad(v.reshape(b, nc, CHUNK, h, d), pad)
    kband = jnp.concatenate([kp[:, o:o + nc] for o in range(BAND_CHUNKS + 1)], axis=2)
    vband = jnp.concatenate([vp[:, o:o + nc] for o in range(BAND_CHUNKS + 1)], axis=2)
    sc = jnp.einsum('bnqhd,bnkhd->bnhqk', qc, kband).astype(jnp.float32) * B_SCALE
    a = jnp.arange(CHUNK)[:, None]
    kk = jnp.arange(w)[None, :]
    dist = (BAND_CHUNKS - kk // CHUNK) * CHUNK + a - kk % CHUNK
    sc = sc + rel_bias_lookup(rel_bias, dist)[None, None]
    valid = (jnp.arange(nc)[:, None] - BAND_CHUNKS + (jnp.arange(w) // CHUNK)[None, :]) >= 0
    sc = jnp.where(valid[None, :, None, None, :], sc, NEG)
    p = jax.nn.softmax(sc, axis=-1).astype(v.dtype)
    o = jnp.einsum('bnhqk,bnkhd->bnqhd', p, vband)
    return o.reshape(b, s, h, d)


def band_sample(q, k_new, v_new, kc, vc, rel_bias, past):
    t = q.shape[1]
    kl = kc.shape[1]
    k = jnp.concatenate([kc, k_new], axis=1)
    v = jnp.concatenate([vc, v_new], axis=1)
    qpos = past + jnp.arange(t)
    kpos = jnp.concatenate([past - kl + jnp.arange(kl), past + jnp.arange(t)])
    dist = qpos[:, None] - kpos[None, :]
    sc = jnp.einsum('bqhd,bkhd->bhqk', q, k).astype(jnp.float32) * B_SCALE
    sc = sc + rel_bias_lookup(rel_bias, dist)[None]
    p = jax.nn.softmax(sc, axis=-1).astype(v.dtype)
    return jnp.einsum('bhqk,bkhd->bqhd', p, v)


def merge_heads(o_lat, w_uv, g_a, o_b, g_b, w_out):
    b, s = o_b.shape[:2]
    o_a = jnp.einsum('bshc,chd->bshd', o_lat, w_uv).reshape(b, s, A_WIDTH)
    y = jnp.concatenate([o_a * jax.nn.silu(g_a), o_b.reshape(b, s, B_WIDTH) * jax.nn.silu(g_b)], axis=-1)
    return y @ w_out


def setup_inputs(seed: int = 0) -> dict:
    key = jax.random.key(seed)
    ks = jax.random.split(key, 16)
    kb_len = min(BAND_CHUNKS * CHUNK, PAST_LEN)
    f32 = jnp.float32
    n = lambda k, shape, s=1.0: (jax.random.normal(k, shape, f32) * s)
    return {
        "x_prompt": n(ks[0], (BATCH, SEQ, D_MODEL)),
        "x_sample": n(ks[1], (DEC_BATCH, DEC_SEQ, D_MODEL)),
        "cache_ckv": n(ks[2], (DEPTH, DEC_BATCH, PAST_LEN, KV_LORA)),
        "cache_kpe": n(ks[3], (DEPTH, DEC_BATCH, PAST_LEN, ROPE_DIM)),
        "cache_kb": n(ks[4], (DEPTH, DEC_BATCH, kb_len, B_HEADS, HEAD_DIM)),
        "cache_vb": n(ks[5], (DEPTH, DEC_BATCH, kb_len, B_HEADS, HEAD_DIM)),
        "w_in": n(ks[6], (DEPTH, D_MODEL, IN_WIDTH), D_MODEL ** -0.5),
        "g_mix": 1.0 + n(ks[7], (DEPTH, D_MODEL), 0.02),
        "g_cq": 1.0 + n(ks[8], (DEPTH, Q_LORA), 0.02),
        "w_uq": n(ks[9], (DEPTH, Q_LORA, A_HEADS, NOPE_DIM + ROPE_DIM), Q_LORA ** -0.5),
        "g_ckv": 1.0 + n(ks[10], (DEPTH, KV_LORA), 0.02),
        "w_uk": n(ks[11], (DEPTH, KV_LORA, A_HEADS, NOPE_DIM), KV_LORA ** -0.5),
        "w_uv": n(ks[12], (DEPTH, KV_LORA, A_HEADS, HEAD_DIM), KV_LORA ** -0.5),
        "rel_bias": n(ks[13], (DEPTH, B_HEADS, N_REL), 0.5),
        "w_out": n(ks[14], (DEPTH, MIX_WIDTH, D_MODEL), MIX_WIDTH ** -0.5),
        "g_final": 1.0 + n(ks[15], (D_MODEL,), 0.02),
    }


def reference(x_prompt, x_sample, cache_ckv, cache_kpe, cache_kb, cache_vb,
              w_in, g_mix, g_cq, w_uq, g_ckv, w_uk, w_uv, rel_bias, w_out, g_final):
    s = x_prompt.shape[1]
    t = x_sample.shape[1]
    past = cache_ckv.shape[2]
    kbp = min(BAND_CHUNKS * CHUNK, s)
    pos_p = jnp.arange(s)
    pos_s = past + jnp.arange(t)
    xp, xs = x_prompt, x_sample
    ckv_p, kpe_p, kb_p, vb_p = [], [], [], []
    ckv_s, kpe_s, kb_s, vb_s = [], [], [], []
    for l in range(DEPTH):
        q_lat, q_pe, ckv, kpe, g_a, qb, kb, vb, g_b = mixer_inputs(
            rmsnorm(xp, g_mix[l]), pos_p, w_in[l], g_cq[l], w_uq[l], g_ckv[l], w_uk[l])
        o_lat = mla_prompt(q_lat, q_pe, ckv, kpe)
        o_b = band_prompt(qb, kb, vb, rel_bias[l])
        xp = xp + merge_heads(o_lat, w_uv[l], g_a, o_b, g_b, w_out[l])
        ckv_p.append(ckv); kpe_p.append(kpe)
        kb_p.append(kb[:, s - kbp:]); vb_p.append(vb[:, s - kbp:])
        q_lat, q_pe, ckv, kpe, g_a, qb, kb, vb, g_b = mixer_inputs(
            rmsnorm(xs, g_mix[l]), pos_s, w_in[l], g_cq[l], w_uq[l], g_ckv[l], w_uk[l])
        ckv_all = jnp.concatenate([cache_ckv[l], ckv], axis=1)
        kpe_all = jnp.concatenate([cache_kpe[l], kpe], axis=1)
        o_lat = mla_attend(q_lat, q_pe, ckv_all, kpe_all, None)
        o_b = band_sample(qb, kb, vb, cache_kb[l], cache_vb[l], rel_bias[l], past)
        xs = xs + merge_heads(o_lat, w_uv[l], g_a, o_b, g_b, w_out[l])
        ckv_s.append(ckv); kpe_s.append(kpe); kb_s.append(kb); vb_s.append(vb)
    y_prompt = rmsnorm(xp, g_final)
    y_sample = rmsnorm(xs, g_final)
    return (y_prompt, y_sample,
            jnp.stack(ckv_p), jnp.stack(kpe_p), jnp.stack(kb_p), jnp.stack(vb_p),
            jnp.stack(ckv_s), jnp.stack(kpe_s), jnp.stack(kb_s), jnp.stack(vb_s))
```

```python
import contextlib
import numpy as np
import concourse.bass as bass
import concourse.mybir as mybir
from concourse.bass_utils import run_bass_kernel_spmd

F32 = mybir.dt.float32
BF16 = mybir.dt.bfloat16
I32 = mybir.dt.int32
AF = mybir.ActivationFunctionType
ALU = mybir.AluOpType

D = 1024
SEQ = 4096
TS = 32
PAST = 4096
EPS = 1e-6
MLA_SCALE = 96.0 ** -0.5
B_SCALE = 64.0 ** -0.5
NEGM = -30000.0
DBG_STAGE = 99
DBG_A = 99


class _Stop(Exception):
    pass


def _sec(k):
    if k > DBG_A:
        raise _Stop()
C_CQ, C_GA, C_GB, C_QB, C_KB, C_KR, C_CKV, C_VB, NCOL = 0, 256, 768, 1280, 1792, 2304, 2368, 2496, 3008


class Instr:
    __slots__ = ("eng", "fn", "deps", "signal", "sig_idx", "is_dma", "slot", "dma_val")

    def __init__(self, eng, fn):
        self.eng = eng
        self.fn = fn
        self.deps = []
        self.signal = False
        self.sig_idx = 0
        self.is_dma = False
        self.slot = None
        self.dma_val = 0


class Prog:
    ENGS = ("pe", "act", "dve", "pool", "sp")

    def __init__(self):
        self.q = {e: [] for e in self.ENGS}
        self.last_writer = {}
        self.readers = {}
        self.slot_count = {}
        self.slot_last = {}
        self.n = 0

    def _track(self, ins, reads, writes):
        deps = []
        for k in reads:
            w = self.last_writer.get(k)
            if w is not None:
                deps.append((w, "raw"))
            if isinstance(k, tuple) and k[0] == "ps":
                for r in self.readers.get(k, ()):
                    deps.append((r, "war"))
        for k in writes:
            w = self.last_writer.get(k)
            if w is not None:
                deps.append((w, "waw"))
            for r in self.readers.get(k, ()):
                deps.append((r, "war"))
        for k in writes:
            self.last_writer[k] = ins
            self.readers[k] = []
        for k in reads:
            if k not in writes:
                self.readers.setdefault(k, []).append(ins)
        seen = set()
        for d, kind in deps:
            if d is ins or id(d) in seen:
                continue
            seen.add(id(d))
            ins.deps.append((d, kind))

    def op(self, eng, fn, reads=(), writes=()):
        ins = Instr(eng, fn)
        self.n += 1
        self._track(ins, tuple(reads), tuple(writes))
        self.q[eng].append(ins)
        return ins

    def dma(self, fn, reads=(), writes=(), slot=None, eng="sp"):
        ins = Instr(eng, fn)
        self.n += 1
        ins.is_dma = True
        ins.slot = slot
        self.slot_count[slot] = self.slot_count.get(slot, 0) + 1
        ins.dma_val = 16 * self.slot_count[slot]
        self._track(ins, tuple(reads), tuple(writes))
        prev = self.slot_last.get(slot)
        if prev is not None and all(d is not prev for d, _ in ins.deps):
            ins.deps.append((prev, "raw"))
        self.slot_last[slot] = ins
        self.q[eng].append(ins)
        return ins

    def emit(self, nc):
        for e in self.ENGS:
            for ins in self.q[e]:
                for d, kind in ins.deps:
                    if d.is_dma:
                        continue
                    if d.eng == ins.eng and d.eng in ("pe", "sp"):
                        continue
                    d.signal = True
        for e in self.ENGS:
            c = 0
            for ins in self.q[e]:
                if ins.signal and not ins.is_dma:
                    c += 1
                    ins.sig_idx = c
        slots = sorted(self.slot_count.keys(), key=str)
        with contextlib.ExitStack() as st:
            esem = {e: st.enter_context(nc.semaphore("s_" + e)) for e in self.ENGS}
            ssem = {s: st.enter_context(nc.semaphore("d_%d" % i)) for i, s in enumerate(slots)}
            block = st.enter_context(nc.Block())
            prog = self

            def run(engname, eh):
                waited = {}
                for ins in prog.q[engname]:
                    for d, kind in ins.deps:
                        if d.is_dma:
                            sem, val, key = ssem[d.slot], d.dma_val, ("d", d.slot)
                        else:
                            if d.eng == engname and engname in ("pe", "sp"):
                                continue
                            sem, val, key = esem[d.eng], d.sig_idx, ("e", d.eng)
                        if waited.get(key, 0) >= val:
                            continue
                        waited[key] = val
                        eh.wait_ge(sem, val)
                    h = ins.fn(eh)
                    if ins.is_dma:
                        h.then_inc(ssem[ins.slot], 16)
                    elif ins.signal:
                        h.then_inc(esem[engname], 1)
                if engname == "sp":
                    for s in slots:
                        eh.wait_ge(ssem[s], 16 * prog.slot_count[s])

            @block.tensor
            def _(eh):
                run("pe", eh)

            @block.scalar
            def _(eh):
                run("act", eh)

            @block.vector
            def _(eh):
                run("dve", eh)

            @block.gpsimd
            def _(eh):
                run("pool", eh)

            @block.sync
            def _(eh):
                run("sp", eh)
        return self.n


def build_nc(NBLK=8, SAMPLE=True):
    nc = bass.Bass("TRN2", target_bir_lowering=False, dynamic_dma_scratch_size=1024)

    def din(name, shape):
        return nc.dram_tensor(name, list(shape), F32, kind="ExternalInput").ap()

    def dout(name, shape):
        return nc.dram_tensor(name, list(shape), F32, kind="ExternalOutput").ap()

    x_d = din("x", [SEQ, D])
    xs_d = din("xs", [TS, D])
    cckv_d = din("cckv", [PAST, 128])
    ckpe_d = din("ckpe", [PAST, 32])
    ckb_d = din("ckb", [512, 512])
    cvb_d = din("cvb", [512, 512])
    w_d = din("w_perm", [D, NCOL])
    wout_d = din("w_out", [D, D])
    wuv_d = din("wuv", [128, 512])
    uqn_d = din("uqnT", [64, 8 * 256])
    uk_d = din("ukT", [64, 8 * 128])
    wpe_d = din("wpe", [256, 512])
    gmix_d = din("gmix", [128, 8])
    gcq_d = din("gcq", [128, 2])
    gckv_d = din("gckv_bc", [128, 128])
    gfin_d = din("gfin_bc", [128, D])
    ropec_d = din("ropec", [64, 4])
    rb_d = din("rb", [8, 257])
    ident_d = din("ident", [128, 128])

    y_o = dout("y", [SEQ, D])
    ys_o = dout("ys", [TS, D])
    ckv_o = dout("ckv_o", [SEQ, 128])
    kpe_o = dout("kpe_o", [SEQ, 32])
    kb_o = dout("kb_o", [512, 512])
    vb_o = dout("vb_o", [512, 512])
    ckvs_o = dout("ckvs_o", [TS, 128])
    kpes_o = dout("kpes_o", [TS, 32])
    kbs_o = dout("kbs_o", [TS, 512])
    vbs_o = dout("vbs_o", [TS, 512])
    scr = nc.dram_tensor("scr", [8, 128, 384], F32, kind="Internal").ap()
    tabd = nc.dram_tensor("tabd", [64, SEQ + TS], F32, kind="Internal").ap()

    P = Prog()
    st = contextlib.ExitStack()
    with st:
        def sb(name, shape, dt):
            return st.enter_context(nc.sbuf_tensor("sb_" + name, list(shape), dt))

        def psum(name, shape, dt):
            return st.enter_context(nc.psum_tensor("ps_" + name, list(shape), dt))

        W = sb("W", [128, 8, NCOL], BF16)
        Wout = sb("Wout", [128, 8, D], BF16)
        Wlat = sb("Wlat", [128, 2, 8, 128], BF16)
        Wpe = sb("Wpe", [128, 2, 8, 64], BF16)
        Wuv = sb("Wuv", [128, 512], BF16)
        gmix = sb("gmix", [128, 8], F32)
        gcq = sb("gcq", [128, 2], F32)
        gckv = sb("gckv", [128, 128], F32)
        gfin = sb("gfin", [128, D], F32)
        ropec = sb("ropec", [64, 4], F32)
        identf = sb("identf", [32, 32], F32)
        identb = sb("identb", [128, 128], BF16)
        onesb = sb("onesb", [128, 128], BF16)
        tabblk = sb("tabblk", [64, 1, 512], F32)
        BT = sb("BT", [128, 8, 256], F32)
        cb = sb("cb", [128, 8], F32)
        cbm = sb("cbm", [128, 8], F32)
        KTc = sb("KTc", [128, SEQ], BF16)
        KTp = sb("KTp", [128, SEQ], BF16)
        V = sb("V", [128, 32, 128], BF16)
        kbT = sb("kbT", [128, 4, 2, 512], BF16)
        vbx = sb("vbx", [128, 8, 8, 128], BF16)
        stage = sb("stage", [128, 2, 1024], F32)
        xnb = sb("xnb", [128, D], BF16)
        xnT = sb("xnT", [128, 8, 512], BF16)
        cqT = sb("cqT", [128, 2, 512], BF16)
        sq = sb("sq", [128, 2, 512], BF16)
        rq = sb("rq", [128, 512], F32)
        QlatT = sb("QlatT", [128, 8, 512], BF16)
        QpeT = sb("QpeT", [128, 8, 512], BF16)
        OlatT = QlatT
        gates = sb("gates", [128, 8, 512], BF16)
        qbT = sb("qbT", [128, 4, 2, 512], BF16)
        yT = xnT
        tmpA = sb("tmpA", [128, 2, 512], F32)
        tmpB = sb("tmpB", [128, 2, 512], F32)
        PT = sb("PT", [128, 4, 512], BF16)
        PTb = sb("PTb", [128, 3, 512], BF16)
        sbb = sb("sbb", [128, 2, 512], F32)
        rec = sb("rec", [128, 2, 512], F32)
        small = sb("small", [128, 32], F32)
        acc = sb("acc", [128, 2, 512], F32)
        acch = sb("acch", [128, 2, 512], BF16)
        maskv = sb("maskv", [128, 1], F32)
        ckv32 = sb("ckv32", [128, 2, 128], F32)
        ckvb = sb("ckvb", [128, 128], BF16)
        kpst = sb("kpst", [128, 4, 32], F32)
        KTc_s = sb("KTc_s", [128, 32], BF16)
        KTp_s = sb("KTp_s", [128, 32], BF16)
        V_s = sb("V_s", [32, 128], BF16)
        vbx_s = sb("vbx_s", [32, 8, 128], BF16)
        kbT_s = sb("kbT_s", [128, 4, 32], BF16)

        pb = [psum("pb%d" % i, [128, 512], F32) for i in range(7)]
        pT = psum("pT", [128, 1024], BF16)
        PSK = [("ps", i) for i in range(7)]
        PTK = ("ps", 7)
        pT32 = pT.bitcast(F32)

        def mm(out, lhsT, rhs, start, stop, reads, writes):
            P.op("pe", lambda e: e.matmul(out, lhsT=lhsT, rhs=rhs, start=start, stop=stop), reads, writes)

        def tr(out, in_, ident, reads, writes):
            P.op("pe", lambda e: e.transpose(out=out, in_=in_, identity=ident), reads, writes)

        def act(out, in_, func, reads, writes, **kw):
            P.op("act", lambda e: e.activation(out=out, in_=in_, func=func, **kw), reads, writes)

        def tt(eng, out, in0, in1, op, reads, writes):
            P.op(eng, lambda e: e.tensor_tensor(out=out, in0=in0, in1=in1, op=op), reads, writes)

        def ts(eng, out, in0, s1, s2, op0, op1, reads, writes):
            if op1 is None:
                P.op(eng, lambda e: e.tensor_scalar(out=out, in0=in0, scalar1=s1, scalar2=None, op0=op0), reads, writes)
            else:
                P.op(eng, lambda e: e.tensor_scalar(out=out, in0=in0, scalar1=s1, scalar2=s2, op0=op0, op1=op1), reads, writes)

        def stt(out, in0, scalar, in1, op0, op1, reads, writes):
            P.op("dve", lambda e: e.scalar_tensor_tensor(out=out, in0=in0, scalar=scalar, in1=in1, op0=op0, op1=op1), reads, writes)

        def cp(eng, out, in_, reads, writes):
            if eng == "act":
                act(out, in_, AF.Copy, reads, writes)
            else:
                P.op(eng, lambda e: e.tensor_copy(out=out, in_=in_), reads, writes)

        def mset(eng, ap, val, writes):
            P.op(eng, lambda e: e.memset(ap, val), (), writes)

        def dma(out, in_, reads, writes, slot):
            P.dma(lambda e: e.dma_start(out=out, in_=in_), reads, writes, slot)

        sm_ctr = [0]

        def smcol():
            c = sm_ctr[0] % 32
            sm_ctr[0] += 1
            return c

        def rstd_from_ss(ss_ap, ss_key, dim, n):
            c1, c2 = smcol(), smcol()
            l_ap = small[0:n, c1:c1 + 1]
            r_ap = small[0:n, c2:c2 + 1]
            act(l_ap, ss_ap, AF.Ln, [ss_key], [("sm", c1)], scale=1.0 / dim, bias=EPS)
            act(r_ap, l_ap, AF.Exp, [("sm", c1)], [("sm", c2)], scale=-0.5)
            return r_ap, ("sm", c2)

        TA2 = [("tmpA", 0), ("tmpA", 1)]
        TB2 = [("tmpB", 0), ("tmpB", 1)]
        SB2 = [("sbb", 0), ("sbb", 1)]
        RC2 = [("rec", 0), ("rec", 1)]
        SQ2 = [("sq", 0), ("sq", 1)]
        sbbI = sbb.bitcast(I32)
        kpe32 = rec[0:32, 1, :]
        G = rec[0:8, :, :].rearrange("p a b -> p (a b)")
        cst = tmpA[:, :, :].rearrange("p a b -> p (a b)")
        cstb = sq[:, :, :].rearrange("p a b -> p (a b)")
        dma(gmix[:], gmix_d, [], ["gmix"], "c0")
        dma(gcq[:], gcq_d, [], ["gcq"], "c1")
        dma(ropec[:], ropec_d, [], ["ropec"], "c4")
        dma(G[:, 0:256], rb_d[:, 1:257], [], RC2, "c6")
        dma(stage[:, 0, 0:128], ident_d, [], [("stage", 0)], ("wst", 0))
        dma(gckv[:], gckv_d, [], ["gckv"], "c2")
        dma(gfin[:], gfin_d, [], ["gfin"], "c3")
        ts("dve", G[:, 256:384], G[:, 0:128], 0.0, G[:, 255:256], ALU.mult, ALU.add, RC2, RC2)

        def g2scr(e):
            g = G[:, 0:384]
            src = bass.AP(g.tensor, g.offset, [list(g.ap[0]), [0, 128], [1, 384]])
            return e.dma_start(out=scr, in_=src)
        P.dma(g2scr, RC2, ["scr"], "c7")
        cp("dve", identb[:], stage[:, 0, 0:128], [("stage", 0)], ["identb"])
        cp("dve", identf[:], stage[0:32, 0, 0:32], [("stage", 0)], ["identf"])

        mset("pool", onesb[:], 1.0, ["onesb"])
        mset("pool", KTp[:, :], 0.0, [("KTp", t) for t in range(32)])
        mset("pool", QpeT[:, :, :], 0.0, [("Qp", h) for h in range(8)])
        mset("pool", KTp_s[:, :], 0.0, ["KTp_s"])
        mset("pool", qbT[:, :, :, :], 0.0, [("qbT", p_) for p_ in range(4)])
        mset("pool", vbx[:, :, :, :], 1.0, [("vbx", t) for t in range(8)])
        mset("pool", vbx_s[:, :, :], 1.0, ["vbx_s"])
        mset("pool", maskv[:, :], 0.0, ["maskv"])
        mset("pool", maskv[64:128, :], NEGM, ["maskv"])

        NT = SEQ + TS

        def table_chunk(c0):
            n = min(1024, NT - c0)
            tA = tmpA[0:64, :, :].rearrange("p a b -> p (a b)")[:, 0:n]
            tB = tmpB[0:64, :, :].rearrange("p a b -> p (a b)")[:, 0:n]
            tI = sbbI[0:64, :, :].rearrange("p a b -> p (a b)")[:, 0:n]
            P.op("pool", lambda e, tA=tA, c0=c0, n=n: e.iota(tA, pattern=[[1, n]], base=c0, channel_multiplier=0,
                                                             allow_small_or_imprecise_dtypes=True), (), TA2)
            ts("dve", tB, tA, ropec[:, 0:1], ropec[:, 1:2], ALU.mult, ALU.add, TA2 + ["ropec"], TB2)
            cp("dve", tI, tB, TB2, SB2)
            cp("dve", tA, tI, SB2, TA2)
            tt("dve", tB, tB, tA, ALU.subtract, TA2 + TB2, TB2)
            act(tA, tB, AF.Sin, TB2 + ["ropec"], TA2, scale=ropec[:, 2:3])
            P.dma(lambda e, c0=c0, n=n, tA=tA: e.dma_start(out=tabd[:, c0:c0 + n], in_=tA), TA2, ["tabd"], "c9", eng="act")

        gatesF = gates.bitcast(F32)[:, :, :].rearrange("p a b -> p (a b)")
        xyF = xnT.bitcast(F32)[:, :, :].rearrange("p a b -> p (a b)")
        wslots = [(stage[:, 0, :], [("stage", 0)], ("wst", 0)), (stage[:, 1, :], [("stage", 1)], ("wst", 1)),
                  (gatesF[:, 0:1024], [("gate", g_) for g_ in range(0, 4)], "wst2"),
                  (gatesF[:, 1024:2048], [("gate", g_) for g_ in range(4, 8)], "wst3"),
                  (xyF[:, 0:1024], [("XY", g_, t_) for g_ in range(0, 4) for t_ in range(4)], "wst4"),
                  (xyF[:, 1024:2048], [("XY", g_, t_) for g_ in range(4, 8) for t_ in range(4)], "wst5")]
        pieces = [(0, 1024, "act"), (1024, 2048, "dve"), (2048, NCOL, "act")]
        wi = 0
        for dc in range(8):
            for pi_, (c0, c1, eng) in enumerate(pieces):
                buf, bkeys, bslot = wslots[wi % 6]
                wi += 1
                dma(buf[:, 0:c1 - c0], w_d[dc * 128:(dc + 1) * 128, c0:c1], [], bkeys, bslot)
                if eng == "act":
                    act(W[:, dc, c0:c1], buf[:, 0:c1 - c0], AF.Copy, bkeys + ["gmix"], [("W", dc, pi_)], scale=gmix[:, dc:dc + 1])
                else:
                    ts("dve", W[:, dc, c0:c1], buf[:, 0:c1 - c0], gmix[:, dc:dc + 1], None, ALU.mult, None,
                       bkeys + ["gmix"], [("W", dc, pi_)])
            if dc * 1024 < NT:
                table_chunk(dc * 1024)
        for dc in range(8):
            buf, bkeys, bslot = wslots[wi % 6]
            wi += 1
            dma(buf[:, 0:D], wout_d[dc * 128:(dc + 1) * 128, :], [], bkeys, bslot)
            cp("pool" if dc % 2 else "dve", Wout[:, dc, :], buf[:, 0:D], bkeys, [("Wout", dc)])
        WK = [("W", dc, i) for dc in range(8) for i in range(3)]
        WOK = [("Wout", dc) for dc in range(8)]

        def scr2bt(e):
            src = bass.AP(scr.tensor, scr.offset + 127, [[383, 128], [128 * 384, 8], [1, 256]])
            return e.dma_start(out=BT[:], in_=src)
        P.dma(scr2bt, ["scr"], ["BT"], "c8")
        cp("dve", cb[:, :], BT[:, :, 255], ["BT"], ["cb"])
        cp("dve", cbm[:, :], BT[:, :, 255], ["BT"], ["cbm"])
        mset("pool", cbm[0:64, :], NEGM, ["cbm"])
        mset("pool", BT[64:128, :, 0:64], NEGM, ["BT"])
        act(BT[:, :, :].rearrange("p a b -> p (a b)"), BT[:, :, :].rearrange("p a b -> p (a b)"), AF.Exp, ["BT", "cb", "cbm"], ["BT"])

        dma(stage[:, 0, 0:512], wuv_d, [], [("stage", 0)], ("wst", 0))
        cp("dve", Wuv[:], stage[:, 0, 0:512], [("stage", 0)], ["Wuv"])
        dma(stage[:, 1, 0:1024].rearrange("p (a b) -> p a b", a=2), wpe_d.rearrange("(a p) n -> p a n", p=128), [], [("stage", 1)], ("wst", 1))
        for rc in range(2):
            ts("dve", Wpe[:, rc, :, :].rearrange("p a b -> p (a b)"), stage[:, 1, rc * 512:(rc + 1) * 512], gcq[:, rc:rc + 1], None,
               ALU.mult, None, [("stage", 1), "gcq"], ["Wpe"])
        CQ2 = [("cqT", 0), ("cqT", 1)]
        ukb = sq[0:64, :, :].rearrange("p a b -> p (a b)")
        uqb = cqT[0:64, :, :].rearrange("p a b -> p (a b)")
        dma(stage[0:64, 1, 0:1024], uk_d, [], [("stage", 1)], ("wst", 1))
        cp("dve", ukb, stage[0:64, 1, 0:1024], [("stage", 1)], SQ2)
        for hg in range(2):
            dma(stage[0:64, 0, 0:1024], uqn_d[:, hg * 1024:(hg + 1) * 1024], [], [("stage", 0)], ("wst", 0))
            cp("dve", uqb, stage[0:64, 0, 0:1024], [("stage", 0)], CQ2)
            for rc in range(2):
                bank, bk = pb[rc], PSK[rc]
                for hl in range(4):
                    h = hg * 4 + hl
                    mm(bank[:, hl * 128:(hl + 1) * 128], uqb[:, hl * 256 + rc * 128: hl * 256 + (rc + 1) * 128],
                       ukb[:, h * 128:(h + 1) * 128], True, True, SQ2 + CQ2, [bk])
                ts("dve", Wlat[:, rc, hg * 4:(hg + 1) * 4, :].rearrange("p a b -> p (a b)"), bank[:, :], gcq[:, rc:rc + 1], None,
                   ALU.mult, None, [bk, "gcq"], ["Wlat"])

        bank_rr = [0]

        def nextbank(choices=(0, 1, 6, 2, 3, 4, 5)):
            i = choices[bank_rr[0] % len(choices)]
            bank_rr[0] += 1
            return pb[i], PSK[i]

        xslot = [0]

        def load_x_tile(src_ap, tsz):
            s = xslot[0] % 2
            xslot[0] += 1
            xt = stage[0:tsz, s, 0:D]
            dma(xt, src_ap, [], [("stage", s)], ("wst", s))
            return xt, ("stage", s)

        def XYt(t_):
            return [("XY", fc, t_) for fc in range(8)]

        def XYall(ntile_):
            return [("XY", fc, t_) for fc in range(8) for t_ in range(ntile_)]

        def XYfc(fc, ntile_):
            return [("XY", fc, t_) for t_ in range(ntile_)]
        tab_rr = [0]

        def a_front_tile(xt, xkeys, tsz, tti):
            c = smcol()
            ss = small[0:tsz, c:c + 1]
            act(xnb[0:tsz, :], xt, AF.Square, xkeys, ["xnb", ("sm", c)], accum_out=ss)
            r_ap, rk = rstd_from_ss(ss, ("sm", c), D, tsz)
            act(xnb[0:tsz, :], xt, AF.Copy, xkeys + [rk], ["xnb"], scale=r_ap)
            for dc in range(8):
                tr(pT[:, dc * tsz:(dc + 1) * tsz], xnb[0:tsz, dc * 128:(dc + 1) * 128], identb[0:tsz, 0:tsz],
                   ["xnb", "identb"], [PTK])
            cp("dve", xnT[:, :, tti * tsz:(tti + 1) * tsz], pT[:, 0:8 * tsz].rearrange("p (a b) -> p a b", a=8),
               [PTK], XYt(tti))

        def phase_a(x_src, t0, nt, tsz, tab0, KTc_dst, KTp_dst, V_dst, kv_keys, kb_dst, kb_keys, vbx_dst, vbx_key,
                    ckv_out, kpe_out, kb_out, vb_out, front_done=False):
            ntile = nt // tsz
            tsl = 0
            TKEY = ("tab", tsl)
            dma(tabblk[:, tsl, 0:nt], tabd[:, tab0:tab0 + nt], ["tabd"], [TKEY], ("tabld", tsl))
            cosT = tabblk[0:32, tsl, 0:nt]
            sinT = tabblk[32:64, tsl, 0:nt]
            if not front_done:
                for tti in range(ntile):
                    xt, xk = load_x_tile(x_src[t0 + tti * tsz: t0 + (tti + 1) * tsz, :], tsz)
                    a_front_tile(xt, [xk], tsz, tti)
            XK = XYall(ntile)

            def fm_group(col0, m):
                bank, bk = nextbank()
                for dc in range(8):
                    mm(bank[0:m, 0:nt], W[:, dc, col0:col0 + m], xnT[:, dc, 0:nt], dc == 0, dc == 7, WK + XK, [bk])
                return bank, bk

            _sec(2)
            for rc in range(2):
                bank, bk = fm_group(C_CQ + rc * 128, 128)
                cp("dve", cqT[:, rc, 0:nt], bank[:, 0:nt], [bk], [("cqT", rc)])
                act(sq[:, rc, 0:nt], bank[:, 0:nt], AF.Square, [bk], [("sq", rc)])
            bank, bk = nextbank()
            for rc in range(2):
                mm(bank[:, 0:nt], onesb[:, :], sq[:, rc, 0:nt], rc == 0, rc == 1, ["onesb", ("sq", rc)], [bk])
            act(tmpA[:, 0, 0:nt], bank[:, 0:nt], AF.Ln, [bk], [("tmpA", 0)], scale=1.0 / 256, bias=EPS)
            act(rq[:, 0:nt], tmpA[:, 0, 0:nt], AF.Exp, [("tmpA", 0)], ["rq"], scale=-0.5)
            CQK = [("cqT", 0), ("cqT", 1)]
            _sec(3)
            for h in range(8):
                bank, bk = nextbank()
                for rc in range(2):
                    mm(bank[:, 0:nt], Wlat[:, rc, h, :], cqT[:, rc, 0:nt], rc == 0, rc == 1, ["Wlat"] + CQK, [bk])
                tt("dve", QlatT[:, h, 0:nt], bank[:, 0:nt], rq[:, 0:nt], ALU.mult, [bk, "rq"], [("QO", h, jl) for jl in range(8)])
            _sec(4)
            cosq = tmpB[0:32, 0, 0:nt]
            sinq = tmpB[32:64, 0, 0:nt]
            tt("dve", tmpB[0:64, 0, 0:nt], tabblk[0:64, tsl, 0:nt], rq[0:64, 0:nt], ALU.mult, [TKEY, "rq"], [("tmpB", 0)])
            for hp in range(4):
                bank, bk = nextbank()
                for rc in range(2):
                    mm(bank[:, 0:nt], Wpe[:, rc, 2 * hp:2 * hp + 2, :].rearrange("p a b -> p (a b)"), cqT[:, rc, 0:nt], rc == 0, rc == 1,
                       ["Wpe"] + CQK, [bk])
                for hh in range(2):
                    h = 2 * hp + hh
                    s = hh
                    r0 = 64 * hh
                    tt("dve", sbb[0:32, s, 0:nt], bank[r0:r0 + 32, 0:nt], cosq, ALU.mult, [bk, ("tmpB", 0)], [("sbb", s)])
                    tt("dve", rec[0:32, s, 0:nt], bank[r0 + 32:r0 + 64, 0:nt], sinq, ALU.mult, [bk, ("tmpB", 0)], [("rec", s)])
                    tt("dve", QpeT[0:32, h, 0:nt], sbb[0:32, s, 0:nt], rec[0:32, s, 0:nt], ALU.add, [("sbb", s), ("rec", s)], [("Qp", h)])
            _sec(5)
            for gi in range(8):
                col0 = (C_GA if gi < 4 else C_GB) + (gi % 4) * 128
                bank, bk = fm_group(col0, 128)
                s = gi % 2
                act(tmpA[:, s, 0:nt], bank[:, 0:nt], AF.Exp, [bk], [("tmpA", s)], scale=-1.0)
                act(tmpA[:, s, 0:nt], tmpA[:, s, 0:nt], AF.Ln, [("tmpA", s)], [("tmpA", s)], bias=1.0)
                act(tmpA[:, s, 0:nt], tmpA[:, s, 0:nt], AF.Exp, [("tmpA", s)], [("tmpA", s)], scale=-1.0)
                tt("dve", gates[:, gi, 0:nt], bank[:, 0:nt], tmpA[:, s, 0:nt], ALU.mult, [bk, ("tmpA", s)], [("gate", gi)])
            _sec(6)
            for p_ in range(4):
                bank, bk = fm_group(C_QB + p_ * 128, 128)
                cp("act", qbT[0:64, p_, 0, 0:nt], bank[0:64, 0:nt], [bk], [("qbT", p_)])
                cp("act", qbT[64:128, p_, 1, 0:nt], bank[64:128, 0:nt], [bk], [("qbT", p_)])
            for p_ in range(4):
                bank, bk = fm_group(C_KB + p_ * 128, 128)
                cp("dve", kb_dst(p_), bank[:, 0:nt], [bk], kb_keys(p_))
            _sec(7)
            bank, bk = fm_group(C_KR, 128)
            tt("dve", tmpB[0:32, 1, 0:nt], bank[0:32, 0:nt], cosT, ALU.mult, [bk, TKEY], [("tmpB", 1)])
            tt("dve", sbb[0:32, 0, 0:nt], bank[32:64, 0:nt], sinT, ALU.mult, [bk, TKEY], [("sbb", 0)])
            tt("dve", kpe32[:, 0:nt], tmpB[0:32, 1, 0:nt], sbb[0:32, 0, 0:nt], ALU.add, [("tmpB", 1), ("sbb", 0)], [("rec", 1)])
            cp("dve", KTp_dst, kpe32[:, 0:nt], [("rec", 1)], kv_keys[1])

            def kpe_out_tr():
                bank, bk = nextbank((6,))
                for tti in range(ntile):
                    tr(bank[0:tsz, tti * 32:(tti + 1) * 32], kpe32[0:32, tti * tsz:(tti + 1) * tsz], identf[0:32, 0:32],
                       [("rec", 1), "identf"], [bk])
                cp("act", kpst[0:tsz, 0:ntile, :], bank[0:tsz, 0:ntile * 32].rearrange("p (a b) -> p a b", a=ntile), [bk], ["kpst"])
                dma(kpe_out, kpst[0:tsz, 0:ntile, :], ["kpst"], [], "o_kpe")
            _sec(8)
            pend_tr = [kpe_out_tr]
            for tti in range(ntile):
                tok = slice(tti * tsz, (tti + 1) * tsz)
                bank, bk = nextbank()
                for dc in range(8):
                    mm(bank[0:tsz, 0:128], xnT[:, dc, tok], W[:, dc, C_CKV:C_CKV + 128], dc == 0, dc == 7, WK + XK, [bk])
                c = smcol()
                ss = small[0:tsz, c:c + 1]
                act(ckvb[0:tsz, :], bank[0:tsz, 0:128], AF.Square, [bk], ["ckvb", ("sm", c)], accum_out=ss)
                r_ap, rk = rstd_from_ss(ss, ("sm", c), 128, tsz)
                s = tti % 2
                stt(ckv32[0:tsz, s, :], bank[0:tsz, 0:128], r_ap, gckv[0:tsz, :], ALU.mult, ALU.mult, [bk, rk, "gckv"], [("ckv32", s)])
                dma(ckv_out(tti), ckv32[0:tsz, s, :], [("ckv32", s)], [], ("o_ckv", s))
                cp("act", V_dst(tti), ckv32[0:tsz, s, :], [("ckv32", s)], [kv_keys[2](tti)])

                def do_tr(tti=tti):
                    tr(pT[:, 0:tsz], V_dst(tti), identb[0:tsz, 0:tsz], [kv_keys[2](tti), "identb"], [PTK])
                    cp("dve", KTc_dst(tti), pT[:, 0:tsz], [PTK], [kv_keys[0](tti)])
                bank, bk = nextbank()
                for dc in range(8):
                    mm(bank[0:tsz, :], xnT[:, dc, tok], W[:, dc, C_VB:C_VB + 512], dc == 0, dc == 7, WK + XK, [bk])
                cp("act", vbx_dst(tti)[:, :, 0:64], bank[0:tsz, :].rearrange("p (a b) -> p a b", a=8), [bk], [vbx_key(tti)])
                if vb_out is not None:
                    cp("dve", sbb[0:tsz, 0, :], bank[0:tsz, :], [bk], [("sbb", 0)])
                    dma(vb_out(tti), sbb[0:tsz, 0, :], [("sbb", 0)], [], ("o_kv", 0))
                if kb_out is not None:
                    bank, bk = nextbank()
                    for dc in range(8):
                        mm(bank[0:tsz, :], xnT[:, dc, tok], W[:, dc, C_KB:C_KB + 512], dc == 0, dc == 7, WK + XK, [bk])
                    cp("dve", sbb[0:tsz, 1, :], bank[0:tsz, :], [bk], [("sbb", 1)])
                    dma(kb_out(tti), sbb[0:tsz, 1, :], [("sbb", 1)], [], ("o_kv", 1))
                while pend_tr:
                    pend_tr.pop(0)()
                pend_tr.append(do_tr)
            while pend_tr:
                pend_tr.pop(0)()

        pt_rr = [0]
        sbank_rr = [0]
        ol_rr = [0]
        mla_pend = {"q": []}
        PV_LAG = 2

        def mla_chunk_steps(nq, jl, tiles):
            ncol = 8 * nq
            qcols = slice(jl * nq, (jl + 1) * nq)
            par = ol_rr[0] % 2
            cid = ol_rr[0]
            ol_rr[0] += 1
            psO, kO = pb[2 + par], PSK[2 + par]
            psL, kL = pb[4], PSK[4]
            QK = [("QO", h, jl) for h in range(8)] + [("Qp", h) for h in range(8)]
            nt_ = len(tiles)
            state = mla_pend

            def issue_pv(i, pi):
                ktc, ktp, v, M, keys = tiles[i][:5]
                mm(psO[:, 0:ncol], v, PT[0:M, pi, 0:ncol], i == 0, i == nt_ - 1, [("PT", pi)] + list(keys), [kO])

            def mk(i):
                def step():
                    ktc, ktp, v, M, keys = tiles[i][:5]
                    sbk = (0, 1, 6)[sbank_rr[0] % 3]
                    sbank_rr[0] += 1
                    psS, kS = pb[sbk], PSK[sbk]
                    out_ap = psS[0:M, 0:ncol].rearrange("p (a b) -> p a b", a=8)
                    mm(out_ap, ktc, QlatT[:, :, qcols], True, False, QK + list(keys), [kS])
                    mm(out_ap, ktp, QpeT[:, :, qcols], False, True, QK + list(keys), [kS])
                    pi = pt_rr[0] % 4
                    pt_rr[0] += 1
                    if len(tiles[i]) > 5 and tiles[i][5]:
                        act(PT[0:M, pi, 0:ncol], psS[0:M, 0:ncol], AF.Exp, [kS, "maskv"], [("PT", pi)], scale=MLA_SCALE, bias=maskv[:, 0:1])
                    else:
                        act(PT[0:M, pi, 0:ncol], psS[0:M, 0:ncol], AF.Exp, [kS], [("PT", pi)], scale=MLA_SCALE)
                    e_ = i % 2
                    if i < 2 and M == 128:
                        cp("dve", acc[:, e_, 0:ncol], PT[:, pi, 0:ncol], [("PT", pi)], [("acc", e_)])
                    else:
                        if i < 2:
                            mset("pool", acc[:, e_, 0:ncol], 0.0, [("acc", e_)])
                        tt("dve", acc[0:M, e_, 0:ncol], acc[0:M, e_, 0:ncol], PT[0:M, pi, 0:ncol], ALU.add,
                           [("acc", e_), ("PT", pi)], [("acc", e_)])
                    state["q"].append((cid, lambda i=i, pi=pi: issue_pv(i, pi)))
                    while len(state["q"]) > PV_LAG:
                        state["q"].pop(0)[1]()
                return step

            def fin_a():
                for e_ in range(min(nt_, 2)):
                    cp("dve", acch[:, e_, 0:ncol], acc[:, e_, 0:ncol], [("acc", e_)], [("acch", e_)])

            def fin_b():
                while state["q"] and state["q"][0][0] <= cid:
                    state["q"].pop(0)[1]()
                ne = min(nt_, 2)
                for e_ in range(ne):
                    mm(psL[:, 0:ncol], onesb[:, :], acch[:, e_, 0:ncol], e_ == 0, e_ == ne - 1, ["onesb", ("acch", e_)], [kL])
                act(rec[:, par, 0:ncol], psL[:, 0:ncol], AF.Ln, [kL], [("rec", par)])
                act(rec[:, par, 0:ncol], rec[:, par, 0:ncol], AF.Exp, [("rec", par)], [("rec", par)], scale=-1.0)
                tt("dve", OlatT[:, :, qcols], psO[:, 0:ncol].rearrange("p (a b) -> p a b", a=8),
                   rec[:, par, 0:ncol].rearrange("p (a b) -> p a b", a=8), ALU.mult, [kO, ("rec", par)],
                   [("QO", h, jl) for h in range(8)])

            return [mk(i) for i in range(nt_)] + [fin_a], fin_b

        def mla_steps_for(chunks):
            flat = []
            pending = []
            for (nq, jl, tiles) in chunks:
                steps, fin_b = mla_chunk_steps(nq, jl, tiles)
                for k, stp in enumerate(steps):
                    if pending and k == min(2, len(steps) - 1):
                        flat.append(pending.pop(0))
                    flat.append(stp)
                pending.append(fin_b)

            def flush():
                while mla_pend["q"]:
                    mla_pend["q"].pop(0)[1]()
            flat.append(flush)
            flat.extend(pending)
            return flat

        def merge_a(nt, njl):
            for p_ in range(4):
                bank, bk = nextbank((0, 1))
                for hh in range(2):
                    h = 2 * p_ + hh
                    mm(bank[hh * 64:(hh + 1) * 64, 0:nt], Wuv[:, h * 64:(h + 1) * 64], OlatT[:, h, 0:nt], True, True,
                       ["Wuv"] + [("QO", h, jl) for jl in range(njl)], [bk])
                tt("dve", yT[:, p_, 0:nt], bank[:, 0:nt], gates[:, p_, 0:nt], ALU.mult, [bk, ("gate", p_)], XYfc(p_, max(1, nt // 128)))

        ptb_rr = [0]

        def band_steps_for(nt, heads_tiles):
            items = []
            for (h, tiles) in heads_tiles:
                for i, tl in enumerate(tiles):
                    items.append({"h": h, "t": tl, "first": i == 0, "last": i == len(tiles) - 1})
            N = len(items)

            def stage1(it):
                h = it["h"]
                p_ = h // 2
                pbase = 64 * (h % 2)
                kb_ap, vx_ap, M, q0, q1, qr0, keys = it["t"]
                n = q1 - q0
                mm(pT32[0:M, 0:n], kb_ap, qbT[:, p_, h % 2, q0:q1], True, True, [("qbT", p_)] + list(keys), [PTK])

            def stage2(it):
                h = it["h"]
                kb_ap, vx_ap, M, q0, q1, qr0, keys = it["t"]
                n = q1 - q0
                pi = ptb_rr[0] % 3
                ptb_rr[0] += 1
                it["pi"] = pi
                psS, kS = pT32, PTK
                lo, hi = qr0, qr0 + n
                if lo < 256:
                    a1 = min(hi, 256)
                    c1_ = a1 - lo
                    act(PTb[0:M, pi, 0:c1_], psS[0:M, 0:c1_], AF.Exp, [kS], [("PTb", pi, 0)], scale=B_SCALE)
                    tt("pool", PTb[0:M, pi, 0:c1_], PTb[0:M, pi, 0:c1_], BT[0:M, h, lo:a1], ALU.mult, [("PTb", pi, 0), "BT"], [("PTb", pi, 0)])
                b0, b1 = max(lo, 256), min(hi, 576)
                if b1 > b0:
                    act(PTb[0:M, pi, b0 - lo:b1 - lo], psS[0:M, b0 - lo:b1 - lo], AF.Exp, [kS, "cb"], [("PTb", pi, 1)],
                        scale=B_SCALE, bias=cb[0:M, h:h + 1])
                e0 = max(lo, 576)
                if hi > e0:
                    act(PTb[0:M, pi, e0 - lo:hi - lo], psS[0:M, e0 - lo:hi - lo], AF.Exp, [kS, "cbm"], [("PTb", pi, 2)],
                        scale=B_SCALE, bias=cbm[0:M, h:h + 1])

            def stage3(it):
                h = it["h"]
                kb_ap, vx_ap, M, q0, q1, qr0, keys = it["t"]
                n = q1 - q0
                pi = it["pi"]
                psOb, kOb = pb[5], PSK[5]
                PK = [("PTb", pi, 0), ("PTb", pi, 1), ("PTb", pi, 2)]
                mm(psOb[:, q0:q1], vx_ap, PTb[0:M, pi, 0:n], it["first"], it["last"], PK + list(keys), [kOb])

            def finalize(h):
                p_ = h // 2
                pbase = 64 * (h % 2)
                s = h % 2
                psOb, kOb = pb[5], PSK[5]
                act(tmpA[64:128, s, 0:nt], psOb[64:128, 0:nt], AF.Ln, [kOb], [("tmpA", s)])
                act(tmpA[64:128, s, 0:nt], tmpA[64:128, s, 0:nt], AF.Exp, [("tmpA", s)], [("tmpA", s)], scale=-1.0)
                tt("dve", tmpB[pbase:pbase + 64, s, 0:nt], psOb[0:64, 0:nt], tmpA[64:128, s, 0:nt], ALU.mult, [kOb, ("tmpA", s)], [("tmpB", s)])
                tt("pool", yT[pbase:pbase + 64, 4 + p_, 0:nt], tmpB[pbase:pbase + 64, s, 0:nt], gates[pbase:pbase + 64, 4 + p_, 0:nt],
                   ALU.mult, [("tmpB", s), ("gate", 4 + p_)], XYfc(4 + p_, max(1, nt // 128)))

            out = []
            for k in range(N + 3):
                def pre(k=k):
                    if 0 <= k - 1 < N:
                        stage2(items[k - 1])
                    if 0 <= k - 3 < N:
                        stage3(items[k - 3])

                def post(k=k):
                    if 0 <= k - 3 < N and items[k - 3]["last"]:
                        finalize(items[k - 3]["h"])
                    if k < N:
                        stage1(items[k])
                out.append((pre, post))
            return out

        def run_interleaved(mla_flat, band_flat):
            nm, nb = len(mla_flat), len(band_flat)
            mi = 0
            for k in range(nb):
                band_flat[k][0]()
                target = ((k + 1) * nm) // nb
                while mi < target:
                    mla_flat[mi]()
                    mi += 1
                band_flat[k][1]()
            while mi < nm:
                mla_flat[mi]()
                mi += 1

        def d_buffers(tsz):
            return [(stage[0:tsz, 0, 0:D], [("stage", 0)], ("wst", 0)),
                    (stage[0:tsz, 1, 0:D], [("stage", 1)], ("wst", 1)),
                    (sbb[0:tsz, :, :].rearrange("p a b -> p (a b)"), [("sbb", 0), ("sbb", 1)], "xd2"),
                    (rec[0:tsz, :, :].rearrange("p a b -> p (a b)"), [("rec", 0), ("rec", 1)], "xd3")]

        def d_prefetch(x_src, t0, tsz, which):
            bufs = d_buffers(tsz)
            out = {}
            for i in which:
                buf, keys, slot = bufs[i]
                dma(buf, x_src[t0 + i * tsz: t0 + (i + 1) * tsz, :], [], keys, slot)
                out[i] = True
            return out

        def d_tile_a(tti, tsz):
            bufs = d_buffers(tsz)
            tok = slice(tti * tsz, (tti + 1) * tsz)
            xt, xkeys, slot = bufs[tti]
            banks = [nextbank((0, 1, 2, 3)), nextbank((0, 1, 2, 3))]
            for half in range(2):
                bank, bk = banks[half]
                for fc in range(8):
                    mm(bank[0:tsz, :], yT[:, fc, tok], Wout[:, fc, half * 512:(half + 1) * 512], fc == 0, fc == 7, XYt(tti) + WOK, [bk])
            for half in range(2):
                bank, bk = banks[half]
                tt("dve", xt[:, half * 512:(half + 1) * 512], bank[0:tsz, :], xt[:, half * 512:(half + 1) * 512], ALU.add, [bk] + xkeys, xkeys)

        def d_tile_b(tti, tsz, y_out):
            bufs = d_buffers(tsz)
            xt, xkeys, slot = bufs[tti]
            c = smcol()
            ss = small[0:tsz, c:c + 1]
            act(xnb[0:tsz, :], xt, AF.Square, xkeys, ["xnb", ("sm", c)], accum_out=ss)
            r_ap, rk = rstd_from_ss(ss, ("sm", c), D, tsz)
            stt(xt, xt, r_ap, gfin[0:tsz, :], ALU.mult, ALU.mult, xkeys + [rk, "gfin"], xkeys)
            dma(y_out(tti), xt, xkeys, [], ("o_y", tti))

        def d_tile(tti, tsz, y_out):
            d_tile_a(tti, tsz)
            d_tile_b(tti, tsz, y_out)

        def phase_d(x_src, t0, nt, tsz, y_out, pre=None):
            ntile = nt // tsz
            bufs = d_buffers(tsz)
            pre = pre or {}
            for tti in range(ntile):
                if tti not in pre:
                    buf, keys, slot = bufs[tti]
                    dma(buf, x_src[t0 + tti * tsz: t0 + (tti + 1) * tsz, :], [], keys, slot)
            for tti in range(ntile):
                d_tile(tti, tsz, y_out)

        if SAMPLE:
            cst3 = cst.rearrange("p (t c) -> p t c", c=128)
            for g in range(4):
                dma(cst3, cckv_d[g * 1024:(g + 1) * 1024, :].rearrange("(t p) c -> p t c", p=128), [], TA2, "cin")
                cp("pool" if g % 2 else "dve", V[:, g * 8:(g + 1) * 8, :], cst3, TA2, [("V", g * 8 + i) for i in range(8)])
                for i in range(8):
                    t = g * 8 + i
                    tr(pT[:, i * 128:(i + 1) * 128], V[:, t, :], identb[:], [("V", t), "identb"], [PTK])
                cp("act", KTc[:, g * 1024:(g + 1) * 1024], pT[:, :], [PTK], [("KTc", g * 8 + i) for i in range(8)])
            for g in range(4):
                cv = cst[:, 0:256].rearrange("p (t r) -> p t r", r=32)
                cvb_ = cstb[:, 0:256].rearrange("p (t r) -> p t r", r=32)
                dma(cv, ckpe_d[g * 1024:(g + 1) * 1024, :].rearrange("(t p) r -> p t r", p=128), [], TA2, "cin")
                cp("dve", cvb_, cv, TA2, SQ2)
                for i in range(8):
                    tr(pT[0:32, i * 128:(i + 1) * 128], cvb_[:, i, :], identb[:], SQ2 + ["identb"], [PTK])
                cp("act", KTp[0:32, g * 1024:(g + 1) * 1024], pT[0:32, :], [PTK], [("KTp", g * 8 + i) for i in range(8)])
            for g in range(2):
                cv = cst.rearrange("p (t n) -> p t n", n=512)
                cvb_ = cstb.rearrange("p (t n) -> p t n", n=512)
                dma(cv, ckb_d[g * 256:(g + 1) * 256, :].rearrange("(t p) n -> p t n", p=128), [], TA2, "cin")
                cp("dve", cvb_, cv, TA2, SQ2)
                for i in range(2):
                    tmi = g * 2 + i
                    for p_ in range(4):
                        tr(pT[:, p_ * 128:(p_ + 1) * 128], cvb_[:, i, p_ * 128:(p_ + 1) * 128], identb[:], SQ2 + ["identb"], [PTK])
                    cp("act", kbT[:, :, 1, tmi * 128:(tmi + 1) * 128], pT[:, 0:512].rearrange("p (a b) -> p a b", a=4), [PTK],
                       [("kbT", 1, p_, tmi) for p_ in range(4)])
            for g in range(2):
                cv = cst.rearrange("p (t n) -> p t n", n=512)
                dma(cv, cvb_d[g * 256:(g + 1) * 256, :].rearrange("(t p) n -> p t n", p=128), [], TA2, "cin")
                for i in range(2):
                    tmi = g * 2 + i
                    cp("dve", vbx[:, 4 + tmi, :, 0:64], cv[:, i, :].rearrange("p (a b) -> p a b", a=8), TA2, [("vbx", 4 + tmi)])

            phase_a(xs_d, 0, TS, TS, SEQ,
                    KTc_dst=lambda tti: KTc_s[:, 0:TS], KTp_dst=KTp_s[0:32, 0:TS], V_dst=lambda tti: V_s[0:TS, :],
                    kv_keys=(lambda tti: "KTc_s", ["KTp_s"], lambda tti: "V_s"),
                    kb_dst=lambda p_: kbT_s[:, p_, 0:TS], kb_keys=lambda p_: [("kbT_s", p_)],
                    vbx_dst=lambda tti: vbx_s[0:TS, :, :], vbx_key=lambda tti: "vbx_s",
                    ckv_out=lambda tti: ckvs_o, kpe_out=kpes_o.rearrange("(t p) r -> p t r", p=TS),
                    kb_out=lambda tti: kbs_o, vb_out=lambda tti: vbs_o)
            tiles = []
            for kt in range(32):
                tiles.append((KTc[:, kt * 128:(kt + 1) * 128], KTp[:, kt * 128:(kt + 1) * 128], V[:, kt, :], 128,
                              [("KTc", kt), ("KTp", kt), ("V", kt)]))
            tiles.append((KTc_s[:, 0:TS], KTp_s[:, 0:TS], V_s[0:TS, :], TS, ["KTc_s", "KTp_s", "V_s"]))
            mla_flat = mla_steps_for([(TS, 0, tiles)])
            heads_tiles = []
            for h in range(8):
                p_ = h // 2
                pbase = 64 * (h % 2)
                tiles = []
                for tmi in range(4):
                    tiles.append((kbT[:, p_, 1, tmi * 128:(tmi + 1) * 128], vbx[:, 4 + tmi, h, :], 128, 0, TS,
                                  512 - 128 * tmi, [("kbT", 1, p_, tmi), ("vbx", 4 + tmi)]))
                tiles.append((kbT_s[:, p_, 0:TS], vbx_s[0:TS, h, :], TS, 0, TS, 0, [("kbT_s", p_), "vbx_s"]))
                heads_tiles.append((h, tiles))
            run_interleaved(mla_flat, band_steps_for(TS, heads_tiles))
            merge_a(TS, 1)
            phase_d(xs_d, 0, TS, TS, lambda tti: ys_o)

        for b in range(NBLK if DBG_STAGE >= 1 else 0):
            t0 = 512 * b
            slot = b % 2
            last = (b == SEQ // 512 - 1)
            try:
              phase_a(x_d, t0, 512, 128, t0,
                    KTc_dst=lambda tti, t0=t0: KTc[:, t0 + tti * 128: t0 + (tti + 1) * 128],
                    KTp_dst=KTp[0:32, t0:t0 + 512],
                    V_dst=lambda tti, b=b: V[:, 4 * b + tti, :],
                    kv_keys=(lambda tti, b=b: ("KTc", 4 * b + tti), [("KTp", 4 * b + i) for i in range(4)], lambda tti, b=b: ("V", 4 * b + tti)),
                    kb_dst=lambda p_, slot=slot: kbT[:, p_, slot, :],
                    kb_keys=lambda p_, slot=slot: [("kbT", slot, p_, tmi) for tmi in range(4)],
                    vbx_dst=lambda tti, slot=slot: vbx[:, 4 * slot + tti, :, :],
                    vbx_key=lambda tti, slot=slot: ("vbx", 4 * slot + tti),
                    ckv_out=lambda tti, t0=t0: ckv_o[t0 + tti * 128: t0 + (tti + 1) * 128, :],
                    kpe_out=kpe_o[t0:t0 + 512, :].rearrange("(t p) r -> p t r", p=128),
                    kb_out=(lambda tti: kb_o[tti * 128:(tti + 1) * 128, :]) if last else None,
                    vb_out=(lambda tti: vb_o[tti * 128:(tti + 1) * 128, :]) if last else None,
                    front_done=(b > 0 and DBG_STAGE >= 4))
            except _Stop:
                pass
            pre = d_prefetch(x_d, t0, 128, [0, 1, 2]) if DBG_STAGE >= 4 else None
            chunks = []
            for jl in range(8 if DBG_STAGE >= 2 else 0):
                j = 8 * b + jl
                tiles = []
                for kt in range(j // 2 + 1):
                    half = (kt == j // 2 and j % 2 == 0)
                    tiles.append((KTc[:, kt * 128:(kt + 1) * 128], KTp[:, kt * 128:(kt + 1) * 128], V[:, kt, :], 128,
                                  [("KTc", kt), ("KTp", kt), ("V", kt)], half))
                chunks.append((64, jl, tiles))
            mla_flat = mla_steps_for(chunks)
            ms = [m for m in range(4 * b - 4, 4 * b + 4) if m >= 0]
            first = 4 * b - 1 if b > 0 else 0
            ms = [first] + [m for m in ms if m != first]
            heads_tiles = []
            for h in range(8 if DBG_STAGE >= 3 else 0):
                p_ = h // 2
                pbase = 64 * (h % 2)
                tiles = []
                for m in ms:
                    bm, tmi = m // 4, m % 4
                    sl = bm % 2
                    c_lo = max(2 * m, 8 * b)
                    c_hi = min(2 * m + 9, 8 * b + 7)
                    q0 = (c_lo - 8 * b) * 64
                    q1 = (c_hi - 8 * b + 1) * 64
                    qr0 = (c_lo - 2 * m) * 64
                    tiles.append((kbT[:, p_, sl, tmi * 128:(tmi + 1) * 128], vbx[:, 4 * sl + tmi, h, :], 128, q0, q1, qr0,
                                  [("kbT", sl, p_, tmi), ("vbx", 4 * sl + tmi)]))
                heads_tiles.append((h, tiles))
            run_interleaved(mla_flat, band_steps_for(512, heads_tiles))
            if DBG_STAGE >= 2:
                merge_a(512, 8)
            if DBG_STAGE >= 4:
                nxt = (b + 1 < NBLK)
                a_bufs = [(tmpA[:, :, :].rearrange("p a b -> p (a b)"), TA2, "xa0"), (tmpB[:, :, :].rearrange("p a b -> p (a b)"), TB2, "xa1"),
                          (stage[:, 0, 0:D], [("stage", 0)], ("wst", 0)), (stage[:, 1, 0:D], [("stage", 1)], ("wst", 1))]
                t1 = t0 + 512
                if 3 not in pre:
                    buf, keys, slot = d_buffers(128)[3]
                    dma(buf, x_d[t0 + 384: t0 + 512, :], [], keys, slot)
                if nxt:
                    for i_ in range(2):
                        buf, keys, slot = a_bufs[i_]
                        dma(buf, x_d[t1 + i_ * 128: t1 + (i_ + 1) * 128, :], [], keys, slot)
                yo = lambda tti, t0=t0: y_o[t0 + tti * 128: t0 + (tti + 1) * 128, :]
                for tti in range(5):
                    if tti < 4:
                        d_tile_a(tti, 128)
                    if nxt and tti >= 1:
                        buf, keys, slot = a_bufs[tti - 1]
                        a_front_tile(buf, keys, 128, tti - 1)
                    if tti < 4:
                        d_tile_b(tti, 128, yo)
                        if nxt and tti < 2:
                            buf, keys, slot = a_bufs[2 + tti]
                            dma(buf, x_d[t1 + (2 + tti) * 128: t1 + (3 + tti) * 128, :], [], keys, slot)

        n = P.emit(nc)
    return nc, n


def _prep_shared(w_in, g_mix, g_cq, w_uq, g_ckv, w_uk, w_uv, rel_bias, w_out, g_final):
    f = np.float32
    wi = np.asarray(w_in[0], f)
    perm = np.r_[16:32, 0:16]
    kr = wi[:, 384:416]
    cols = [wi[:, 0:256], wi[:, 416:928], wi[:, 2464:2976], wi[:, 928:1440], wi[:, 1440:1952], kr, kr[:, perm],
            wi[:, 256:384], wi[:, 1952:2464]]
    w_perm = np.ascontiguousarray(np.concatenate(cols, axis=1))
    assert w_perm.shape == (D, NCOL)
    wuq = np.asarray(w_uq[0], f)
    uqnT = np.ascontiguousarray(wuq[:, :, :64].transpose(2, 1, 0)).reshape(64, 8 * 256)
    ukT = np.ascontiguousarray(np.asarray(w_uk[0], f).transpose(2, 1, 0)).reshape(64, 8 * 128)
    pe = wuq[:, :, 64:96]
    wpe = np.ascontiguousarray(np.concatenate([pe, pe[:, :, perm]], axis=2)).reshape(256, 512)
    half = 16
    inv = (10000.0 ** (-np.arange(half, dtype=np.float64) / half))
    ropec = np.zeros((64, 4), np.float64)
    for r in range(32):
        ropec[r, 0] = inv[r % 16] / (2 * np.pi)
        ropec[r, 1] = 0.25
        ropec[r, 2] = 2 * np.pi
        ropec[32 + r, 0] = inv[r % 16] / (2 * np.pi)
        ropec[32 + r, 1] = 0.0
        ropec[32 + r, 2] = (-2 * np.pi) if r < 16 else (2 * np.pi)
    return {
        "w_perm": w_perm,
        "w_out": np.ascontiguousarray(np.asarray(w_out[0], f)),
        "wuv": np.ascontiguousarray(np.asarray(w_uv[0], f).reshape(128, 512)),
        "uqnT": uqnT, "ukT": ukT, "wpe": wpe,
        "gmix": np.ascontiguousarray(np.asarray(g_mix[0], f).reshape(8, 128).T),
        "gcq": np.ascontiguousarray(np.asarray(g_cq[0], f).reshape(2, 128).T),
        "gckv_bc": np.ascontiguousarray(np.broadcast_to(np.asarray(g_ckv[0], f)[None, :], (128, 128))),
        "gfin_bc": np.ascontiguousarray(np.broadcast_to(np.asarray(g_final, f)[None, :], (128, D))),
        "ropec": ropec.astype(f),
        "rb": np.ascontiguousarray(np.asarray(rel_bias[0], f)),
        "ident": np.eye(128, dtype=f),
    }


_NC_CACHE = {}


def kernel(x_prompt, x_sample, cache_ckv, cache_kpe, cache_kb, cache_vb, w_in, g_mix, g_cq, w_uq, g_ckv, w_uk, w_uv,
           rel_bias, w_out, g_final, _nblk=8, _sample=True):
    f = np.float32
    shared = _prep_shared(w_in, g_mix, g_cq, w_uq, g_ckv, w_uk, w_uv, rel_bias, w_out, g_final)
    key = (_nblk, _sample)
    if key not in _NC_CACHE:
        _NC_CACHE[key] = build_nc(_nblk, _sample)[0]
    nc = _NC_CACHE[key]
    in_maps = []
    for i in range(8):
        m = dict(shared)
        m["x"] = np.ascontiguousarray(np.asarray(x_prompt[i], f))
        m["xs"] = np.ascontiguousarray(np.asarray(x_sample[i], f))
        m["cckv"] = np.ascontiguousarray(np.asarray(cache_ckv[0, i], f))
        m["ckpe"] = np.ascontiguousarray(np.asarray(cache_kpe[0, i], f))
        m["ckb"] = np.ascontiguousarray(np.asarray(cache_kb[0, i], f).reshape(512, 512))
        m["cvb"] = np.ascontiguousarray(np.asarray(cache_vb[0, i], f).reshape(512, 512))
        in_maps.append(m)
    res = run_bass_kernel_spmd(nc, in_maps, core_ids=list(range(8)))
    R = res.results
    st = lambda k: np.stack([np.asarray(R[i][k], f) for i in range(8)], axis=0)
    y = st("y")
    ys = st("ys")
    return (y, ys,
            st("ckv_o")[None], st("kpe_o")[None],
            st("kb_o").reshape(8, 512, 8, 64)[None], st("vb_o").reshape(8, 512, 8, 64)[None],
            st("ckvs_o")[None], st("kpes_o")[None],
            st("kbs_o").reshape(8, TS, 8, 64)[None], st("vbs_o").reshape(8, TS, 8, 64)[None])
```

```python
import contextlib
import numpy as np
import concourse.bass as bass
import concourse.mybir as mybir
from concourse.bass_utils import run_bass_kernel_spmd

F32 = mybir.dt.float32
BF16 = mybir.dt.bfloat16
I32 = mybir.dt.int32
AF = mybir.ActivationFunctionType
ALU = mybir.AluOpType

D = 1024
SEQ = 4096
TS = 32
PAST = 4096
EPS = 1e-6
MLA_SCALE = 96.0 ** -0.5
B_SCALE = 64.0 ** -0.5
NEGM = -30000.0
DBG_STAGE = 99
DBG_A = 99


class _Stop(Exception):
    pass


def _sec(k):
    if k > DBG_A:
        raise _Stop()
C_CQ, C_GA, C_GB, C_QB, C_KB, C_KR, C_CKV, C_VB, NCOL = 0, 256, 768, 1280, 1792, 2304, 2368, 2496, 3008


class Instr:
    __slots__ = ("eng", "fn", "deps", "signal", "sig_idx", "is_dma", "slot", "dma_val")

    def __init__(self, eng, fn):
        self.eng = eng
        self.fn = fn
        self.deps = []
        self.signal = False
        self.sig_idx = 0
        self.is_dma = False
        self.slot = None
        self.dma_val = 0


class Prog:
    ENGS = ("pe", "act", "dve", "pool", "sp")

    def __init__(self):
        self.q = {e: [] for e in self.ENGS}
        self.last_writer = {}
        self.readers = {}
        self.slot_count = {}
        self.slot_last = {}
        self.n = 0

    def _track(self, ins, reads, writes):
        deps = []
        for k in reads:
            w = self.last_writer.get(k)
            if w is not None:
                deps.append((w, "raw"))
            if isinstance(k, tuple) and k[0] == "ps":
                for r in self.readers.get(k, ()):
                    deps.append((r, "war"))
        for k in writes:
            w = self.last_writer.get(k)
            if w is not None:
                deps.append((w, "waw"))
            for r in self.readers.get(k, ()):
                deps.append((r, "war"))
        for k in writes:
            self.last_writer[k] = ins
            self.readers[k] = []
        for k in reads:
            if k not in writes:
                self.readers.setdefault(k, []).append(ins)
        seen = set()
        for d, kind in deps:
            if d is ins or id(d) in seen:
                continue
            seen.add(id(d))
            ins.deps.append((d, kind))

    def op(self, eng, fn, reads=(), writes=()):
        ins = Instr(eng, fn)
        self.n += 1
        self._track(ins, tuple(reads), tuple(writes))
        self.q[eng].append(ins)
        return ins

    def dma(self, fn, reads=(), writes=(), slot=None, eng="sp"):
        ins = Instr(eng, fn)
        self.n += 1
        ins.is_dma = True
        ins.slot = slot
        self.slot_count[slot] = self.slot_count.get(slot, 0) + 1
        ins.dma_val = 16 * self.slot_count[slot]
        self._track(ins, tuple(reads), tuple(writes))
        prev = self.slot_last.get(slot)
        if prev is not None and all(d is not prev for d, _ in ins.deps):
            ins.deps.append((prev, "raw"))
        self.slot_last[slot] = ins
        self.q[eng].append(ins)
        return ins

    def emit(self, nc):
        for e in self.ENGS:
            for ins in self.q[e]:
                for d, kind in ins.deps:
                    if d.is_dma:
                        continue
                    if d.eng == ins.eng and d.eng in ("pe", "sp"):
                        continue
                    d.signal = True
        for e in self.ENGS:
            c = 0
            for ins in self.q[e]:
                if ins.signal and not ins.is_dma:
                    c += 1
                    ins.sig_idx = c
        slots = sorted(self.slot_count.keys(), key=str)
        with contextlib.ExitStack() as st:
            esem = {e: st.enter_context(nc.semaphore("s_" + e)) for e in self.ENGS}
            ssem = {s: st.enter_context(nc.semaphore("d_%d" % i)) for i, s in enumerate(slots)}
            block = st.enter_context(nc.Block())
            prog = self

            def run(engname, eh):
                waited = {}
                for ins in prog.q[engname]:
                    for d, kind in ins.deps:
                        if d.is_dma:
                            sem, val, key = ssem[d.slot], d.dma_val, ("d", d.slot)
                        else:
                            if d.eng == engname and engname in ("pe", "sp"):
                                continue
                            sem, val, key = esem[d.eng], d.sig_idx, ("e", d.eng)
                        if waited.get(key, 0) >= val:
                            continue
                        waited[key] = val
                        eh.wait_ge(sem, val)
                    h = ins.fn(eh)
                    if ins.is_dma:
                        h.then_inc(ssem[ins.slot], 16)
                    elif ins.signal:
                        h.then_inc(esem[engname], 1)
                if engname == "sp":
                    for s in slots:
                        eh.wait_ge(ssem[s], 16 * prog.slot_count[s])

            @block.tensor
            def _(eh):
                run("pe", eh)

            @block.scalar
            def _(eh):
                run("act", eh)

            @block.vector
            def _(eh):
                run("dve", eh)

            @block.gpsimd
            def _(eh):
                run("pool", eh)

            @block.sync
            def _(eh):
                run("sp", eh)
        return self.n


def build_nc(NBLK=8, SAMPLE=True):
    nc = bass.Bass("TRN2", target_bir_lowering=False, dynamic_dma_scratch_size=1024)

    def din(name, shape):
        return nc.dram_tensor(name, list(shape), F32, kind="ExternalInput").ap()

    def dout(name, shape):
        return nc.dram_tensor(name, list(shape), F32, kind="ExternalOutput").ap()

    x_d = din("x", [SEQ, D])
    xs_d = din("xs", [TS, D])
    cckv_d = din("cckv", [PAST, 128])
    ckpe_d = din("ckpe", [PAST, 32])
    ckb_d = din("ckb", [512, 512])
    cvb_d = din("cvb", [512, 512])
    w_d = din("w_perm", [D, NCOL])
    wout_d = din("w_out", [D, D])
    wuv_d = din("wuv", [128, 512])
    uqn_d = din("uqnT", [64, 8 * 256])
    uk_d = din("ukT", [64, 8 * 128])
    wpe_d = din("wpe", [256, 512])
    gmix_d = din("gmix", [128, 8])
    gcq_d = din("gcq", [128, 2])
    gckv_d = din("gckv_bc", [128, 128])
    gfin_d = din("gfin_bc", [128, D])
    ropec_d = din("ropec", [64, 4])
    rb_d = din("rb", [8, 257])
    ident_d = din("ident", [128, 128])

    y_o = dout("y", [SEQ, D])
    ys_o = dout("ys", [TS, D])
    ckv_o = dout("ckv_o", [SEQ, 128])
    kpe_o = dout("kpe_o", [SEQ, 32])
    kb_o = dout("kb_o", [512, 512])
    vb_o = dout("vb_o", [512, 512])
    ckvs_o = dout("ckvs_o", [TS, 128])
    kpes_o = dout("kpes_o", [TS, 32])
    kbs_o = dout("kbs_o", [TS, 512])
    vbs_o = dout("vbs_o", [TS, 512])
    scr = nc.dram_tensor("scr", [8, 128, 384], F32, kind="Internal").ap()
    tabd = nc.dram_tensor("tabd", [64, SEQ + TS], F32, kind="Internal").ap()

    P = Prog()
    st = contextlib.ExitStack()
    with st:
        def sb(name, shape, dt):
            return st.enter_context(nc.sbuf_tensor("sb_" + name, list(shape), dt))

        def psum(name, shape, dt):
            return st.enter_context(nc.psum_tensor("ps_" + name, list(shape), dt))

        W = sb("W", [128, 8, NCOL], BF16)
        Wout = sb("Wout", [128, 8, D], BF16)
        Wlat = sb("Wlat", [128, 2, 8, 128], BF16)
        Wpe = sb("Wpe", [128, 2, 8, 64], BF16)
        Wuv = sb("Wuv", [128, 512], BF16)
        gmix = sb("gmix", [128, 8], F32)
        gcq = sb("gcq", [128, 2], F32)
        gckv = sb("gckv", [128, 128], F32)
        gfin = sb("gfin", [128, D], F32)
        ropec = sb("ropec", [64, 4], F32)
        identf = sb("identf", [32, 32], F32)
        identb = sb("identb", [128, 128], BF16)
        onesb = sb("onesb", [128, 128], BF16)
        tabblk = sb("tabblk", [64, 1, 512], F32)
        BT = sb("BT", [128, 8, 256], F32)
        cb = sb("cb", [128, 8], F32)
        cbm = sb("cbm", [128, 8], F32)
        KTc = sb("KTc", [128, SEQ], BF16)
        KTp = sb("KTp", [128, SEQ], BF16)
        V = sb("V", [128, 32, 128], BF16)
        kbT = sb("kbT", [128, 4, 2, 512], BF16)
        vbx = sb("vbx", [128, 8, 8, 128], BF16)
        stage = sb("stage", [128, 2, 1024], F32)
        xnb = sb("xnb", [128, D], BF16)
        xnT = sb("xnT", [128, 8, 512], BF16)
        cqT = sb("cqT", [128, 2, 512], BF16)
        sq = sb("sq", [128, 2, 512], BF16)
        rq = sb("rq", [128, 512], F32)
        QlatT = sb("QlatT", [128, 8, 512], BF16)
        QpeT = sb("QpeT", [128, 8, 512], BF16)
        OlatT = QlatT
        gates = sb("gates", [128, 8, 512], BF16)
        qbT = sb("qbT", [128, 4, 2, 512], BF16)
        yT = xnT
        tmpA = sb("tmpA", [128, 2, 512], F32)
        tmpB = sb("tmpB", [128, 2, 512], F32)
        PT = sb("PT", [128, 4, 512], BF16)
        PTb = sb("PTb", [128, 2, 512], BF16)
        sbb = sb("sbb", [128, 2, 512], F32)
        rec = sb("rec", [128, 2, 512], F32)
        small = sb("small", [128, 64], F32)
        acc = sb("acc", [128, 2, 512], F32)
        acch = sb("acch", [128, 2, 512], BF16)
        maskv = sb("maskv", [128, 1], F32)
        ckv32 = sb("ckv32", [128, 2, 128], F32)
        ckvb = sb("ckvb", [128, 128], BF16)
        kpst = sb("kpst", [128, 4, 32], F32)
        KTc_s = sb("KTc_s", [128, 32], BF16)
        KTp_s = sb("KTp_s", [128, 32], BF16)
        V_s = sb("V_s", [32, 128], BF16)
        vbx_s = sb("vbx_s", [32, 8, 128], BF16)
        kbT_s = sb("kbT_s", [128, 4, 32], BF16)

        pb = [psum("pb%d" % i, [128, 512], F32) for i in range(7)]
        pT = psum("pT", [128, 1024], BF16)
        PSK = [("ps", i) for i in range(7)]
        PTK = ("ps", 7)
        pT32 = pT.bitcast(F32)

        def mm(out, lhsT, rhs, start, stop, reads, writes):
            P.op("pe", lambda e: e.matmul(out, lhsT=lhsT, rhs=rhs, start=start, stop=stop), reads, writes)

        def tr(out, in_, ident, reads, writes):
            P.op("pe", lambda e: e.transpose(out=out, in_=in_, identity=ident), reads, writes)

        def act(out, in_, func, reads, writes, **kw):
            P.op("act", lambda e: e.activation(out=out, in_=in_, func=func, **kw), reads, writes)

        def tt(eng, out, in0, in1, op, reads, writes):
            P.op(eng, lambda e: e.tensor_tensor(out=out, in0=in0, in1=in1, op=op), reads, writes)

        def ts(eng, out, in0, s1, s2, op0, op1, reads, writes):
            if op1 is None:
                P.op(eng, lambda e: e.tensor_scalar(out=out, in0=in0, scalar1=s1, scalar2=None, op0=op0), reads, writes)
            else:
                P.op(eng, lambda e: e.tensor_scalar(out=out, in0=in0, scalar1=s1, scalar2=s2, op0=op0, op1=op1), reads, writes)

        def stt(out, in0, scalar, in1, op0, op1, reads, writes):
            P.op("dve", lambda e: e.scalar_tensor_tensor(out=out, in0=in0, scalar=scalar, in1=in1, op0=op0, op1=op1), reads, writes)

        def cp(eng, out, in_, reads, writes):
            if eng == "act":
                act(out, in_, AF.Copy, reads, writes)
            else:
                P.op(eng, lambda e: e.tensor_copy(out=out, in_=in_), reads, writes)

        def mset(eng, ap, val, writes):
            P.op(eng, lambda e: e.memset(ap, val), (), writes)

        def dma(out, in_, reads, writes, slot):
            P.dma(lambda e: e.dma_start(out=out, in_=in_), reads, writes, slot)

        sm_ctr = [0]

        def smcol():
            c = sm_ctr[0] % 64
            sm_ctr[0] += 1
            return c

        def rstd_from_ss(ss_ap, ss_key, dim, n):
            c1, c2 = smcol(), smcol()
            l_ap = small[0:n, c1:c1 + 1]
            r_ap = small[0:n, c2:c2 + 1]
            act(l_ap, ss_ap, AF.Ln, [ss_key], [("sm", c1)], scale=1.0 / dim, bias=EPS)
            act(r_ap, l_ap, AF.Exp, [("sm", c1)], [("sm", c2)], scale=-0.5)
            return r_ap, ("sm", c2)

        TA2 = [("tmpA", 0), ("tmpA", 1)]
        TB2 = [("tmpB", 0), ("tmpB", 1)]
        SB2 = [("sbb", 0), ("sbb", 1)]
        RC2 = [("rec", 0), ("rec", 1)]
        SQ2 = [("sq", 0), ("sq", 1)]
        sbbI = sbb.bitcast(I32)
        kpe32 = rec[0:32, 1, :]
        G = rec[0:8, :, :].rearrange("p a b -> p (a b)")
        cst = tmpA[:, :, :].rearrange("p a b -> p (a b)")
        cstb = sq[:, :, :].rearrange("p a b -> p (a b)")
        dma(gmix[:], gmix_d, [], ["gmix"], "c0")
        dma(gcq[:], gcq_d, [], ["gcq"], "c1")
        dma(ropec[:], ropec_d, [], ["ropec"], "c4")
        dma(G[:, 0:256], rb_d[:, 1:257], [], RC2, "c6")
        dma(stage[:, 0, 0:128], ident_d, [], [("stage", 0)], ("wst", 0))
        dma(gckv[:], gckv_d, [], ["gckv"], "c2")
        dma(gfin[:], gfin_d, [], ["gfin"], "c3")
        ts("dve", G[:, 256:384], G[:, 0:128], 0.0, G[:, 255:256], ALU.mult, ALU.add, RC2, RC2)

        def g2scr(e):
            g = G[:, 0:384]
            src = bass.AP(g.tensor, g.offset, [list(g.ap[0]), [0, 128], [1, 384]])
            return e.dma_start(out=scr, in_=src)
        P.dma(g2scr, RC2, ["scr"], "c7")
        cp("dve", identb[:], stage[:, 0, 0:128], [("stage", 0)], ["identb"])
        cp("dve", identf[:], stage[0:32, 0, 0:32], [("stage", 0)], ["identf"])

        mset("pool", onesb[:], 1.0, ["onesb"])
        mset("pool", KTp[:, :], 0.0, [("KTp", t) for t in range(32)])
        mset("pool", QpeT[:, :, :], 0.0, [("Qp", h) for h in range(8)])
        mset("pool", KTp_s[:, :], 0.0, ["KTp_s"])
        mset("pool", qbT[:, :, :, :], 0.0, [("qbT", p_) for p_ in range(4)])
        mset("pool", vbx[:, :, :, :], 1.0, [("vbx", t) for t in range(8)])
        mset("pool", vbx_s[:, :, :], 1.0, ["vbx_s"])
        mset("pool", maskv[:, :], 0.0, ["maskv"])
        mset("pool", maskv[64:128, :], NEGM, ["maskv"])

        NT = SEQ + TS

        def table_chunk(c0):
            n = min(1024, NT - c0)
            tA = tmpA[0:64, :, :].rearrange("p a b -> p (a b)")[:, 0:n]
            tB = tmpB[0:64, :, :].rearrange("p a b -> p (a b)")[:, 0:n]
            tI = sbbI[0:64, :, :].rearrange("p a b -> p (a b)")[:, 0:n]
            P.op("pool", lambda e, tA=tA, c0=c0, n=n: e.iota(tA, pattern=[[1, n]], base=c0, channel_multiplier=0,
                                                             allow_small_or_imprecise_dtypes=True), (), TA2)
            ts("dve", tB, tA, ropec[:, 0:1], ropec[:, 1:2], ALU.mult, ALU.add, TA2 + ["ropec"], TB2)
            cp("dve", tI, tB, TB2, SB2)
            cp("dve", tA, tI, SB2, TA2)
            tt("dve", tB, tB, tA, ALU.subtract, TA2 + TB2, TB2)
            act(tA, tB, AF.Sin, TB2 + ["ropec"], TA2, scale=ropec[:, 2:3])
            P.dma(lambda e, c0=c0, n=n, tA=tA: e.dma_start(out=tabd[:, c0:c0 + n], in_=tA), TA2, ["tabd"], "c9", eng="act")

        gatesF = gates.bitcast(F32)[:, :, :].rearrange("p a b -> p (a b)")
        xyF = xnT.bitcast(F32)[:, :, :].rearrange("p a b -> p (a b)")
        wslots = [(stage[:, 0, :], [("stage", 0)], ("wst", 0)), (stage[:, 1, :], [("stage", 1)], ("wst", 1)),
                  (gatesF[:, 0:1024], [("gate", g_) for g_ in range(0, 4)], "wst2"),
                  (gatesF[:, 1024:2048], [("gate", g_) for g_ in range(4, 8)], "wst3"),
                  (xyF[:, 0:1024], [("XY", g_, t_) for g_ in range(0, 4) for t_ in range(4)], "wst4"),
                  (xyF[:, 1024:2048], [("XY", g_, t_) for g_ in range(4, 8) for t_ in range(4)], "wst5")]
        pieces = [(0, 1024, "act"), (1024, 2048, "dve"), (2048, NCOL, "act")]
        wi = 0
        for dc in range(8):
            for pi_, (c0, c1, eng) in enumerate(pieces):
                buf, bkeys, bslot = wslots[wi % 6]
                wi += 1
                dma(buf[:, 0:c1 - c0], w_d[dc * 128:(dc + 1) * 128, c0:c1], [], bkeys, bslot)
                if eng == "act":
                    act(W[:, dc, c0:c1], buf[:, 0:c1 - c0], AF.Copy, bkeys + ["gmix"], [("W", dc, pi_)], scale=gmix[:, dc:dc + 1])
                else:
                    ts("dve", W[:, dc, c0:c1], buf[:, 0:c1 - c0], gmix[:, dc:dc + 1], None, ALU.mult, None,
                       bkeys + ["gmix"], [("W", dc, pi_)])
            if dc * 1024 < NT:
                table_chunk(dc * 1024)
        for dc in range(8):
            buf, bkeys, bslot = wslots[wi % 6]
            wi += 1
            dma(buf[:, 0:D], wout_d[dc * 128:(dc + 1) * 128, :], [], bkeys, bslot)
            cp("pool" if dc % 2 else "dve", Wout[:, dc, :], buf[:, 0:D], bkeys, [("Wout", dc)])
        WK = [("W", dc, i) for dc in range(8) for i in range(3)]
        WOK = [("Wout", dc) for dc in range(8)]

        def scr2bt(e):
            src = bass.AP(scr.tensor, scr.offset + 127, [[383, 128], [128 * 384, 8], [1, 256]])
            return e.dma_start(out=BT[:], in_=src)
        P.dma(scr2bt, ["scr"], ["BT"], "c8")
        cp("dve", cb[:, :], BT[:, :, 255], ["BT"], ["cb"])
        cp("dve", cbm[:, :], BT[:, :, 255], ["BT"], ["cbm"])
        mset("pool", cbm[0:64, :], NEGM, ["cbm"])
        mset("pool", BT[64:128, :, 0:64], NEGM, ["BT"])
        act(BT[:, :, :].rearrange("p a b -> p (a b)"), BT[:, :, :].rearrange("p a b -> p (a b)"), AF.Exp, ["BT", "cb", "cbm"], ["BT"])

        dma(stage[:, 0, 0:512], wuv_d, [], [("stage", 0)], ("wst", 0))
        cp("dve", Wuv[:], stage[:, 0, 0:512], [("stage", 0)], ["Wuv"])
        dma(stage[:, 1, 0:1024].rearrange("p (a b) -> p a b", a=2), wpe_d.rearrange("(a p) n -> p a n", p=128), [], [("stage", 1)], ("wst", 1))
        for rc in range(2):
            ts("dve", Wpe[:, rc, :, :].rearrange("p a b -> p (a b)"), stage[:, 1, rc * 512:(rc + 1) * 512], gcq[:, rc:rc + 1], None,
               ALU.mult, None, [("stage", 1), "gcq"], ["Wpe"])
        CQ2 = [("cqT", 0), ("cqT", 1)]
        ukb = sq[0:64, :, :].rearrange("p a b -> p (a b)")
        uqb = cqT[0:64, :, :].rearrange("p a b -> p (a b)")
        dma(stage[0:64, 1, 0:1024], uk_d, [], [("stage", 1)], ("wst", 1))
        cp("dve", ukb, stage[0:64, 1, 0:1024], [("stage", 1)], SQ2)
        for hg in range(2):
            dma(stage[0:64, 0, 0:1024], uqn_d[:, hg * 1024:(hg + 1) * 1024], [], [("stage", 0)], ("wst", 0))
            cp("dve", uqb, stage[0:64, 0, 0:1024], [("stage", 0)], CQ2)
            for rc in range(2):
                bank, bk = pb[rc], PSK[rc]
                for hl in range(4):
                    h = hg * 4 + hl
                    mm(bank[:, hl * 128:(hl + 1) * 128], uqb[:, hl * 256 + rc * 128: hl * 256 + (rc + 1) * 128],
                       ukb[:, h * 128:(h + 1) * 128], True, True, SQ2 + CQ2, [bk])
                ts("dve", Wlat[:, rc, hg * 4:(hg + 1) * 4, :].rearrange("p a b -> p (a b)"), bank[:, :], gcq[:, rc:rc + 1], None,
                   ALU.mult, None, [bk, "gcq"], ["Wlat"])

        bank_rr = [0]

        def nextbank(choices=(0, 1, 6, 2, 3, 4, 5)):
            i = choices[bank_rr[0] % len(choices)]
            bank_rr[0] += 1
            return pb[i], PSK[i]

        xslot = [0]

        def load_x_tile(src_ap, tsz):
            s = xslot[0] % 2
            xslot[0] += 1
            xt = stage[0:tsz, s, 0:D]
            dma(xt, src_ap, [], [("stage", s)], ("wst", s))
            return xt, ("stage", s)

        def XYt(t_):
            return [("XY", fc, t_) for fc in range(8)]

        def XYall(ntile_):
            return [("XY", fc, t_) for fc in range(8) for t_ in range(ntile_)]

        def XYfc(fc, ntile_):
            return [("XY", fc, t_) for t_ in range(ntile_)]
        tab_rr = [0]

        def a_front_tile(xt, xkeys, tsz, tti):
            c = smcol()
            ss = small[0:tsz, c:c + 1]
            act(xnb[0:tsz, :], xt, AF.Square, xkeys, ["xnb", ("sm", c)], accum_out=ss)
            r_ap, rk = rstd_from_ss(ss, ("sm", c), D, tsz)
            act(xnb[0:tsz, :], xt, AF.Copy, xkeys + [rk], ["xnb"], scale=r_ap)
            for dc in range(8):
                tr(pT[:, dc * tsz:(dc + 1) * tsz], xnb[0:tsz, dc * 128:(dc + 1) * 128], identb[0:tsz, 0:tsz],
                   ["xnb", "identb"], [PTK])
            cp("dve", xnT[:, :, tti * tsz:(tti + 1) * tsz], pT[:, 0:8 * tsz].rearrange("p (a b) -> p a b", a=8),
               [PTK], XYt(tti))

        def phase_a(x_src, t0, nt, tsz, tab0, KTc_dst, KTp_dst, V_dst, kv_keys, kb_dst, kb_keys, vbx_dst, vbx_key,
                    ckv_out, kpe_out, kb_out, vb_out, front_done=False):
            ntile = nt // tsz
            tsl = 0
            TKEY = ("tab", tsl)
            dma(tabblk[:, tsl, 0:nt], tabd[:, tab0:tab0 + nt], ["tabd"], [TKEY], ("tabld", tsl))
            cosT = tabblk[0:32, tsl, 0:nt]
            sinT = tabblk[32:64, tsl, 0:nt]
            if not front_done:
                for tti in range(ntile):
                    xt, xk = load_x_tile(x_src[t0 + tti * tsz: t0 + (tti + 1) * tsz, :], tsz)
                    a_front_tile(xt, [xk], tsz, tti)
            XK = XYall(ntile)

            def fm_group(col0, m):
                bank, bk = nextbank()
                for dc in range(8):
                    mm(bank[0:m, 0:nt], W[:, dc, col0:col0 + m], xnT[:, dc, 0:nt], dc == 0, dc == 7, WK + XK, [bk])
                return bank, bk

            _sec(2)
            for rc in range(2):
                bank, bk = fm_group(C_CQ + rc * 128, 128)
                cp("dve", cqT[:, rc, 0:nt], bank[:, 0:nt], [bk], [("cqT", rc)])
                act(sq[:, rc, 0:nt], bank[:, 0:nt], AF.Square, [bk], [("sq", rc)])
            bank, bk = nextbank()
            for rc in range(2):
                mm(bank[:, 0:nt], onesb[:, :], sq[:, rc, 0:nt], rc == 0, rc == 1, ["onesb", ("sq", rc)], [bk])
            act(tmpA[:, 0, 0:nt], bank[:, 0:nt], AF.Ln, [bk], [("tmpA", 0)], scale=1.0 / 256, bias=EPS)
            act(rq[:, 0:nt], tmpA[:, 0, 0:nt], AF.Exp, [("tmpA", 0)], ["rq"], scale=-0.5)
            CQK = [("cqT", 0), ("cqT", 1)]
            _sec(3)
            for h in range(8):
                bank, bk = nextbank()
                for rc in range(2):
                    mm(bank[:, 0:nt], Wlat[:, rc, h, :], cqT[:, rc, 0:nt], rc == 0, rc == 1, ["Wlat"] + CQK, [bk])
                tt("dve", QlatT[:, h, 0:nt], bank[:, 0:nt], rq[:, 0:nt], ALU.mult, [bk, "rq"], [("QO", h, jl) for jl in range(8)])
            _sec(4)
            cosq = tmpB[0:32, 0, 0:nt]
            sinq = tmpB[32:64, 0, 0:nt]
            tt("dve", tmpB[0:64, 0, 0:nt], tabblk[0:64, tsl, 0:nt], rq[0:64, 0:nt], ALU.mult, [TKEY, "rq"], [("tmpB", 0)])
            for hp in range(4):
                bank, bk = nextbank()
                for rc in range(2):
                    mm(bank[:, 0:nt], Wpe[:, rc, 2 * hp:2 * hp + 2, :].rearrange("p a b -> p (a b)"), cqT[:, rc, 0:nt], rc == 0, rc == 1,
                       ["Wpe"] + CQK, [bk])
                for hh in range(2):
                    h = 2 * hp + hh
                    s = hh
                    r0 = 64 * hh
                    tt("dve", sbb[0:32, s, 0:nt], bank[r0:r0 + 32, 0:nt], cosq, ALU.mult, [bk, ("tmpB", 0)], [("sbb", s)])
                    tt("dve", rec[0:32, s, 0:nt], bank[r0 + 32:r0 + 64, 0:nt], sinq, ALU.mult, [bk, ("tmpB", 0)], [("rec", s)])
                    tt("dve", QpeT[0:32, h, 0:nt], sbb[0:32, s, 0:nt], rec[0:32, s, 0:nt], ALU.add, [("sbb", s), ("rec", s)], [("Qp", h)])
            _sec(5)
            for gi in range(8):
                col0 = (C_GA if gi < 4 else C_GB) + (gi % 4) * 128
                bank, bk = fm_group(col0, 128)
                s = gi % 2
                act(tmpA[:, s, 0:nt], bank[:, 0:nt], AF.Exp, [bk], [("tmpA", s)], scale=-1.0)
                act(tmpA[:, s, 0:nt], tmpA[:, s, 0:nt], AF.Ln, [("tmpA", s)], [("tmpA", s)], bias=1.0)
                act(tmpA[:, s, 0:nt], tmpA[:, s, 0:nt], AF.Exp, [("tmpA", s)], [("tmpA", s)], scale=-1.0)
                tt("dve", gates[:, gi, 0:nt], bank[:, 0:nt], tmpA[:, s, 0:nt], ALU.mult, [bk, ("tmpA", s)], [("gate", gi)])
            _sec(6)
            for p_ in range(4):
                bank, bk = fm_group(C_QB + p_ * 128, 128)
                cp("act", qbT[0:64, p_, 0, 0:nt], bank[0:64, 0:nt], [bk], [("qbT", p_)])
                cp("act", qbT[64:128, p_, 1, 0:nt], bank[64:128, 0:nt], [bk], [("qbT", p_)])
            for p_ in range(4):
                bank, bk = fm_group(C_KB + p_ * 128, 128)
                cp("dve", kb_dst(p_), bank[:, 0:nt], [bk], kb_keys(p_))
            _sec(7)
            bank, bk = fm_group(C_KR, 128)
            tt("dve", tmpB[0:32, 1, 0:nt], bank[0:32, 0:nt], cosT, ALU.mult, [bk, TKEY], [("tmpB", 1)])
            tt("dve", sbb[0:32, 0, 0:nt], bank[32:64, 0:nt], sinT, ALU.mult, [bk, TKEY], [("sbb", 0)])
            tt("dve", kpe32[:, 0:nt], tmpB[0:32, 1, 0:nt], sbb[0:32, 0, 0:nt], ALU.add, [("tmpB", 1), ("sbb", 0)], [("rec", 1)])
            cp("dve", KTp_dst, kpe32[:, 0:nt], [("rec", 1)], kv_keys[1])

            def kpe_out_tr():
                bank, bk = nextbank((6,))
                for tti in range(ntile):
                    tr(bank[0:tsz, tti * 32:(tti + 1) * 32], kpe32[0:32, tti * tsz:(tti + 1) * tsz], identf[0:32, 0:32],
                       [("rec", 1), "identf"], [bk])
                cp("act", kpst[0:tsz, 0:ntile, :], bank[0:tsz, 0:ntile * 32].rearrange("p (a b) -> p a b", a=ntile), [bk], ["kpst"])
                dma(kpe_out, kpst[0:tsz, 0:ntile, :], ["kpst"], [], "o_kpe")
            _sec(8)
            pend_tr = [kpe_out_tr]
            for tti in range(ntile):
                tok = slice(tti * tsz, (tti + 1) * tsz)
                bank, bk = nextbank()
                for dc in range(8):
                    mm(bank[0:tsz, 0:128], xnT[:, dc, tok], W[:, dc, C_CKV:C_CKV + 128], dc == 0, dc == 7, WK + XK, [bk])
                c = smcol()
                ss = small[0:tsz, c:c + 1]
                act(ckvb[0:tsz, :], bank[0:tsz, 0:128], AF.Square, [bk], ["ckvb", ("sm", c)], accum_out=ss)
                r_ap, rk = rstd_from_ss(ss, ("sm", c), 128, tsz)
                s = tti % 2
                stt(ckv32[0:tsz, s, :], bank[0:tsz, 0:128], r_ap, gckv[0:tsz, :], ALU.mult, ALU.mult, [bk, rk, "gckv"], [("ckv32", s)])
                dma(ckv_out(tti), ckv32[0:tsz, s, :], [("ckv32", s)], [], ("o_ckv", s))
                cp("act", V_dst(tti), ckv32[0:tsz, s, :], [("ckv32", s)], [kv_keys[2](tti)])

                def do_tr(tti=tti):
                    tr(pT[:, 0:tsz], V_dst(tti), identb[0:tsz, 0:tsz], [kv_keys[2](tti), "identb"], [PTK])
                    cp("dve", KTc_dst(tti), pT[:, 0:tsz], [PTK], [kv_keys[0](tti)])
                bank, bk = nextbank()
                for dc in range(8):
                    mm(bank[0:tsz, :], xnT[:, dc, tok], W[:, dc, C_VB:C_VB + 512], dc == 0, dc == 7, WK + XK, [bk])
                cp("act", vbx_dst(tti)[:, :, 0:64], bank[0:tsz, :].rearrange("p (a b) -> p a b", a=8), [bk], [vbx_key(tti)])
                if vb_out is not None:
                    cp("dve", sbb[0:tsz, 0, :], bank[0:tsz, :], [bk], [("sbb", 0)])
                    dma(vb_out(tti), sbb[0:tsz, 0, :], [("sbb", 0)], [], ("o_kv", 0))
                if kb_out is not None:
                    bank, bk = nextbank()
                    for dc in range(8):
                        mm(bank[0:tsz, :], xnT[:, dc, tok], W[:, dc, C_KB:C_KB + 512], dc == 0, dc == 7, WK + XK, [bk])
                    cp("dve", sbb[0:tsz, 1, :], bank[0:tsz, :], [bk], [("sbb", 1)])
                    dma(kb_out(tti), sbb[0:tsz, 1, :], [("sbb", 1)], [], ("o_kv", 1))
                while pend_tr:
                    pend_tr.pop(0)()
                pend_tr.append(do_tr)
            while pend_tr:
                pend_tr.pop(0)()

        pt_rr = [0]
        sbank_rr = [0]
        ol_rr = [0]
        mla_pend = {"q": []}
        PV_LAG = 2

        def mla_chunk_steps(nq, jl, tiles):
            ncol = 8 * nq
            qcols = slice(jl * nq, (jl + 1) * nq)
            par = ol_rr[0] % 2
            cid = ol_rr[0]
            ol_rr[0] += 1
            psO, kO = pb[2 + par], PSK[2 + par]
            psL, kL = pb[4], PSK[4]
            QK = [("QO", h, jl) for h in range(8)] + [("Qp", h) for h in range(8)]
            nt_ = len(tiles)
            state = mla_pend

            def issue_pv(i, pi):
                ktc, ktp, v, M, keys = tiles[i][:5]
                mm(psO[:, 0:ncol], v, PT[0:M, pi, 0:ncol], i == 0, i == nt_ - 1, [("PT", pi)] + list(keys), [kO])

            def mk(i):
                def step():
                    ktc, ktp, v, M, keys = tiles[i][:5]
                    sbk = sbank_rr[0] % 2
                    sbank_rr[0] += 1
                    psS, kS = pb[sbk], PSK[sbk]
                    out_ap = psS[0:M, 0:ncol].rearrange("p (a b) -> p a b", a=8)
                    mm(out_ap, ktc, QlatT[:, :, qcols], True, False, QK + list(keys), [kS])
                    mm(out_ap, ktp, QpeT[:, :, qcols], False, True, QK + list(keys), [kS])
                    pi = pt_rr[0] % 4
                    pt_rr[0] += 1
                    if len(tiles[i]) > 5 and tiles[i][5]:
                        act(PT[0:M, pi, 0:ncol], psS[0:M, 0:ncol], AF.Exp, [kS, "maskv"], [("PT", pi)], scale=MLA_SCALE, bias=maskv[:, 0:1])
                    else:
                        act(PT[0:M, pi, 0:ncol], psS[0:M, 0:ncol], AF.Exp, [kS], [("PT", pi)], scale=MLA_SCALE)
                    e_ = i % 2
                    if i < 2 and M == 128:
                        cp("dve", acc[:, e_, 0:ncol], PT[:, pi, 0:ncol], [("PT", pi)], [("acc", e_)])
                    else:
                        if i < 2:
                            mset("pool", acc[:, e_, 0:ncol], 0.0, [("acc", e_)])
                        tt("dve", acc[0:M, e_, 0:ncol], acc[0:M, e_, 0:ncol], PT[0:M, pi, 0:ncol], ALU.add,
                           [("acc", e_), ("PT", pi)], [("acc", e_)])
                    state["q"].append((cid, lambda i=i, pi=pi: issue_pv(i, pi)))
                    while len(state["q"]) > PV_LAG:
                        state["q"].pop(0)[1]()
                return step

            def fin_a():
                for e_ in range(min(nt_, 2)):
                    cp("dve", acch[:, e_, 0:ncol], acc[:, e_, 0:ncol], [("acc", e_)], [("acch", e_)])

            def fin_b():
                while state["q"] and state["q"][0][0] <= cid:
                    state["q"].pop(0)[1]()
                ne = min(nt_, 2)
                for e_ in range(ne):
                    mm(psL[:, 0:ncol], onesb[:, :], acch[:, e_, 0:ncol], e_ == 0, e_ == ne - 1, ["onesb", ("acch", e_)], [kL])
                act(rec[:, par, 0:ncol], psL[:, 0:ncol], AF.Ln, [kL], [("rec", par)])
                act(rec[:, par, 0:ncol], rec[:, par, 0:ncol], AF.Exp, [("rec", par)], [("rec", par)], scale=-1.0)
                tt("dve", OlatT[:, :, qcols], psO[:, 0:ncol].rearrange("p (a b) -> p a b", a=8),
                   rec[:, par, 0:ncol].rearrange("p (a b) -> p a b", a=8), ALU.mult, [kO, ("rec", par)],
                   [("QO", h, jl) for h in range(8)])

            return [mk(i) for i in range(nt_)] + [fin_a], fin_b

        def mla_steps_for(chunks):
            flat = []
            pending = []
            for (nq, jl, tiles) in chunks:
                steps, fin_b = mla_chunk_steps(nq, jl, tiles)
                for k, stp in enumerate(steps):
                    if pending and k == min(2, len(steps) - 1):
                        flat.append(pending.pop(0))
                    flat.append(stp)
                pending.append(fin_b)

            def flush():
                while mla_pend["q"]:
                    mla_pend["q"].pop(0)[1]()
            flat.append(flush)
            flat.extend(pending)
            return flat

        def merge_a(nt, njl):
            for p_ in range(4):
                bank, bk = nextbank((0, 1))
                for hh in range(2):
                    h = 2 * p_ + hh
                    mm(bank[hh * 64:(hh + 1) * 64, 0:nt], Wuv[:, h * 64:(h + 1) * 64], OlatT[:, h, 0:nt], True, True,
                       ["Wuv"] + [("QO", h, jl) for jl in range(njl)], [bk])
                tt("dve", yT[:, p_, 0:nt], bank[:, 0:nt], gates[:, p_, 0:nt], ALU.mult, [bk, ("gate", p_)], XYfc(p_, max(1, nt // 128)))

        ptb_rr = [0]

        def band_steps_for(nt, heads_tiles):
            items = []
            for (h, tiles) in heads_tiles:
                for i, tl in enumerate(tiles):
                    items.append({"h": h, "t": tl, "first": i == 0, "last": i == len(tiles) - 1})
            N = len(items)

            def stage1(it):
                h = it["h"]
                p_ = h // 2
                pbase = 64 * (h % 2)
                kb_ap, vx_ap, M, q0, q1, qr0, keys = it["t"]
                n = q1 - q0
                mm(pT32[0:M, 0:n], kb_ap, qbT[:, p_, h % 2, q0:q1], True, True, [("qbT", p_)] + list(keys), [PTK])

            def stage2(it):
                h = it["h"]
                kb_ap, vx_ap, M, q0, q1, qr0, keys = it["t"]
                n = q1 - q0
                pi = ptb_rr[0] % 2
                ptb_rr[0] += 1
                it["pi"] = pi
                psS, kS = pT32, PTK
                lo, hi = qr0, qr0 + n
                if lo < 256:
                    a1 = min(hi, 256)
                    c1_ = a1 - lo
                    act(PTb[0:M, pi, 0:c1_], psS[0:M, 0:c1_], AF.Exp, [kS], [("PTb", pi, 0)], scale=B_SCALE)
                    tt("pool", PTb[0:M, pi, 0:c1_], PTb[0:M, pi, 0:c1_], BT[0:M, h, lo:a1], ALU.mult, [("PTb", pi, 0), "BT"], [("PTb", pi, 0)])
                b0, b1 = max(lo, 256), min(hi, 576)
                if b1 > b0:
                    act(PTb[0:M, pi, b0 - lo:b1 - lo], psS[0:M, b0 - lo:b1 - lo], AF.Exp, [kS, "cb"], [("PTb", pi, 1)],
                        scale=B_SCALE, bias=cb[0:M, h:h + 1])
                e0 = max(lo, 576)
                if hi > e0:
                    act(PTb[0:M, pi, e0 - lo:hi - lo], psS[0:M, e0 - lo:hi - lo], AF.Exp, [kS, "cbm"], [("PTb", pi, 2)],
                        scale=B_SCALE, bias=cbm[0:M, h:h + 1])

            def stage3(it):
                h = it["h"]
                kb_ap, vx_ap, M, q0, q1, qr0, keys = it["t"]
                n = q1 - q0
                pi = it["pi"]
                psOb, kOb = pb[5 + h % 2], PSK[5 + h % 2]
                PK = [("PTb", pi, 0), ("PTb", pi, 1), ("PTb", pi, 2)]
                mm(psOb[:, q0:q1], vx_ap, PTb[0:M, pi, 0:n], it["first"], it["last"], PK + list(keys), [kOb])

            def finalize(h):
                p_ = h // 2
                pbase = 64 * (h % 2)
                s = h % 2
                psOb, kOb = pb[5 + s], PSK[5 + s]
                act(tmpA[64:128, s, 0:nt], psOb[64:128, 0:nt], AF.Ln, [kOb], [("tmpA", s)])
                act(tmpA[64:128, s, 0:nt], tmpA[64:128, s, 0:nt], AF.Exp, [("tmpA", s)], [("tmpA", s)], scale=-1.0)
                tt("dve", tmpB[pbase:pbase + 64, s, 0:nt], psOb[0:64, 0:nt], tmpA[64:128, s, 0:nt], ALU.mult, [kOb, ("tmpA", s)], [("tmpB", s)])
                tt("pool", yT[pbase:pbase + 64, 4 + p_, 0:nt], tmpB[pbase:pbase + 64, s, 0:nt], gates[pbase:pbase + 64, 4 + p_, 0:nt],
                   ALU.mult, [("tmpB", s), ("gate", 4 + p_)], XYfc(4 + p_, max(1, nt // 128)))

            out = []
            for k in range(N + 2):
                def pre(k=k):
                    if 0 <= k - 1 < N:
                        stage2(items[k - 1])
                    if 0 <= k - 2 < N:
                        stage3(items[k - 2])

                def post(k=k):
                    if 0 <= k - 2 < N and items[k - 2]["last"]:
                        finalize(items[k - 2]["h"])
                    if k < N:
                        stage1(items[k])
                out.append((pre, post))
            return out

        def run_interleaved(mla_flat, band_flat):
            nm, nb = len(mla_flat), len(band_flat)
            mi = 0
            for k in range(nb):
                band_flat[k][0]()
                target = ((k + 1) * nm) // nb
                while mi < target:
                    mla_flat[mi]()
                    mi += 1
                band_flat[k][1]()
            while mi < nm:
                mla_flat[mi]()
                mi += 1

        def d_buffers(tsz):
            return [(stage[0:tsz, 0, 0:D], [("stage", 0)], ("wst", 0)),
                    (stage[0:tsz, 1, 0:D], [("stage", 1)], ("wst", 1)),
                    (sbb[0:tsz, :, :].rearrange("p a b -> p (a b)"), [("sbb", 0), ("sbb", 1)], "xd2"),
                    (rec[0:tsz, :, :].rearrange("p a b -> p (a b)"), [("rec", 0), ("rec", 1)], "xd3")]

        def d_prefetch(x_src, t0, tsz, which):
            bufs = d_buffers(tsz)
            out = {}
            for i in which:
                buf, keys, slot = bufs[i]
                dma(buf, x_src[t0 + i * tsz: t0 + (i + 1) * tsz, :], [], keys, slot)
                out[i] = True
            return out

        def d_tile_a(tti, tsz):
            bufs = d_buffers(tsz)
            tok = slice(tti * tsz, (tti + 1) * tsz)
            xt, xkeys, slot = bufs[tti]
            banks = [nextbank((0, 1, 2, 3)), nextbank((0, 1, 2, 3))]
            for half in range(2):
                bank, bk = banks[half]
                for fc in range(8):
                    mm(bank[0:tsz, :], yT[:, fc, tok], Wout[:, fc, half * 512:(half + 1) * 512], fc == 0, fc == 7, XYt(tti) + WOK, [bk])
            for half in range(2):
                bank, bk = banks[half]
                tt("dve", xt[:, half * 512:(half + 1) * 512], bank[0:tsz, :], xt[:, half * 512:(half + 1) * 512], ALU.add, [bk] + xkeys, xkeys)

        def d_tile_b(tti, tsz, y_out):
            bufs = d_buffers(tsz)
            xt, xkeys, slot = bufs[tti]
            c = smcol()
            ss = small[0:tsz, c:c + 1]
            jk = PT[0:tsz, 0:2, :].rearrange("p a b -> p (a b)")
            act(jk, xt, AF.Square, xkeys, [("PT", 0), ("PT", 1), ("sm", c)], accum_out=ss)
            r_ap, rk = rstd_from_ss(ss, ("sm", c), D, tsz)
            stt(xt, xt, r_ap, gfin[0:tsz, :], ALU.mult, ALU.mult, xkeys + [rk, "gfin"], xkeys)
            dma(y_out(tti), xt, xkeys, [], ("o_y", tti))

        def d_tile(tti, tsz, y_out):
            d_tile_a(tti, tsz)
            d_tile_b(tti, tsz, y_out)

        def phase_d(x_src, t0, nt, tsz, y_out, pre=None):
            ntile = nt // tsz
            bufs = d_buffers(tsz)
            pre = pre or {}
            for tti in range(ntile):
                if tti not in pre:
                    buf, keys, slot = bufs[tti]
                    dma(buf, x_src[t0 + tti * tsz: t0 + (tti + 1) * tsz, :], [], keys, slot)
            for tti in range(ntile):
                d_tile(tti, tsz, y_out)

        if SAMPLE:
            cst3 = cst.rearrange("p (t c) -> p t c", c=128)
            for g in range(4):
                dma(cst3, cckv_d[g * 1024:(g + 1) * 1024, :].rearrange("(t p) c -> p t c", p=128), [], TA2, "cin")
                cp("pool" if g % 2 else "dve", V[:, g * 8:(g + 1) * 8, :], cst3, TA2, [("V", g * 8 + i) for i in range(8)])
                for i in range(8):
                    t = g * 8 + i
                    tr(pT[:, i * 128:(i + 1) * 128], V[:, t, :], identb[:], [("V", t), "identb"], [PTK])
                cp("act", KTc[:, g * 1024:(g + 1) * 1024], pT[:, :], [PTK], [("KTc", g * 8 + i) for i in range(8)])
            for g in range(4):
                cv = cst[:, 0:256].rearrange("p (t r) -> p t r", r=32)
                cvb_ = cstb[:, 0:256].rearrange("p (t r) -> p t r", r=32)
                dma(cv, ckpe_d[g * 1024:(g + 1) * 1024, :].rearrange("(t p) r -> p t r", p=128), [], TA2, "cin")
                cp("dve", cvb_, cv, TA2, SQ2)
                for i in range(8):
                    tr(pT[0:32, i * 128:(i + 1) * 128], cvb_[:, i, :], identb[:], SQ2 + ["identb"], [PTK])
                cp("act", KTp[0:32, g * 1024:(g + 1) * 1024], pT[0:32, :], [PTK], [("KTp", g * 8 + i) for i in range(8)])
            for g in range(2):
                cv = cst.rearrange("p (t n) -> p t n", n=512)
                cvb_ = cstb.rearrange("p (t n) -> p t n", n=512)
                dma(cv, ckb_d[g * 256:(g + 1) * 256, :].rearrange("(t p) n -> p t n", p=128), [], TA2, "cin")
                cp("dve", cvb_, cv, TA2, SQ2)
                for i in range(2):
                    tmi = g * 2 + i
                    for p_ in range(4):
                        tr(pT[:, p_ * 128:(p_ + 1) * 128], cvb_[:, i, p_ * 128:(p_ + 1) * 128], identb[:], SQ2 + ["identb"], [PTK])
                    cp("act", kbT[:, :, 1, tmi * 128:(tmi + 1) * 128], pT[:, 0:512].rearrange("p (a b) -> p a b", a=4), [PTK],
                       [("kbT", 1, p_, tmi) for p_ in range(4)])
            for g in range(2):
                cv = cst.rearrange("p (t n) -> p t n", n=512)
                dma(cv, cvb_d[g * 256:(g + 1) * 256, :].rearrange("(t p) n -> p t n", p=128), [], TA2, "cin")
                for i in range(2):
                    tmi = g * 2 + i
                    cp("dve", vbx[:, 4 + tmi, :, 0:64], cv[:, i, :].rearrange("p (a b) -> p a b", a=8), TA2, [("vbx", 4 + tmi)])

            phase_a(xs_d, 0, TS, TS, SEQ,
                    KTc_dst=lambda tti: KTc_s[:, 0:TS], KTp_dst=KTp_s[0:32, 0:TS], V_dst=lambda tti: V_s[0:TS, :],
                    kv_keys=(lambda tti: "KTc_s", ["KTp_s"], lambda tti: "V_s"),
                    kb_dst=lambda p_: kbT_s[:, p_, 0:TS], kb_keys=lambda p_: [("kbT_s", p_)],
                    vbx_dst=lambda tti: vbx_s[0:TS, :, :], vbx_key=lambda tti: "vbx_s",
                    ckv_out=lambda tti: ckvs_o, kpe_out=kpes_o.rearrange("(t p) r -> p t r", p=TS),
                    kb_out=lambda tti: kbs_o, vb_out=lambda tti: vbs_o)
            tiles = []
            for kt in range(32):
                tiles.append((KTc[:, kt * 128:(kt + 1) * 128], KTp[:, kt * 128:(kt + 1) * 128], V[:, kt, :], 128,
                              [("KTc", kt), ("KTp", kt), ("V", kt)]))
            tiles.append((KTc_s[:, 0:TS], KTp_s[:, 0:TS], V_s[0:TS, :], TS, ["KTc_s", "KTp_s", "V_s"]))
            mla_flat = mla_steps_for([(TS, 0, tiles)])
            heads_tiles = []
            for h in range(8):
                p_ = h // 2
                pbase = 64 * (h % 2)
                tiles = []
                for tmi in range(4):
                    tiles.append((kbT[:, p_, 1, tmi * 128:(tmi + 1) * 128], vbx[:, 4 + tmi, h, :], 128, 0, TS,
                                  512 - 128 * tmi, [("kbT", 1, p_, tmi), ("vbx", 4 + tmi)]))
                tiles.append((kbT_s[:, p_, 0:TS], vbx_s[0:TS, h, :], TS, 0, TS, 0, [("kbT_s", p_), "vbx_s"]))
                heads_tiles.append((h, tiles))
            run_interleaved(mla_flat, band_steps_for(TS, heads_tiles))
            merge_a(TS, 1)
            phase_d(xs_d, 0, TS, TS, lambda tti: ys_o)

        for b in range(NBLK if DBG_STAGE >= 1 else 0):
            t0 = 512 * b
            slot = b % 2
            last = (b == SEQ // 512 - 1)
            try:
              phase_a(x_d, t0, 512, 128, t0,
                    KTc_dst=lambda tti, t0=t0: KTc[:, t0 + tti * 128: t0 + (tti + 1) * 128],
                    KTp_dst=KTp[0:32, t0:t0 + 512],
                    V_dst=lambda tti, b=b: V[:, 4 * b + tti, :],
                    kv_keys=(lambda tti, b=b: ("KTc", 4 * b + tti), [("KTp", 4 * b + i) for i in range(4)], lambda tti, b=b: ("V", 4 * b + tti)),
                    kb_dst=lambda p_, slot=slot: kbT[:, p_, slot, :],
                    kb_keys=lambda p_, slot=slot: [("kbT", slot, p_, tmi) for tmi in range(4)],
                    vbx_dst=lambda tti, slot=slot: vbx[:, 4 * slot + tti, :, :],
                    vbx_key=lambda tti, slot=slot: ("vbx", 4 * slot + tti),
                    ckv_out=lambda tti, t0=t0: ckv_o[t0 + tti * 128: t0 + (tti + 1) * 128, :],
                    kpe_out=kpe_o[t0:t0 + 512, :].rearrange("(t p) r -> p t r", p=128),
                    kb_out=(lambda tti: kb_o[tti * 128:(tti + 1) * 128, :]) if last else None,
                    vb_out=(lambda tti: vb_o[tti * 128:(tti + 1) * 128, :]) if last else None,
                    front_done=(b > 0 and DBG_STAGE >= 4))
            except _Stop:
                pass
            pre = d_prefetch(x_d, t0, 128, [0, 1, 2]) if DBG_STAGE >= 4 else None
            chunks = []
            for jl in range(8 if DBG_STAGE >= 2 else 0):
                j = 8 * b + jl
                tiles = []
                for kt in range(j // 2 + 1):
                    half = (kt == j // 2 and j % 2 == 0)
                    tiles.append((KTc[:, kt * 128:(kt + 1) * 128], KTp[:, kt * 128:(kt + 1) * 128], V[:, kt, :], 128,
                                  [("KTc", kt), ("KTp", kt), ("V", kt)], half))
                chunks.append((64, jl, tiles))
            mla_flat = mla_steps_for(chunks)
            ms = [m for m in range(4 * b - 4, 4 * b + 4) if m >= 0]
            first = 4 * b - 1 if b > 0 else 0
            ms = [first] + [m for m in ms if m != first]
            heads_tiles = []
            for h in range(8 if DBG_STAGE >= 3 else 0):
                p_ = h // 2
                pbase = 64 * (h % 2)
                tiles = []
                for m in ms:
                    bm, tmi = m // 4, m % 4
                    sl = bm % 2
                    c_lo = max(2 * m, 8 * b)
                    c_hi = min(2 * m + 9, 8 * b + 7)
                    q0 = (c_lo - 8 * b) * 64
                    q1 = (c_hi - 8 * b + 1) * 64
                    qr0 = (c_lo - 2 * m) * 64
                    tiles.append((kbT[:, p_, sl, tmi * 128:(tmi + 1) * 128], vbx[:, 4 * sl + tmi, h, :], 128, q0, q1, qr0,
                                  [("kbT", sl, p_, tmi), ("vbx", 4 * sl + tmi)]))
                heads_tiles.append((h, tiles))
            run_interleaved(mla_flat, band_steps_for(512, heads_tiles))
            if DBG_STAGE >= 2:
                merge_a(512, 8)
            if DBG_STAGE >= 4:
                nxt = (b + 1 < NBLK)
                a_bufs = [(tmpA[:, :, :].rearrange("p a b -> p (a b)"), TA2, "xa0"), (tmpB[:, :, :].rearrange("p a b -> p (a b)"), TB2, "xa1"),
                          (stage[:, 0, 0:D], [("stage", 0)], ("wst", 0)), (stage[:, 1, 0:D], [("stage", 1)], ("wst", 1))]
                t1 = t0 + 512
                if 3 not in pre:
                    buf, keys, slot = d_buffers(128)[3]
                    dma(buf, x_d[t0 + 384: t0 + 512, :], [], keys, slot)
                if nxt:
                    for i_ in range(2):
                        buf, keys, slot = a_bufs[i_]
                        dma(buf, x_d[t1 + i_ * 128: t1 + (i_ + 1) * 128, :], [], keys, slot)
                yo = lambda tti, t0=t0: y_o[t0 + tti * 128: t0 + (tti + 1) * 128, :]
                for tti in range(5):
                    if tti < 4:
                        d_tile_a(tti, 128)
                    if nxt and tti >= 1:
                        buf, keys, slot = a_bufs[tti - 1]
                        a_front_tile(buf, keys, 128, tti - 1)
                    if tti < 4:
                        d_tile_b(tti, 128, yo)
                        if nxt and tti < 2:
                            buf, keys, slot = a_bufs[2 + tti]
                            dma(buf, x_d[t1 + (2 + tti) * 128: t1 + (3 + tti) * 128, :], [], keys, slot)

        n = P.emit(nc)
    return nc, n


def _prep_shared(w_in, g_mix, g_cq, w_uq, g_ckv, w_uk, w_uv, rel_bias, w_out, g_final):
    f = np.float32
    wi = np.asarray(w_in[0], f)
    perm = np.r_[16:32, 0:16]
    kr = wi[:, 384:416]
    cols = [wi[:, 0:256], wi[:, 416:928], wi[:, 2464:2976], wi[:, 928:1440], wi[:, 1440:1952], kr, kr[:, perm],
            wi[:, 256:384], wi[:, 1952:2464]]
    w_perm = np.ascontiguousarray(np.concatenate(cols, axis=1))
    assert w_perm.shape == (D, NCOL)
    wuq = np.asarray(w_uq[0], f)
    uqnT = np.ascontiguousarray(wuq[:, :, :64].transpose(2, 1, 0)).reshape(64, 8 * 256)
    ukT = np.ascontiguousarray(np.asarray(w_uk[0], f).transpose(2, 1, 0)).reshape(64, 8 * 128)
    pe = wuq[:, :, 64:96]
    wpe = np.ascontiguousarray(np.concatenate([pe, pe[:, :, perm]], axis=2)).reshape(256, 512)
    half = 16
    inv = (10000.0 ** (-np.arange(half, dtype=np.float64) / half))
    ropec = np.zeros((64, 4), np.float64)
    for r in range(32):
        ropec[r, 0] = inv[r % 16] / (2 * np.pi)
        ropec[r, 1] = 0.25
        ropec[r, 2] = 2 * np.pi
        ropec[32 + r, 0] = inv[r % 16] / (2 * np.pi)
        ropec[32 + r, 1] = 0.0
        ropec[32 + r, 2] = (-2 * np.pi) if r < 16 else (2 * np.pi)
    return {
        "w_perm": w_perm,
        "w_out": np.ascontiguousarray(np.asarray(w_out[0], f)),
        "wuv": np.ascontiguousarray(np.asarray(w_uv[0], f).reshape(128, 512)),
        "uqnT": uqnT, "ukT": ukT, "wpe": wpe,
        "gmix": np.ascontiguousarray(np.asarray(g_mix[0], f).reshape(8, 128).T),
        "gcq": np.ascontiguousarray(np.asarray(g_cq[0], f).reshape(2, 128).T),
        "gckv_bc": np.ascontiguousarray(np.broadcast_to(np.asarray(g_ckv[0], f)[None, :], (128, 128))),
        "gfin_bc": np.ascontiguousarray(np.broadcast_to(np.asarray(g_final, f)[None, :], (128, D))),
        "ropec": ropec.astype(f),
        "rb": np.ascontiguousarray(np.asarray(rel_bias[0], f)),
        "ident": np.eye(128, dtype=f),
    }


_NC_CACHE = {}


def kernel(x_prompt, x_sample, cache_ckv, cache_kpe, cache_kb, cache_vb, w_in, g_mix, g_cq, w_uq, g_ckv, w_uk, w_uv,
           rel_bias, w_out, g_final, _nblk=8, _sample=True):
    f = np.float32
    shared = _prep_shared(w_in, g_mix, g_cq, w_uq, g_ckv, w_uk, w_uv, rel_bias, w_out, g_final)
    key = (_nblk, _sample)
    if key not in _NC_CACHE:
        _NC_CACHE[key] = build_nc(_nblk, _sample)[0]
    nc = _NC_CACHE[key]
    in_maps = []
    for i in range(8):
        m = dict(shared)
        m["x"] = np.ascontiguousarray(np.asarray(x_prompt[i], f))
        m["xs"] = np.ascontiguousarray(np.asarray(x_sample[i], f))
        m["cckv"] = np.ascontiguousarray(np.asarray(cache_ckv[0, i], f))
        m["ckpe"] = np.ascontiguousarray(np.asarray(cache_kpe[0, i], f))
        m["ckb"] = np.ascontiguousarray(np.asarray(cache_kb[0, i], f).reshape(512, 512))
        m["cvb"] = np.ascontiguousarray(np.asarray(cache_vb[0, i], f).reshape(512, 512))
        in_maps.append(m)
    res = run_bass_kernel_spmd(nc, in_maps, core_ids=list(range(8)))
    R = res.results
    st = lambda k: np.stack([np.asarray(R[i][k], f) for i in range(8)], axis=0)
    y = st("y")
    ys = st("ys")
    return (y, ys,
            st("ckv_o")[None], st("kpe_o")[None],
            st("kb_o").reshape(8, 512, 8, 64)[None], st("vb_o").reshape(8, 512, 8, 64)[None],
            st("ckvs_o")[None], st("kpes_o")[None],
            st("kbs_o").reshape(8, TS, 8, 64)[None], st("vbs_o").reshape(8, TS, 8, 64)[None])
```

```python
import contextlib
import numpy as np
import concourse.bass as bass
import concourse.mybir as mybir
from concourse.bass_utils import run_bass_kernel_spmd

F32 = mybir.dt.float32
BF16 = mybir.dt.bfloat16
I32 = mybir.dt.int32
AF = mybir.ActivationFunctionType
ALU = mybir.AluOpType

D = 1024
SEQ = 4096
TS = 32
PAST = 4096
EPS = 1e-6
MLA_SCALE = 96.0 ** -0.5
B_SCALE = 64.0 ** -0.5
NEGM = -30000.0
DBG_STAGE = 99
DBG_A = 99


class _Stop(Exception):
    pass


def _sec(k):
    if k > DBG_A:
        raise _Stop()
C_CQ, C_GA, C_GB, C_QB, C_KB, C_KR, C_CKV, C_VB, NCOL = 0, 256, 768, 1280, 1792, 2304, 2368, 2496, 3008


class Instr:
    __slots__ = ("eng", "fn", "deps", "signal", "sig_idx", "is_dma", "slot", "dma_val")

    def __init__(self, eng, fn):
        self.eng = eng
        self.fn = fn
        self.deps = []
        self.signal = False
        self.sig_idx = 0
        self.is_dma = False
        self.slot = None
        self.dma_val = 0


class Prog:
    ENGS = ("pe", "act", "dve", "pool", "sp")

    def __init__(self):
        self.q = {e: [] for e in self.ENGS}
        self.last_writer = {}
        self.readers = {}
        self.slot_count = {}
        self.slot_last = {}
        self.n = 0

    def _track(self, ins, reads, writes):
        deps = []
        for k in reads:
            w = self.last_writer.get(k)
            if w is not None:
                deps.append((w, "raw"))
            if isinstance(k, tuple) and k[0] == "ps":
                for r in self.readers.get(k, ()):
                    deps.append((r, "war"))
        for k in writes:
            w = self.last_writer.get(k)
            if w is not None:
                deps.append((w, "waw"))
            for r in self.readers.get(k, ()):
                deps.append((r, "war"))
        for k in writes:
            self.last_writer[k] = ins
            self.readers[k] = []
        for k in reads:
            if k not in writes:
                self.readers.setdefault(k, []).append(ins)
        seen = set()
        for d, kind in deps:
            if d is ins or id(d) in seen:
                continue
            seen.add(id(d))
            ins.deps.append((d, kind))

    def op(self, eng, fn, reads=(), writes=()):
        ins = Instr(eng, fn)
        self.n += 1
        self._track(ins, tuple(reads), tuple(writes))
        self.q[eng].append(ins)
        return ins

    def dma(self, fn, reads=(), writes=(), slot=None, eng="sp"):
        ins = Instr(eng, fn)
        self.n += 1
        ins.is_dma = True
        ins.slot = slot
        self.slot_count[slot] = self.slot_count.get(slot, 0) + 1
        ins.dma_val = 16 * self.slot_count[slot]
        self._track(ins, tuple(reads), tuple(writes))
        prev = self.slot_last.get(slot)
        if prev is not None and all(d is not prev for d, _ in ins.deps):
            ins.deps.append((prev, "raw"))
        self.slot_last[slot] = ins
        self.q[eng].append(ins)
        return ins

    def emit(self, nc):
        for e in self.ENGS:
            for ins in self.q[e]:
                for d, kind in ins.deps:
                    if d.is_dma:
                        continue
                    if d.eng == ins.eng and d.eng in ("pe", "sp"):
                        continue
                    d.signal = True
        for e in self.ENGS:
            c = 0
            for ins in self.q[e]:
                if ins.signal and not ins.is_dma:
                    c += 1
                    ins.sig_idx = c
        slots = sorted(self.slot_count.keys(), key=str)
        with contextlib.ExitStack() as st:
            esem = {e: st.enter_context(nc.semaphore("s_" + e)) for e in self.ENGS}
            ssem = {s: st.enter_context(nc.semaphore("d_%d" % i)) for i, s in enumerate(slots)}
            block = st.enter_context(nc.Block())
            prog = self

            def run(engname, eh):
                waited = {}
                for ins in prog.q[engname]:
                    for d, kind in ins.deps:
                        if d.is_dma:
                            sem, val, key = ssem[d.slot], d.dma_val, ("d", d.slot)
                        else:
                            if d.eng == engname and engname in ("pe", "sp"):
                                continue
                            sem, val, key = esem[d.eng], d.sig_idx, ("e", d.eng)
                        if waited.get(key, 0) >= val:
                            continue
                        waited[key] = val
                        eh.wait_ge(sem, val)
                    h = ins.fn(eh)
                    if ins.is_dma:
                        h.then_inc(ssem[ins.slot], 16)
                    elif ins.signal:
                        h.then_inc(esem[engname], 1)
                if engname == "sp":
                    for s in slots:
                        eh.wait_ge(ssem[s], 16 * prog.slot_count[s])

            @block.tensor
            def _(eh):
                run("pe", eh)

            @block.scalar
            def _(eh):
                run("act", eh)

            @block.vector
            def _(eh):
                run("dve", eh)

            @block.gpsimd
            def _(eh):
                run("pool", eh)

            @block.sync
            def _(eh):
                run("sp", eh)
        return self.n


def build_nc(NBLK=8, SAMPLE=True):
    nc = bass.Bass("TRN2", target_bir_lowering=False, dynamic_dma_scratch_size=1024)

    def din(name, shape):
        return nc.dram_tensor(name, list(shape), F32, kind="ExternalInput").ap()

    def dout(name, shape):
        return nc.dram_tensor(name, list(shape), F32, kind="ExternalOutput").ap()

    x_d = din("x", [SEQ, D])
    xs_d = din("xs", [TS, D])
    cckv_d = din("cckv", [PAST, 128])
    ckpe_d = din("ckpe", [PAST, 32])
    ckb_d = din("ckb", [512, 512])
    cvb_d = din("cvb", [512, 512])
    w_d = din("w_perm", [D, NCOL])
    wout_d = din("w_out", [D, D])
    wuv_d = din("wuv", [128, 512])
    uqn_d = din("uqnT", [64, 8 * 256])
    uk_d = din("ukT", [64, 8 * 128])
    wpe_d = din("wpe", [256, 512])
    gmix_d = din("gmix", [128, 8])
    gcq_d = din("gcq", [128, 2])
    gckv_d = din("gckv_bc", [128, 128])
    gfin_d = din("gfin_bc", [128, D])
    ropec_d = din("ropec", [64, 4])
    rb_d = din("rb", [8, 257])
    ident_d = din("ident", [128, 128])

    y_o = dout("y", [SEQ, D])
    ys_o = dout("ys", [TS, D])
    ckv_o = dout("ckv_o", [SEQ, 128])
    kpe_o = dout("kpe_o", [SEQ, 32])
    kb_o = dout("kb_o", [512, 512])
    vb_o = dout("vb_o", [512, 512])
    ckvs_o = dout("ckvs_o", [TS, 128])
    kpes_o = dout("kpes_o", [TS, 32])
    kbs_o = dout("kbs_o", [TS, 512])
    vbs_o = dout("vbs_o", [TS, 512])
    scr = nc.dram_tensor("scr", [8, 128, 384], F32, kind="Internal").ap()
    tabd = nc.dram_tensor("tabd", [64, SEQ + TS], F32, kind="Internal").ap()

    P = Prog()
    st = contextlib.ExitStack()
    with st:
        def sb(name, shape, dt):
            return st.enter_context(nc.sbuf_tensor("sb_" + name, list(shape), dt))

        def psum(name, shape, dt):
            return st.enter_context(nc.psum_tensor("ps_" + name, list(shape), dt))

        W = sb("W", [128, 8, NCOL], BF16)
        Wout = sb("Wout", [128, 8, D], BF16)
        Wlat = sb("Wlat", [128, 2, 8, 128], BF16)
        Wpe = sb("Wpe", [128, 2, 8, 64], BF16)
        Wuv = sb("Wuv", [128, 512], BF16)
        gmix = sb("gmix", [128, 8], F32)
        gcq = sb("gcq", [128, 2], F32)
        gckv = sb("gckv", [128, 128], F32)
        gfin = sb("gfin", [128, D], F32)
        ropec = sb("ropec", [64, 4], F32)
        identf = sb("identf", [32, 32], F32)
        identb = sb("identb", [128, 128], BF16)
        onesb = sb("onesb", [128, 128], BF16)
        tabblk = sb("tabblk", [64, 1, 512], F32)
        BT = sb("BT", [128, 8, 256], F32)
        cb = sb("cb", [128, 8], F32)
        cbm = sb("cbm", [128, 8], F32)
        KTc = sb("KTc", [128, SEQ], BF16)
        KTp = sb("KTp", [128, SEQ], BF16)
        V = sb("V", [128, 32, 128], BF16)
        kbT = sb("kbT", [128, 4, 2, 512], BF16)
        vbx = sb("vbx", [128, 8, 8, 128], BF16)
        stage = sb("stage", [128, 2, 1024], F32)
        xnb = sb("xnb", [128, D], BF16)
        xnT = sb("xnT", [128, 8, 512], BF16)
        cqT = sb("cqT", [128, 2, 512], BF16)
        sq = sb("sq", [128, 2, 512], BF16)
        rq = sb("rq", [128, 512], F32)
        QlatT = sb("QlatT", [128, 8, 512], BF16)
        QpeT = sb("QpeT", [128, 8, 512], BF16)
        OlatT = QlatT
        gates = sb("gates", [128, 8, 512], BF16)
        qbT = sb("qbT", [128, 4, 2, 512], BF16)
        yT = xnT
        tmpA = sb("tmpA", [128, 2, 512], F32)
        tmpB = sb("tmpB", [128, 2, 512], F32)
        PT = sb("PT", [128, 4, 512], BF16)
        PTb = sb("PTb", [128, 2, 512], BF16)
        sbb = sb("sbb", [128, 2, 512], F32)
        rec = sb("rec", [128, 2, 512], F32)
        small = sb("small", [128, 64], F32)
        acc = sb("acc", [128, 2, 512], F32)
        acch = sb("acch", [128, 2, 512], BF16)
        maskv = sb("maskv", [128, 1], F32)
        ckv32 = sb("ckv32", [128, 2, 128], F32)
        ckvb = sb("ckvb", [128, 128], BF16)
        kpst = sb("kpst", [128, 4, 32], F32)
        KTc_s = sb("KTc_s", [128, 32], BF16)
        KTp_s = sb("KTp_s", [128, 32], BF16)
        V_s = sb("V_s", [32, 128], BF16)
        vbx_s = sb("vbx_s", [32, 8, 128], BF16)
        kbT_s = sb("kbT_s", [128, 4, 32], BF16)

        pb = [psum("pb%d" % i, [128, 512], F32) for i in range(7)]
        pT = psum("pT", [128, 1024], BF16)
        PSK = [("ps", i) for i in range(7)]
        PTK = ("ps", 7)
        pT32 = pT.bitcast(F32)

        def mm(out, lhsT, rhs, start, stop, reads, writes):
            P.op("pe", lambda e: e.matmul(out, lhsT=lhsT, rhs=rhs, start=start, stop=stop), reads, writes)

        def tr(out, in_, ident, reads, writes):
            P.op("pe", lambda e: e.transpose(out=out, in_=in_, identity=ident), reads, writes)

        def act(out, in_, func, reads, writes, **kw):
            P.op("act", lambda e: e.activation(out=out, in_=in_, func=func, **kw), reads, writes)

        def tt(eng, out, in0, in1, op, reads, writes):
            P.op(eng, lambda e: e.tensor_tensor(out=out, in0=in0, in1=in1, op=op), reads, writes)

        def ts(eng, out, in0, s1, s2, op0, op1, reads, writes):
            if op1 is None:
                P.op(eng, lambda e: e.tensor_scalar(out=out, in0=in0, scalar1=s1, scalar2=None, op0=op0), reads, writes)
            else:
                P.op(eng, lambda e: e.tensor_scalar(out=out, in0=in0, scalar1=s1, scalar2=s2, op0=op0, op1=op1), reads, writes)

        def stt(out, in0, scalar, in1, op0, op1, reads, writes):
            P.op("dve", lambda e: e.scalar_tensor_tensor(out=out, in0=in0, scalar=scalar, in1=in1, op0=op0, op1=op1), reads, writes)

        def cp(eng, out, in_, reads, writes):
            if eng == "act":
                act(out, in_, AF.Copy, reads, writes)
            else:
                P.op(eng, lambda e: e.tensor_copy(out=out, in_=in_), reads, writes)

        def mset(eng, ap, val, writes):
            P.op(eng, lambda e: e.memset(ap, val), (), writes)

        def dma(out, in_, reads, writes, slot):
            P.dma(lambda e: e.dma_start(out=out, in_=in_), reads, writes, slot)

        sm_ctr = [0]

        def smcol():
            c = sm_ctr[0] % 64
            sm_ctr[0] += 1
            return c

        def rstd_from_ss(ss_ap, ss_key, dim, n):
            c1, c2 = smcol(), smcol()
            l_ap = small[0:n, c1:c1 + 1]
            r_ap = small[0:n, c2:c2 + 1]
            act(l_ap, ss_ap, AF.Ln, [ss_key], [("sm", c1)], scale=1.0 / dim, bias=EPS)
            act(r_ap, l_ap, AF.Exp, [("sm", c1)], [("sm", c2)], scale=-0.5)
            return r_ap, ("sm", c2)

        TA2 = [("tmpA", 0), ("tmpA", 1)]
        TB2 = [("tmpB", 0), ("tmpB", 1)]
        SB2 = [("sbb", 0), ("sbb", 1)]
        RC2 = [("rec", 0), ("rec", 1)]
        SQ2 = [("sq", 0), ("sq", 1)]
        sbbI = sbb.bitcast(I32)
        kpe32 = rec[0:32, 1, :]
        G = rec[0:8, :, :].rearrange("p a b -> p (a b)")
        cst = tmpA[:, :, :].rearrange("p a b -> p (a b)")
        cstb = sq[:, :, :].rearrange("p a b -> p (a b)")
        dma(gmix[:], gmix_d, [], ["gmix"], "c0")
        dma(gcq[:], gcq_d, [], ["gcq"], "c1")
        dma(ropec[:], ropec_d, [], ["ropec"], "c4")
        dma(G[:, 0:256], rb_d[:, 1:257], [], RC2, "c6")
        dma(stage[:, 0, 0:128], ident_d, [], [("stage", 0)], ("wst", 0))
        dma(gckv[:], gckv_d, [], ["gckv"], "c2")
        dma(gfin[:], gfin_d, [], ["gfin"], "c3")
        ts("dve", G[:, 256:384], G[:, 0:128], 0.0, G[:, 255:256], ALU.mult, ALU.add, RC2, RC2)

        def g2scr(e):
            g = G[:, 0:384]
            src = bass.AP(g.tensor, g.offset, [list(g.ap[0]), [0, 128], [1, 384]])
            return e.dma_start(out=scr, in_=src)
        P.dma(g2scr, RC2, ["scr"], "c7")
        cp("dve", identb[:], stage[:, 0, 0:128], [("stage", 0)], ["identb"])
        cp("dve", identf[:], stage[0:32, 0, 0:32], [("stage", 0)], ["identf"])

        mset("pool", onesb[:], 1.0, ["onesb"])
        mset("pool", KTp[:, :], 0.0, [("KTp", t) for t in range(32)])
        mset("pool", QpeT[:, :, :], 0.0, [("Qp", h) for h in range(8)])
        mset("pool", KTp_s[:, :], 0.0, ["KTp_s"])
        mset("pool", qbT[:, :, :, :], 0.0, [("qbT", p_) for p_ in range(4)])
        mset("pool", vbx[:, :, :, :], 1.0, [("vbx", t) for t in range(8)])
        mset("pool", vbx_s[:, :, :], 1.0, ["vbx_s"])
        mset("pool", maskv[:, :], 0.0, ["maskv"])
        mset("pool", maskv[64:128, :], NEGM, ["maskv"])

        NT = SEQ + TS

        def table_chunk(c0):
            n = min(1024, NT - c0)
            tA = tmpA[0:64, :, :].rearrange("p a b -> p (a b)")[:, 0:n]
            tB = tmpB[0:64, :, :].rearrange("p a b -> p (a b)")[:, 0:n]
            tI = sbbI[0:64, :, :].rearrange("p a b -> p (a b)")[:, 0:n]
            P.op("pool", lambda e, tA=tA, c0=c0, n=n: e.iota(tA, pattern=[[1, n]], base=c0, channel_multiplier=0,
                                                             allow_small_or_imprecise_dtypes=True), (), TA2)
            ts("dve", tB, tA, ropec[:, 0:1], ropec[:, 1:2], ALU.mult, ALU.add, TA2 + ["ropec"], TB2)
            cp("dve", tI, tB, TB2, SB2)
            cp("dve", tA, tI, SB2, TA2)
            tt("dve", tB, tB, tA, ALU.subtract, TA2 + TB2, TB2)
            act(tA, tB, AF.Sin, TB2 + ["ropec"], TA2, scale=ropec[:, 2:3])
            P.dma(lambda e, c0=c0, n=n, tA=tA: e.dma_start(out=tabd[:, c0:c0 + n], in_=tA), TA2, ["tabd"], "c9", eng="act")

        gatesF = gates.bitcast(F32)[:, :, :].rearrange("p a b -> p (a b)")
        xyF = xnT.bitcast(F32)[:, :, :].rearrange("p a b -> p (a b)")
        wslots = [(stage[:, 0, :], [("stage", 0)], ("wst", 0)), (stage[:, 1, :], [("stage", 1)], ("wst", 1)),
                  (gatesF[:, 0:1024], [("gate", g_) for g_ in range(0, 4)], "wst2"),
                  (gatesF[:, 1024:2048], [("gate", g_) for g_ in range(4, 8)], "wst3"),
                  (xyF[:, 0:1024], [("XY", g_, t_) for g_ in range(0, 4) for t_ in range(4)], "wst4"),
                  (xyF[:, 1024:2048], [("XY", g_, t_) for g_ in range(4, 8) for t_ in range(4)], "wst5")]
        pieces = [(0, 1024, "act"), (1024, 2048, "dve"), (2048, NCOL, "act")]
        wi = 0
        for dc in range(8):
            for pi_, (c0, c1, eng) in enumerate(pieces):
                buf, bkeys, bslot = wslots[wi % 6]
                wi += 1
                dma(buf[:, 0:c1 - c0], w_d[dc * 128:(dc + 1) * 128, c0:c1], [], bkeys, bslot)
                if eng == "act":
                    act(W[:, dc, c0:c1], buf[:, 0:c1 - c0], AF.Copy, bkeys + ["gmix"], [("W", dc, pi_)], scale=gmix[:, dc:dc + 1])
                else:
                    ts("dve", W[:, dc, c0:c1], buf[:, 0:c1 - c0], gmix[:, dc:dc + 1], None, ALU.mult, None,
                       bkeys + ["gmix"], [("W", dc, pi_)])
            if dc * 1024 < NT:
                table_chunk(dc * 1024)
        for dc in range(8):
            buf, bkeys, bslot = wslots[wi % 6]
            wi += 1
            dma(buf[:, 0:D], wout_d[dc * 128:(dc + 1) * 128, :], [], bkeys, bslot)
            cp("pool" if dc % 2 else "dve", Wout[:, dc, :], buf[:, 0:D], bkeys, [("Wout", dc)])
        WK = [("W", dc, i) for dc in range(8) for i in range(3)]
        WOK = [("Wout", dc) for dc in range(8)]

        def scr2bt(e):
            src = bass.AP(scr.tensor, scr.offset + 127, [[383, 128], [128 * 384, 8], [1, 256]])
            return e.dma_start(out=BT[:], in_=src)
        P.dma(scr2bt, ["scr"], ["BT"], "c8")
        cp("dve", cb[:, :], BT[:, :, 255], ["BT"], ["cb"])
        cp("dve", cbm[:, :], BT[:, :, 255], ["BT"], ["cbm"])
        mset("pool", cbm[0:64, :], NEGM, ["cbm"])
        mset("pool", BT[64:128, :, 0:64], NEGM, ["BT"])
        act(BT[:, :, :].rearrange("p a b -> p (a b)"), BT[:, :, :].rearrange("p a b -> p (a b)"), AF.Exp, ["BT", "cb", "cbm"], ["BT"])

        dma(stage[:, 0, 0:512], wuv_d, [], [("stage", 0)], ("wst", 0))
        cp("dve", Wuv[:], stage[:, 0, 0:512], [("stage", 0)], ["Wuv"])
        dma(stage[:, 1, 0:1024].rearrange("p (a b) -> p a b", a=2), wpe_d.rearrange("(a p) n -> p a n", p=128), [], [("stage", 1)], ("wst", 1))
        for rc in range(2):
            ts("dve", Wpe[:, rc, :, :].rearrange("p a b -> p (a b)"), stage[:, 1, rc * 512:(rc + 1) * 512], gcq[:, rc:rc + 1], None,
               ALU.mult, None, [("stage", 1), "gcq"], ["Wpe"])
        CQ2 = [("cqT", 0), ("cqT", 1)]
        ukb = sq[0:64, :, :].rearrange("p a b -> p (a b)")
        uqb = cqT[0:64, :, :].rearrange("p a b -> p (a b)")
        dma(stage[0:64, 1, 0:1024], uk_d, [], [("stage", 1)], ("wst", 1))
        cp("dve", ukb, stage[0:64, 1, 0:1024], [("stage", 1)], SQ2)
        for hg in range(2):
            dma(stage[0:64, 0, 0:1024], uqn_d[:, hg * 1024:(hg + 1) * 1024], [], [("stage", 0)], ("wst", 0))
            cp("dve", uqb, stage[0:64, 0, 0:1024], [("stage", 0)], CQ2)
            for rc in range(2):
                bank, bk = pb[rc], PSK[rc]
                for hl in range(4):
                    h = hg * 4 + hl
                    mm(bank[:, hl * 128:(hl + 1) * 128], uqb[:, hl * 256 + rc * 128: hl * 256 + (rc + 1) * 128],
                       ukb[:, h * 128:(h + 1) * 128], True, True, SQ2 + CQ2, [bk])
                ts("dve", Wlat[:, rc, hg * 4:(hg + 1) * 4, :].rearrange("p a b -> p (a b)"), bank[:, :], gcq[:, rc:rc + 1], None,
                   ALU.mult, None, [bk, "gcq"], ["Wlat"])

        bank_rr = [0]

        def nextbank(choices=(0, 1, 6, 2, 3, 4, 5)):
            i = choices[bank_rr[0] % len(choices)]
            bank_rr[0] += 1
            return pb[i], PSK[i]

        xslot = [0]

        def load_x_tile(src_ap, tsz):
            s = xslot[0] % 2
            xslot[0] += 1
            xt = stage[0:tsz, s, 0:D]
            dma(xt, src_ap, [], [("stage", s)], ("wst", s))
            return xt, ("stage", s)

        def XYt(t_):
            return [("XY", fc, t_) for fc in range(8)]

        def XYall(ntile_):
            return [("XY", fc, t_) for fc in range(8) for t_ in range(ntile_)]

        def XYfc(fc, ntile_):
            return [("XY", fc, t_) for t_ in range(ntile_)]
        tab_rr = [0]

        def a_front_tile(xt, xkeys, tsz, tti):
            c = smcol()
            ss = small[0:tsz, c:c + 1]
            jk = PT[0:tsz, 2:4, :].rearrange("p a b -> p (a b)")
            act(jk, xt, AF.Square, xkeys, [("PT", 2), ("PT", 3), ("sm", c)], accum_out=ss)
            r_ap, rk = rstd_from_ss(ss, ("sm", c), D, tsz)
            act(xnb[0:tsz, :], xt, AF.Copy, xkeys + [rk], ["xnb"], scale=r_ap)
            for dc in range(8):
                tr(pT[:, dc * tsz:(dc + 1) * tsz], xnb[0:tsz, dc * 128:(dc + 1) * 128], identb[0:tsz, 0:tsz],
                   ["xnb", "identb"], [PTK])
            cp("dve", xnT[:, :, tti * tsz:(tti + 1) * tsz], pT[:, 0:8 * tsz].rearrange("p (a b) -> p a b", a=8),
               [PTK], XYt(tti))

        def phase_a(x_src, t0, nt, tsz, tab0, KTc_dst, KTp_dst, V_dst, kv_keys, kb_dst, kb_keys, vbx_dst, vbx_key,
                    ckv_out, kpe_out, kb_out, vb_out, front_done=False):
            ntile = nt // tsz
            tsl = 0
            TKEY = ("tab", tsl)
            dma(tabblk[:, tsl, 0:nt], tabd[:, tab0:tab0 + nt], ["tabd"], [TKEY], ("tabld", tsl))
            cosT = tabblk[0:32, tsl, 0:nt]
            sinT = tabblk[32:64, tsl, 0:nt]
            if not front_done:
                for tti in range(ntile):
                    xt, xk = load_x_tile(x_src[t0 + tti * tsz: t0 + (tti + 1) * tsz, :], tsz)
                    a_front_tile(xt, [xk], tsz, tti)
            XK = XYall(ntile)

            def fm_group(col0, m):
                bank, bk = nextbank()
                for dc in range(8):
                    mm(bank[0:m, 0:nt], W[:, dc, col0:col0 + m], xnT[:, dc, 0:nt], dc == 0, dc == 7, WK + XK, [bk])
                return bank, bk

            _sec(2)
            for rc in range(2):
                bank, bk = fm_group(C_CQ + rc * 128, 128)
                cp("dve", cqT[:, rc, 0:nt], bank[:, 0:nt], [bk], [("cqT", rc)])
                act(sq[:, rc, 0:nt], bank[:, 0:nt], AF.Square, [bk], [("sq", rc)])
            bank, bk = nextbank()
            for rc in range(2):
                mm(bank[:, 0:nt], onesb[:, :], sq[:, rc, 0:nt], rc == 0, rc == 1, ["onesb", ("sq", rc)], [bk])
            act(tmpA[:, 0, 0:nt], bank[:, 0:nt], AF.Ln, [bk], [("tmpA", 0)], scale=1.0 / 256, bias=EPS)
            act(rq[:, 0:nt], tmpA[:, 0, 0:nt], AF.Exp, [("tmpA", 0)], ["rq"], scale=-0.5)
            CQK = [("cqT", 0), ("cqT", 1)]
            _sec(3)
            for h in range(8):
                bank, bk = nextbank()
                for rc in range(2):
                    mm(bank[:, 0:nt], Wlat[:, rc, h, :], cqT[:, rc, 0:nt], rc == 0, rc == 1, ["Wlat"] + CQK, [bk])
                tt("dve", QlatT[:, h, 0:nt], bank[:, 0:nt], rq[:, 0:nt], ALU.mult, [bk, "rq"], [("QO", h, jl) for jl in range(8)])
            _sec(4)
            cosq = tmpB[0:32, 0, 0:nt]
            sinq = tmpB[32:64, 0, 0:nt]
            tt("dve", tmpB[0:64, 0, 0:nt], tabblk[0:64, tsl, 0:nt], rq[0:64, 0:nt], ALU.mult, [TKEY, "rq"], [("tmpB", 0)])
            for hp in range(4):
                bank, bk = nextbank()
                for rc in range(2):
                    mm(bank[:, 0:nt], Wpe[:, rc, 2 * hp:2 * hp + 2, :].rearrange("p a b -> p (a b)"), cqT[:, rc, 0:nt], rc == 0, rc == 1,
                       ["Wpe"] + CQK, [bk])
                for hh in range(2):
                    h = 2 * hp + hh
                    s = hh
                    r0 = 64 * hh
                    tt("dve", sbb[0:32, s, 0:nt], bank[r0:r0 + 32, 0:nt], cosq, ALU.mult, [bk, ("tmpB", 0)], [("sbb", s)])
                    tt("dve", rec[0:32, s, 0:nt], bank[r0 + 32:r0 + 64, 0:nt], sinq, ALU.mult, [bk, ("tmpB", 0)], [("rec", s)])
                    tt("dve", QpeT[0:32, h, 0:nt], sbb[0:32, s, 0:nt], rec[0:32, s, 0:nt], ALU.add, [("sbb", s), ("rec", s)], [("Qp", h)])
            _sec(5)
            for gi in range(8):
                col0 = (C_GA if gi < 4 else C_GB) + (gi % 4) * 128
                bank, bk = fm_group(col0, 128)
                s = gi % 2
                act(tmpA[:, s, 0:nt], bank[:, 0:nt], AF.Exp, [bk], [("tmpA", s)], scale=-1.0)
                act(tmpA[:, s, 0:nt], tmpA[:, s, 0:nt], AF.Ln, [("tmpA", s)], [("tmpA", s)], bias=1.0)
                act(tmpA[:, s, 0:nt], tmpA[:, s, 0:nt], AF.Exp, [("tmpA", s)], [("tmpA", s)], scale=-1.0)
                tt("dve", gates[:, gi, 0:nt], bank[:, 0:nt], tmpA[:, s, 0:nt], ALU.mult, [bk, ("tmpA", s)], [("gate", gi)])
            _sec(6)
            for p_ in range(4):
                bank, bk = fm_group(C_QB + p_ * 128, 128)
                cp("act", qbT[0:64, p_, 0, 0:nt], bank[0:64, 0:nt], [bk], [("qbT", p_)])
                cp("act", qbT[64:128, p_, 1, 0:nt], bank[64:128, 0:nt], [bk], [("qbT", p_)])
            for p_ in range(4):
                bank, bk = fm_group(C_KB + p_ * 128, 128)
                cp("dve", kb_dst(p_), bank[:, 0:nt], [bk], kb_keys(p_))
            _sec(7)
            bank, bk = fm_group(C_KR, 128)
            tt("dve", tmpB[0:32, 1, 0:nt], bank[0:32, 0:nt], cosT, ALU.mult, [bk, TKEY], [("tmpB", 1)])
            tt("dve", sbb[0:32, 0, 0:nt], bank[32:64, 0:nt], sinT, ALU.mult, [bk, TKEY], [("sbb", 0)])
            tt("dve", kpe32[:, 0:nt], tmpB[0:32, 1, 0:nt], sbb[0:32, 0, 0:nt], ALU.add, [("tmpB", 1), ("sbb", 0)], [("rec", 1)])
            cp("dve", KTp_dst, kpe32[:, 0:nt], [("rec", 1)], kv_keys[1])

            def kpe_out_tr():
                bank, bk = nextbank((6,))
                for tti in range(ntile):
                    tr(bank[0:tsz, tti * 32:(tti + 1) * 32], kpe32[0:32, tti * tsz:(tti + 1) * tsz], identf[0:32, 0:32],
                       [("rec", 1), "identf"], [bk])
                cp("act", kpst[0:tsz, 0:ntile, :], bank[0:tsz, 0:ntile * 32].rearrange("p (a b) -> p a b", a=ntile), [bk], ["kpst"])
                dma(kpe_out, kpst[0:tsz, 0:ntile, :], ["kpst"], [], "o_kpe")
            _sec(8)
            pend_tr = [kpe_out_tr]
            for tti in range(ntile):
                tok = slice(tti * tsz, (tti + 1) * tsz)
                bank, bk = nextbank()
                for dc in range(8):
                    mm(bank[0:tsz, 0:128], xnT[:, dc, tok], W[:, dc, C_CKV:C_CKV + 128], dc == 0, dc == 7, WK + XK, [bk])
                c = smcol()
                ss = small[0:tsz, c:c + 1]
                act(ckvb[0:tsz, :], bank[0:tsz, 0:128], AF.Square, [bk], ["ckvb", ("sm", c)], accum_out=ss)
                r_ap, rk = rstd_from_ss(ss, ("sm", c), 128, tsz)
                s = tti % 2
                stt(ckv32[0:tsz, s, :], bank[0:tsz, 0:128], r_ap, gckv[0:tsz, :], ALU.mult, ALU.mult, [bk, rk, "gckv"], [("ckv32", s)])
                dma(ckv_out(tti), ckv32[0:tsz, s, :], [("ckv32", s)], [], ("o_ckv", s))
                cp("act", V_dst(tti), ckv32[0:tsz, s, :], [("ckv32", s)], [kv_keys[2](tti)])

                def do_tr(tti=tti):
                    tr(pT[:, 0:tsz], V_dst(tti), identb[0:tsz, 0:tsz], [kv_keys[2](tti), "identb"], [PTK])
                    cp("dve", KTc_dst(tti), pT[:, 0:tsz], [PTK], [kv_keys[0](tti)])
                bank, bk = nextbank()
                for dc in range(8):
                    mm(bank[0:tsz, :], xnT[:, dc, tok], W[:, dc, C_VB:C_VB + 512], dc == 0, dc == 7, WK + XK, [bk])
                cp("act", vbx_dst(tti)[:, :, 0:64], bank[0:tsz, :].rearrange("p (a b) -> p a b", a=8), [bk], [vbx_key(tti)])
                if vb_out is not None:
                    cp("dve", sbb[0:tsz, 0, :], bank[0:tsz, :], [bk], [("sbb", 0)])
                    dma(vb_out(tti), sbb[0:tsz, 0, :], [("sbb", 0)], [], ("o_kv", 0))
                if kb_out is not None:
                    bank, bk = nextbank()
                    for dc in range(8):
                        mm(bank[0:tsz, :], xnT[:, dc, tok], W[:, dc, C_KB:C_KB + 512], dc == 0, dc == 7, WK + XK, [bk])
                    cp("dve", sbb[0:tsz, 1, :], bank[0:tsz, :], [bk], [("sbb", 1)])
                    dma(kb_out(tti), sbb[0:tsz, 1, :], [("sbb", 1)], [], ("o_kv", 1))
                while pend_tr:
                    pend_tr.pop(0)()
                pend_tr.append(do_tr)
            while pend_tr:
                pend_tr.pop(0)()

        pt_rr = [0]
        sbank_rr = [0]
        ol_rr = [0]
        mla_pend = {"q": []}
        PV_LAG = 2

        def mla_chunk_steps(nq, jl, tiles):
            ncol = 8 * nq
            qcols = slice(jl * nq, (jl + 1) * nq)
            par = ol_rr[0] % 2
            cid = ol_rr[0]
            ol_rr[0] += 1
            psO, kO = pb[2 + par], PSK[2 + par]
            psL, kL = pb[4], PSK[4]
            QK = [("QO", h, jl) for h in range(8)] + [("Qp", h) for h in range(8)]
            nt_ = len(tiles)
            state = mla_pend

            def issue_pv(i, pi):
                ktc, ktp, v, M, keys = tiles[i][:5]
                mm(psO[:, 0:ncol], v, PT[0:M, pi, 0:ncol], i == 0, i == nt_ - 1, [("PT", pi)] + list(keys), [kO])

            def mk(i):
                def step():
                    ktc, ktp, v, M, keys = tiles[i][:5]
                    sbk = sbank_rr[0] % 2
                    sbank_rr[0] += 1
                    psS, kS = pb[sbk], PSK[sbk]
                    out_ap = psS[0:M, 0:ncol].rearrange("p (a b) -> p a b", a=8)
                    mm(out_ap, ktc, QlatT[:, :, qcols], True, False, QK + list(keys), [kS])
                    mm(out_ap, ktp, QpeT[:, :, qcols], False, True, QK + list(keys), [kS])
                    pi = pt_rr[0] % 4
                    pt_rr[0] += 1
                    if len(tiles[i]) > 5 and tiles[i][5]:
                        act(PT[0:M, pi, 0:ncol], psS[0:M, 0:ncol], AF.Exp, [kS, "maskv"], [("PT", pi)], scale=MLA_SCALE, bias=maskv[:, 0:1])
                    else:
                        act(PT[0:M, pi, 0:ncol], psS[0:M, 0:ncol], AF.Exp, [kS], [("PT", pi)], scale=MLA_SCALE)
                    e_ = i % 2
                    if i < 2 and M == 128:
                        cp("dve", acc[:, e_, 0:ncol], PT[:, pi, 0:ncol], [("PT", pi)], [("acc", e_)])
                    else:
                        if i < 2:
                            mset("pool", acc[:, e_, 0:ncol], 0.0, [("acc", e_)])
                        tt("dve", acc[0:M, e_, 0:ncol], acc[0:M, e_, 0:ncol], PT[0:M, pi, 0:ncol], ALU.add,
                           [("acc", e_), ("PT", pi)], [("acc", e_)])
                    state["q"].append((cid, lambda i=i, pi=pi: issue_pv(i, pi)))
                    while len(state["q"]) > PV_LAG:
                        state["q"].pop(0)[1]()
                return step

            def fin_a():
                for e_ in range(min(nt_, 2)):
                    cp("dve", acch[:, e_, 0:ncol], acc[:, e_, 0:ncol], [("acc", e_)], [("acch", e_)])

            def fin_b():
                while state["q"] and state["q"][0][0] <= cid:
                    state["q"].pop(0)[1]()
                ne = min(nt_, 2)
                for e_ in range(ne):
                    mm(psL[:, 0:ncol], onesb[:, :], acch[:, e_, 0:ncol], e_ == 0, e_ == ne - 1, ["onesb", ("acch", e_)], [kL])
                act(rec[:, par, 0:ncol], psL[:, 0:ncol], AF.Ln, [kL], [("rec", par)])
                act(rec[:, par, 0:ncol], rec[:, par, 0:ncol], AF.Exp, [("rec", par)], [("rec", par)], scale=-1.0)
                tt("dve", OlatT[:, :, qcols], psO[:, 0:ncol].rearrange("p (a b) -> p a b", a=8),
                   rec[:, par, 0:ncol].rearrange("p (a b) -> p a b", a=8), ALU.mult, [kO, ("rec", par)],
                   [("QO", h, jl) for h in range(8)])

            return [mk(i) for i in range(nt_)] + [fin_a], fin_b

        def mla_steps_for(chunks):
            flat = []
            pending = []
            for (nq, jl, tiles) in chunks:
                steps, fin_b = mla_chunk_steps(nq, jl, tiles)
                for k, stp in enumerate(steps):
                    if pending and k == min(2, len(steps) - 1):
                        flat.append(pending.pop(0))
                    flat.append(stp)
                pending.append(fin_b)

            def flush():
                while mla_pend["q"]:
                    mla_pend["q"].pop(0)[1]()
            flat.append(flush)
            flat.extend(pending)
            return flat

        def merge_a(nt, njl):
            for p_ in range(4):
                bank, bk = nextbank((0, 1))
                for hh in range(2):
                    h = 2 * p_ + hh
                    mm(bank[hh * 64:(hh + 1) * 64, 0:nt], Wuv[:, h * 64:(h + 1) * 64], OlatT[:, h, 0:nt], True, True,
                       ["Wuv"] + [("QO", h, jl) for jl in range(njl)], [bk])
                tt("dve", yT[:, p_, 0:nt], bank[:, 0:nt], gates[:, p_, 0:nt], ALU.mult, [bk, ("gate", p_)], XYfc(p_, max(1, nt // 128)))

        ptb_rr = [0]

        def band_steps_for(nt, heads_tiles):
            items = []
            for (h, tiles) in heads_tiles:
                for i, tl in enumerate(tiles):
                    items.append({"h": h, "t": tl, "first": i == 0, "last": i == len(tiles) - 1})
            N = len(items)

            def stage1(it):
                h = it["h"]
                p_ = h // 2
                pbase = 64 * (h % 2)
                kb_ap, vx_ap, M, q0, q1, qr0, keys = it["t"]
                n = q1 - q0
                mm(pT32[0:M, 0:n], kb_ap, qbT[:, p_, h % 2, q0:q1], True, True, [("qbT", p_)] + list(keys), [PTK])

            def stage2(it):
                h = it["h"]
                kb_ap, vx_ap, M, q0, q1, qr0, keys = it["t"]
                n = q1 - q0
                pi = ptb_rr[0] % 2
                ptb_rr[0] += 1
                it["pi"] = pi
                psS, kS = pT32, PTK
                lo, hi = qr0, qr0 + n
                if lo < 256:
                    a1 = min(hi, 256)
                    c1_ = a1 - lo
                    act(PTb[0:M, pi, 0:c1_], psS[0:M, 0:c1_], AF.Exp, [kS], [("PTb", pi, 0)], scale=B_SCALE)
                    tt("pool", PTb[0:M, pi, 0:c1_], PTb[0:M, pi, 0:c1_], BT[0:M, h, lo:a1], ALU.mult, [("PTb", pi, 0), "BT"], [("PTb", pi, 0)])
                b0, b1 = max(lo, 256), min(hi, 576)
                if b1 > b0:
                    act(PTb[0:M, pi, b0 - lo:b1 - lo], psS[0:M, b0 - lo:b1 - lo], AF.Exp, [kS, "cb"], [("PTb", pi, 1)],
                        scale=B_SCALE, bias=cb[0:M, h:h + 1])
                e0 = max(lo, 576)
                if hi > e0:
                    act(PTb[0:M, pi, e0 - lo:hi - lo], psS[0:M, e0 - lo:hi - lo], AF.Exp, [kS, "cbm"], [("PTb", pi, 2)],
                        scale=B_SCALE, bias=cbm[0:M, h:h + 1])

            def stage3(it):
                h = it["h"]
                kb_ap, vx_ap, M, q0, q1, qr0, keys = it["t"]
                n = q1 - q0
                pi = it["pi"]
                psOb, kOb = pb[5 + h % 2], PSK[5 + h % 2]
                PK = [("PTb", pi, 0), ("PTb", pi, 1), ("PTb", pi, 2)]
                mm(psOb[:, q0:q1], vx_ap, PTb[0:M, pi, 0:n], it["first"], it["last"], PK + list(keys), [kOb])

            def finalize(h):
                p_ = h // 2
                pbase = 64 * (h % 2)
                s = h % 2
                psOb, kOb = pb[5 + s], PSK[5 + s]
                act(tmpA[64:128, s, 0:nt], psOb[64:128, 0:nt], AF.Ln, [kOb], [("tmpA", s)])
                act(tmpA[64:128, s, 0:nt], tmpA[64:128, s, 0:nt], AF.Exp, [("tmpA", s)], [("tmpA", s)], scale=-1.0)
                tt("dve", tmpB[pbase:pbase + 64, s, 0:nt], psOb[0:64, 0:nt], tmpA[64:128, s, 0:nt], ALU.mult, [kOb, ("tmpA", s)], [("tmpB", s)])
                tt("pool", yT[pbase:pbase + 64, 4 + p_, 0:nt], tmpB[pbase:pbase + 64, s, 0:nt], gates[pbase:pbase + 64, 4 + p_, 0:nt],
                   ALU.mult, [("tmpB", s), ("gate", 4 + p_)], XYfc(4 + p_, max(1, nt // 128)))

            out = []
            for k in range(N + 2):
                def pre(k=k):
                    if 0 <= k - 1 < N:
                        stage2(items[k - 1])
                    if 0 <= k - 2 < N:
                        stage3(items[k - 2])

                def post(k=k):
                    if 0 <= k - 2 < N and items[k - 2]["last"]:
                        finalize(items[k - 2]["h"])
                    if k < N:
                        stage1(items[k])
                out.append((pre, post))
            return out

        def run_interleaved(mla_flat, band_flat):
            nm, nb = len(mla_flat), len(band_flat)
            mi = 0
            for k in range(nb):
                band_flat[k][0]()
                target = ((k + 1) * nm) // nb
                while mi < target:
                    mla_flat[mi]()
                    mi += 1
                band_flat[k][1]()
            while mi < nm:
                mla_flat[mi]()
                mi += 1

        def d_buffers(tsz):
            return [(stage[0:tsz, 0, 0:D], [("stage", 0)], ("wst", 0)),
                    (stage[0:tsz, 1, 0:D], [("stage", 1)], ("wst", 1)),
                    (sbb[0:tsz, :, :].rearrange("p a b -> p (a b)"), [("sbb", 0), ("sbb", 1)], "xd2"),
                    (rec[0:tsz, :, :].rearrange("p a b -> p (a b)"), [("rec", 0), ("rec", 1)], "xd3")]

        def d_prefetch(x_src, t0, tsz, which):
            bufs = d_buffers(tsz)
            out = {}
            for i in which:
                buf, keys, slot = bufs[i]
                dma(buf, x_src[t0 + i * tsz: t0 + (i + 1) * tsz, :], [], keys, slot)
                out[i] = True
            return out

        def d_tile_a(tti, tsz):
            bufs = d_buffers(tsz)
            tok = slice(tti * tsz, (tti + 1) * tsz)
            xt, xkeys, slot = bufs[tti]
            banks = [nextbank((0, 1, 2, 3)), nextbank((0, 1, 2, 3))]
            for half in range(2):
                bank, bk = banks[half]
                for fc in range(8):
                    mm(bank[0:tsz, :], yT[:, fc, tok], Wout[:, fc, half * 512:(half + 1) * 512], fc == 0, fc == 7, XYt(tti) + WOK, [bk])
            for half in range(2):
                bank, bk = banks[half]
                tt("dve", xt[:, half * 512:(half + 1) * 512], bank[0:tsz, :], xt[:, half * 512:(half + 1) * 512], ALU.add, [bk] + xkeys, xkeys)

        def d_tile_b(tti, tsz, y_out):
            bufs = d_buffers(tsz)
            xt, xkeys, slot = bufs[tti]
            c = smcol()
            ss = small[0:tsz, c:c + 1]
            jk = PT[0:tsz, 0:2, :].rearrange("p a b -> p (a b)")
            act(jk, xt, AF.Square, xkeys, [("PT", 0), ("PT", 1), ("sm", c)], accum_out=ss)
            r_ap, rk = rstd_from_ss(ss, ("sm", c), D, tsz)
            stt(xt, xt, r_ap, gfin[0:tsz, :], ALU.mult, ALU.mult, xkeys + [rk, "gfin"], xkeys)
            dma(y_out(tti), xt, xkeys, [], ("o_y", tti))

        def d_tile(tti, tsz, y_out):
            d_tile_a(tti, tsz)
            d_tile_b(tti, tsz, y_out)

        def phase_d(x_src, t0, nt, tsz, y_out, pre=None):
            ntile = nt // tsz
            bufs = d_buffers(tsz)
            pre = pre or {}
            for tti in range(ntile):
                if tti not in pre:
                    buf, keys, slot = bufs[tti]
                    dma(buf, x_src[t0 + tti * tsz: t0 + (tti + 1) * tsz, :], [], keys, slot)
            for tti in range(ntile):
                d_tile(tti, tsz, y_out)

        if SAMPLE:
            cst3 = cst.rearrange("p (t c) -> p t c", c=128)
            for g in range(4):
                dma(cst3, cckv_d[g * 1024:(g + 1) * 1024, :].rearrange("(t p) c -> p t c", p=128), [], TA2, "cin")
                cp("pool" if g % 2 else "dve", V[:, g * 8:(g + 1) * 8, :], cst3, TA2, [("V", g * 8 + i) for i in range(8)])
                for i in range(8):
                    t = g * 8 + i
                    tr(pT[:, i * 128:(i + 1) * 128], V[:, t, :], identb[:], [("V", t), "identb"], [PTK])
                cp("act", KTc[:, g * 1024:(g + 1) * 1024], pT[:, :], [PTK], [("KTc", g * 8 + i) for i in range(8)])
            for g in range(4):
                cv = cst[:, 0:256].rearrange("p (t r) -> p t r", r=32)
                cvb_ = cstb[:, 0:256].rearrange("p (t r) -> p t r", r=32)
                dma(cv, ckpe_d[g * 1024:(g + 1) * 1024, :].rearrange("(t p) r -> p t r", p=128), [], TA2, "cin")
                cp("dve", cvb_, cv, TA2, SQ2)
                for i in range(8):
                    tr(pT[0:32, i * 128:(i + 1) * 128], cvb_[:, i, :], identb[:], SQ2 + ["identb"], [PTK])
                cp("act", KTp[0:32, g * 1024:(g + 1) * 1024], pT[0:32, :], [PTK], [("KTp", g * 8 + i) for i in range(8)])
            for g in range(2):
                cv = cst.rearrange("p (t n) -> p t n", n=512)
                cvb_ = cstb.rearrange("p (t n) -> p t n", n=512)
                dma(cv, ckb_d[g * 256:(g + 1) * 256, :].rearrange("(t p) n -> p t n", p=128), [], TA2, "cin")
                cp("dve", cvb_, cv, TA2, SQ2)
                for i in range(2):
                    tmi = g * 2 + i
                    for p_ in range(4):
                        tr(pT[:, p_ * 128:(p_ + 1) * 128], cvb_[:, i, p_ * 128:(p_ + 1) * 128], identb[:], SQ2 + ["identb"], [PTK])
                    cp("act", kbT[:, :, 1, tmi * 128:(tmi + 1) * 128], pT[:, 0:512].rearrange("p (a b) -> p a b", a=4), [PTK],
                       [("kbT", 1, p_, tmi) for p_ in range(4)])
            for g in range(2):
                cv = cst.rearrange("p (t n) -> p t n", n=512)
                dma(cv, cvb_d[g * 256:(g + 1) * 256, :].rearrange("(t p) n -> p t n", p=128), [], TA2, "cin")
                for i in range(2):
                    tmi = g * 2 + i
                    cp("dve", vbx[:, 4 + tmi, :, 0:64], cv[:, i, :].rearrange("p (a b) -> p a b", a=8), TA2, [("vbx", 4 + tmi)])

            phase_a(xs_d, 0, TS, TS, SEQ,
                    KTc_dst=lambda tti: KTc_s[:, 0:TS], KTp_dst=KTp_s[0:32, 0:TS], V_dst=lambda tti: V_s[0:TS, :],
                    kv_keys=(lambda tti: "KTc_s", ["KTp_s"], lambda tti: "V_s"),
                    kb_dst=lambda p_: kbT_s[:, p_, 0:TS], kb_keys=lambda p_: [("kbT_s", p_)],
                    vbx_dst=lambda tti: vbx_s[0:TS, :, :], vbx_key=lambda tti: "vbx_s",
                    ckv_out=lambda tti: ckvs_o, kpe_out=kpes_o.rearrange("(t p) r -> p t r", p=TS),
                    kb_out=lambda tti: kbs_o, vb_out=lambda tti: vbs_o)
            tiles = []
            for kt in range(32):
                tiles.append((KTc[:, kt * 128:(kt + 1) * 128], KTp[:, kt * 128:(kt + 1) * 128], V[:, kt, :], 128,
                              [("KTc", kt), ("KTp", kt), ("V", kt)]))
            tiles.append((KTc_s[:, 0:TS], KTp_s[:, 0:TS], V_s[0:TS, :], TS, ["KTc_s", "KTp_s", "V_s"]))
            mla_flat = mla_steps_for([(TS, 0, tiles)])
            heads_tiles = []
            for h in range(8):
                p_ = h // 2
                pbase = 64 * (h % 2)
                tiles = []
                for tmi in range(4):
                    tiles.append((kbT[:, p_, 1, tmi * 128:(tmi + 1) * 128], vbx[:, 4 + tmi, h, :], 128, 0, TS,
                                  512 - 128 * tmi, [("kbT", 1, p_, tmi), ("vbx", 4 + tmi)]))
                tiles.append((kbT_s[:, p_, 0:TS], vbx_s[0:TS, h, :], TS, 0, TS, 0, [("kbT_s", p_), "vbx_s"]))
                heads_tiles.append((h, tiles))
            run_interleaved(mla_flat, band_steps_for(TS, heads_tiles))
            merge_a(TS, 1)
            phase_d(xs_d, 0, TS, TS, lambda tti: ys_o)

        for b in range(NBLK if DBG_STAGE >= 1 else 0):
            t0 = 512 * b
            slot = b % 2
            last = (b == SEQ // 512 - 1)
            try:
              phase_a(x_d, t0, 512, 128, t0,
                    KTc_dst=lambda tti, t0=t0: KTc[:, t0 + tti * 128: t0 + (tti + 1) * 128],
                    KTp_dst=KTp[0:32, t0:t0 + 512],
                    V_dst=lambda tti, b=b: V[:, 4 * b + tti, :],
                    kv_keys=(lambda tti, b=b: ("KTc", 4 * b + tti), [("KTp", 4 * b + i) for i in range(4)], lambda tti, b=b: ("V", 4 * b + tti)),
                    kb_dst=lambda p_, slot=slot: kbT[:, p_, slot, :],
                    kb_keys=lambda p_, slot=slot: [("kbT", slot, p_, tmi) for tmi in range(4)],
                    vbx_dst=lambda tti, slot=slot: vbx[:, 4 * slot + tti, :, :],
                    vbx_key=lambda tti, slot=slot: ("vbx", 4 * slot + tti),
                    ckv_out=lambda tti, t0=t0: ckv_o[t0 + tti * 128: t0 + (tti + 1) * 128, :],
                    kpe_out=kpe_o[t0:t0 + 512, :].rearrange("(t p) r -> p t r", p=128),
                    kb_out=(lambda tti: kb_o[tti * 128:(tti + 1) * 128, :]) if last else None,
                    vb_out=(lambda tti: vb_o[tti * 128:(tti + 1) * 128, :]) if last else None,
                    front_done=(b > 0 and DBG_STAGE >= 4))
            except _Stop:
                pass
            pre = d_prefetch(x_d, t0, 128, [0, 1, 2]) if DBG_STAGE >= 4 else None
            chunks = []
            for jl in range(8 if DBG_STAGE >= 2 else 0):
                j = 8 * b + jl
                tiles = []
                for kt in range(j // 2 + 1):
                    half = (kt == j // 2 and j % 2 == 0)
                    tiles.append((KTc[:, kt * 128:(kt + 1) * 128], KTp[:, kt * 128:(kt + 1) * 128], V[:, kt, :], 128,
                                  [("KTc", kt), ("KTp", kt), ("V", kt)], half))
                chunks.append((64, jl, tiles))
            mla_flat = mla_steps_for(chunks)
            ms = [m for m in range(4 * b - 4, 4 * b + 4) if m >= 0]
            first = 4 * b - 1 if b > 0 else 0
            ms = [first] + [m for m in ms if m != first]
            heads_tiles = []
            for h in range(8 if DBG_STAGE >= 3 else 0):
                p_ = h // 2
                pbase = 64 * (h % 2)
                tiles = []
                for m in ms:
                    bm, tmi = m // 4, m % 4
                    sl = bm % 2
                    c_lo = max(2 * m, 8 * b)
                    c_hi = min(2 * m + 9, 8 * b + 7)
                    q0 = (c_lo - 8 * b) * 64
                    q1 = (c_hi - 8 * b + 1) * 64
                    qr0 = (c_lo - 2 * m) * 64
                    tiles.append((kbT[:, p_, sl, tmi * 128:(tmi + 1) * 128], vbx[:, 4 * sl + tmi, h, :], 128, q0, q1, qr0,
                                  [("kbT", sl, p_, tmi), ("vbx", 4 * sl + tmi)]))
                heads_tiles.append((h, tiles))
            run_interleaved(mla_flat, band_steps_for(512, heads_tiles))
            if DBG_STAGE >= 2:
                merge_a(512, 8)
            if DBG_STAGE >= 4:
                nxt = (b + 1 < NBLK)
                a_bufs = [(tmpA[:, :, :].rearrange("p a b -> p (a b)"), TA2, "xa0"), (tmpB[:, :, :].rearrange("p a b -> p (a b)"), TB2, "xa1"),
                          (stage[:, 0, 0:D], [("stage", 0)], ("wst", 0)), (stage[:, 1, 0:D], [("stage", 1)], ("wst", 1))]
                t1 = t0 + 512
                if 3 not in pre:
                    buf, keys, slot = d_buffers(128)[3]
                    dma(buf, x_d[t0 + 384: t0 + 512, :], [], keys, slot)
                if nxt:
                    for i_ in range(2):
                        buf, keys, slot = a_bufs[i_]
                        dma(buf, x_d[t1 + i_ * 128: t1 + (i_ + 1) * 128, :], [], keys, slot)
                yo = lambda tti, t0=t0: y_o[t0 + tti * 128: t0 + (tti + 1) * 128, :]
                for tti in range(5):
                    if tti < 4:
                        d_tile_a(tti, 128)
                    if nxt and tti >= 1:
                        buf, keys, slot = a_bufs[tti - 1]
                        a_front_tile(buf, keys, 128, tti - 1)
                    if tti < 4:
                        d_tile_b(tti, 128, yo)
                        if nxt and tti < 2:
                            buf, keys, slot = a_bufs[2 + tti]
                            dma(buf, x_d[t1 + (2 + tti) * 128: t1 + (3 + tti) * 128, :], [], keys, slot)

        n = P.emit(nc)
    return nc, n


def _prep_shared(w_in, g_mix, g_cq, w_uq, g_ckv, w_uk, w_uv, rel_bias, w_out, g_final):
    f = np.float32
    wi = np.asarray(w_in[0], f)
    perm = np.r_[16:32, 0:16]
    kr = wi[:, 384:416]
    cols = [wi[:, 0:256], wi[:, 416:928], wi[:, 2464:2976], wi[:, 928:1440], wi[:, 1440:1952], kr, kr[:, perm],
            wi[:, 256:384], wi[:, 1952:2464]]
    w_perm = np.ascontiguousarray(np.concatenate(cols, axis=1))
    assert w_perm.shape == (D, NCOL)
    wuq = np.asarray(w_uq[0], f)
    uqnT = np.ascontiguousarray(wuq[:, :, :64].transpose(2, 1, 0)).reshape(64, 8 * 256)
    ukT = np.ascontiguousarray(np.asarray(w_uk[0], f).transpose(2, 1, 0)).reshape(64, 8 * 128)
    pe = wuq[:, :, 64:96]
    wpe = np.ascontiguousarray(np.concatenate([pe, pe[:, :, perm]], axis=2)).reshape(256, 512)
    half = 16
    inv = (10000.0 ** (-np.arange(half, dtype=np.float64) / half))
    ropec = np.zeros((64, 4), np.float64)
    for r in range(32):
        ropec[r, 0] = inv[r % 16] / (2 * np.pi)
        ropec[r, 1] = 0.25
        ropec[r, 2] = 2 * np.pi
        ropec[32 + r, 0] = inv[r % 16] / (2 * np.pi)
        ropec[32 + r, 1] = 0.0
        ropec[32 + r, 2] = (-2 * np.pi) if r < 16 else (2 * np.pi)
    return {
        "w_perm": w_perm,
        "w_out": np.ascontiguousarray(np.asarray(w_out[0], f)),
        "wuv": np.ascontiguousarray(np.asarray(w_uv[0], f).reshape(128, 512)),
        "uqnT": uqnT, "ukT": ukT, "wpe": wpe,
        "gmix": np.ascontiguousarray(np.asarray(g_mix[0], f).reshape(8, 128).T),
        "gcq": np.ascontiguousarray(np.asarray(g_cq[0], f).reshape(2, 128).T),
        "gckv_bc": np.ascontiguousarray(np.broadcast_to(np.asarray(g_ckv[0], f)[None, :], (128, 128))),
        "gfin_bc": np.ascontiguousarray(np.broadcast_to(np.asarray(g_final, f)[None, :], (128, D))),
        "ropec": ropec.astype(f),
        "rb": np.ascontiguousarray(np.asarray(rel_bias[0], f)),
        "ident": np.eye(128, dtype=f),
    }


_NC_CACHE = {}


def kernel(x_prompt, x_sample, cache_ckv, cache_kpe, cache_kb, cache_vb, w_in, g_mix, g_cq, w_uq, g_ckv, w_uk, w_uv,
           rel_bias, w_out, g_final, _nblk=8, _sample=True):
    f = np.float32
    shared = _prep_shared(w_in, g_mix, g_cq, w_uq, g_ckv, w_uk, w_uv, rel_bias, w_out, g_final)
    key = (_nblk, _sample)
    if key not in _NC_CACHE:
        _NC_CACHE[key] = build_nc(_nblk, _sample)[0]
    nc = _NC_CACHE[key]
    in_maps = []
    for i in range(8):
        m = dict(shared)
        m["x"] = np.ascontiguousarray(np.asarray(x_prompt[i], f))
        m["xs"] = np.ascontiguousarray(np.asarray(x_sample[i], f))
        m["cckv"] = np.ascontiguousarray(np.asarray(cache_ckv[0, i], f))
        m["ckpe"] = np.ascontiguousarray(np.asarray(cache_kpe[0, i], f))
        m["ckb"] = np.ascontiguousarray(np.asarray(cache_kb[0, i], f).reshape(512, 512))
        m["cvb"] = np.ascontiguousarray(np.asarray(cache_vb[0, i], f).reshape(512, 512))
        in_maps.append(m)
    res = run_bass_kernel_spmd(nc, in_maps, core_ids=list(range(8)))
    R = res.results
    st = lambda k: np.stack([np.asarray(R[i][k], f) for i in range(8)], axis=0)
    y = st("y")
    ys = st("ys")
    return (y, ys,
            st("ckv_o")[None], st("kpe_o")[None],
            st("kb_o").reshape(8, 512, 8, 64)[None], st("vb_o").reshape(8, 512, 8, 64)[None],
            st("ckvs_o")[None], st("kpes_o")[None],
            st("kbs_o").reshape(8, TS, 8, 64)[None], st("vbs_o").reshape(8, TS, 8, 64)[None])
```

```python
import contextlib
import numpy as np
import concourse.bass as bass
import concourse.mybir as mybir
from concourse.bass_utils import run_bass_kernel_spmd

F32 = mybir.dt.float32
BF16 = mybir.dt.bfloat16
I32 = mybir.dt.int32
AF = mybir.ActivationFunctionType
ALU = mybir.AluOpType

D = 1024
SEQ = 4096
TS = 32
PAST = 4096
EPS = 1e-6
MLA_SCALE = 96.0 ** -0.5
B_SCALE = 64.0 ** -0.5
NEGM = -30000.0
DBG_STAGE = 99
DBG_A = 99


class _Stop(Exception):
    pass


def _sec(k):
    if k > DBG_A:
        raise _Stop()
C_CQ, C_GA, C_GB, C_QB, C_KB, C_KR, C_CKV, C_VB, NCOL = 0, 256, 768, 1280, 1792, 2304, 2368, 2496, 3008


class Instr:
    __slots__ = ("eng", "fn", "deps", "signal", "sig_idx", "is_dma", "slot", "dma_val")

    def __init__(self, eng, fn):
        self.eng = eng
        self.fn = fn
        self.deps = []
        self.signal = False
        self.sig_idx = 0
        self.is_dma = False
        self.slot = None
        self.dma_val = 0


class Prog:
    ENGS = ("pe", "act", "dve", "pool", "sp")

    def __init__(self):
        self.q = {e: [] for e in self.ENGS}
        self.last_writer = {}
        self.readers = {}
        self.slot_count = {}
        self.slot_last = {}
        self.n = 0

    def _track(self, ins, reads, writes):
        deps = []
        for k in reads:
            w = self.last_writer.get(k)
            if w is not None:
                deps.append((w, "raw"))
            if isinstance(k, tuple) and k[0] == "ps":
                for r in self.readers.get(k, ()):
                    deps.append((r, "war"))
        for k in writes:
            w = self.last_writer.get(k)
            if w is not None:
                deps.append((w, "waw"))
            for r in self.readers.get(k, ()):
                deps.append((r, "war"))
        for k in writes:
            self.last_writer[k] = ins
            self.readers[k] = []
        for k in reads:
            if k not in writes:
                self.readers.setdefault(k, []).append(ins)
        seen = set()
        for d, kind in deps:
            if d is ins or id(d) in seen:
                continue
            seen.add(id(d))
            ins.deps.append((d, kind))

    def op(self, eng, fn, reads=(), writes=()):
        ins = Instr(eng, fn)
        self.n += 1
        self._track(ins, tuple(reads), tuple(writes))
        self.q[eng].append(ins)
        return ins

    def dma(self, fn, reads=(), writes=(), slot=None, eng="sp"):
        ins = Instr(eng, fn)
        self.n += 1
        ins.is_dma = True
        ins.slot = slot
        self.slot_count[slot] = self.slot_count.get(slot, 0) + 1
        ins.dma_val = 16 * self.slot_count[slot]
        self._track(ins, tuple(reads), tuple(writes))
        prev = self.slot_last.get(slot)
        if prev is not None and all(d is not prev for d, _ in ins.deps):
            ins.deps.append((prev, "raw"))
        self.slot_last[slot] = ins
        self.q[eng].append(ins)
        return ins

    def emit(self, nc):
        for e in self.ENGS:
            for ins in self.q[e]:
                for d, kind in ins.deps:
                    if d.is_dma:
                        continue
                    if d.eng == ins.eng and d.eng in ("pe", "sp"):
                        continue
                    d.signal = True
        for e in self.ENGS:
            c = 0
            for ins in self.q[e]:
                if ins.signal and not ins.is_dma:
                    c += 1
                    ins.sig_idx = c
        slots = sorted(self.slot_count.keys(), key=str)
        with contextlib.ExitStack() as st:
            esem = {e: st.enter_context(nc.semaphore("s_" + e)) for e in self.ENGS}
            ssem = {s: st.enter_context(nc.semaphore("d_%d" % i)) for i, s in enumerate(slots)}
            block = st.enter_context(nc.Block())
            prog = self

            def run(engname, eh):
                waited = {}
                for ins in prog.q[engname]:
                    for d, kind in ins.deps:
                        if d.is_dma:
                            sem, val, key = ssem[d.slot], d.dma_val, ("d", d.slot)
                        else:
                            if d.eng == engname and engname in ("pe", "sp"):
                                continue
                            sem, val, key = esem[d.eng], d.sig_idx, ("e", d.eng)
                        if waited.get(key, 0) >= val:
                            continue
                        waited[key] = val
                        eh.wait_ge(sem, val)
                    h = ins.fn(eh)
                    if ins.is_dma:
                        h.then_inc(ssem[ins.slot], 16)
                    elif ins.signal:
                        h.then_inc(esem[engname], 1)
                if engname == "sp":
                    for s in slots:
                        eh.wait_ge(ssem[s], 16 * prog.slot_count[s])

            @block.tensor
            def _(eh):
                run("pe", eh)

            @block.scalar
            def _(eh):
                run("act", eh)

            @block.vector
            def _(eh):
                run("dve", eh)

            @block.gpsimd
            def _(eh):
                run("pool", eh)

            @block.sync
            def _(eh):
                run("sp", eh)
        return self.n


def build_nc(NBLK=8, SAMPLE=True):
    nc = bass.Bass("TRN2", target_bir_lowering=False, dynamic_dma_scratch_size=1024)

    def din(name, shape):
        return nc.dram_tensor(name, list(shape), F32, kind="ExternalInput").ap()

    def dout(name, shape):
        return nc.dram_tensor(name, list(shape), F32, kind="ExternalOutput").ap()

    x_d = din("x", [SEQ, D])
    xs_d = din("xs", [TS, D])
    cckv_d = din("cckv", [PAST, 128])
    ckpe_d = din("ckpe", [PAST, 32])
    ckb_d = din("ckb", [512, 512])
    cvb_d = din("cvb", [512, 512])
    w_d = din("w_perm", [D, NCOL])
    wout_d = din("w_out", [D, D])
    wuv_d = din("wuv", [128, 512])
    uqn_d = din("uqnT", [64, 8 * 256])
    uk_d = din("ukT", [64, 8 * 128])
    wpe_d = din("wpe", [256, 512])
    gmix_d = din("gmix", [128, 8])
    gcq_d = din("gcq", [128, 2])
    gckv_d = din("gckv_bc", [128, 128])
    gfin_d = din("gfin_bc", [128, D])
    ropec_d = din("ropec", [64, 4])
    rb_d = din("rb", [8, 257])
    ident_d = din("ident", [128, 128])

    y_o = dout("y", [SEQ, D])
    ys_o = dout("ys", [TS, D])
    ckv_o = dout("ckv_o", [SEQ, 128])
    kpe_o = dout("kpe_o", [SEQ, 32])
    kb_o = dout("kb_o", [512, 512])
    vb_o = dout("vb_o", [512, 512])
    ckvs_o = dout("ckvs_o", [TS, 128])
    kpes_o = dout("kpes_o", [TS, 32])
    kbs_o = dout("kbs_o", [TS, 512])
    vbs_o = dout("vbs_o", [TS, 512])
    scr = nc.dram_tensor("scr", [8, 128, 384], F32, kind="Internal").ap()
    tabd = nc.dram_tensor("tabd", [64, SEQ + TS], F32, kind="Internal").ap()

    P = Prog()
    st = contextlib.ExitStack()
    with st:
        def sb(name, shape, dt):
            return st.enter_context(nc.sbuf_tensor("sb_" + name, list(shape), dt))

        def psum(name, shape, dt):
            return st.enter_context(nc.psum_tensor("ps_" + name, list(shape), dt))

        W = sb("W", [128, 8, NCOL], BF16)
        Wout = sb("Wout", [128, 8, D], BF16)
        Wlat = sb("Wlat", [128, 2, 8, 128], BF16)
        Wpe = sb("Wpe", [128, 2, 8, 64], BF16)
        Wuv = sb("Wuv", [128, 512], BF16)
        gmix = sb("gmix", [128, 8], F32)
        gcq = sb("gcq", [128, 2], F32)
        gckv = sb("gckv", [128, 128], F32)
        gfin = sb("gfin", [128, D], F32)
        ropec = sb("ropec", [64, 4], F32)
        identf = sb("identf", [32, 32], F32)
        identb = sb("identb", [128, 128], BF16)
        onesb = sb("onesb", [128, 128], BF16)
        tabblk = sb("tabblk", [64, 1, 512], F32)
        BT = sb("BT", [128, 8, 256], F32)
        cb = sb("cb", [128, 8], F32)
        cbm = sb("cbm", [128, 8], F32)
        KTc = sb("KTc", [128, SEQ], BF16)
        KTp = sb("KTp", [128, SEQ], BF16)
        V = sb("V", [128, 32, 128], BF16)
        kbT = sb("kbT", [128, 4, 2, 512], BF16)
        vbx = sb("vbx", [128, 8, 8, 128], BF16)
        stage = sb("stage", [128, 2, 1024], F32)
        xnb = sb("xnb", [128, D], BF16)
        xnT = sb("xnT", [128, 8, 512], BF16)
        cqT = sb("cqT", [128, 2, 512], BF16)
        sq = sb("sq", [128, 2, 512], BF16)
        rq = sb("rq", [128, 512], F32)
        QlatT = sb("QlatT", [128, 8, 512], BF16)
        QpeT = sb("QpeT", [128, 8, 512], BF16)
        OlatT = QlatT
        gates = sb("gates", [128, 8, 512], BF16)
        qbT = sb("qbT", [128, 4, 2, 512], BF16)
        yT = xnT
        tmpA = sb("tmpA", [128, 2, 512], F32)
        tmpB = sb("tmpB", [128, 2, 512], F32)
        PT = sb("PT", [128, 4, 512], BF16)
        PTb = sb("PTb", [128, 2, 512], BF16)
        sbb = sb("sbb", [128, 2, 512], F32)
        rec = sb("rec", [128, 2, 512], F32)
        small = sb("small", [128, 64], F32)
        acc = sb("acc", [128, 2, 512], F32)
        acch = sb("acch", [128, 2, 512], BF16)
        maskv = sb("maskv", [128, 1], F32)
        ckv32 = sb("ckv32", [128, 2, 128], F32)
        ckvb = sb("ckvb", [128, 128], BF16)
        kpst = sb("kpst", [128, 4, 32], F32)
        KTc_s = sb("KTc_s", [128, 32], BF16)
        KTp_s = sb("KTp_s", [128, 32], BF16)
        V_s = sb("V_s", [32, 128], BF16)
        vbx_s = sb("vbx_s", [32, 8, 128], BF16)
        kbT_s = sb("kbT_s", [128, 4, 32], BF16)

        pb = [psum("pb%d" % i, [128, 512], F32) for i in range(7)]
        pT = psum("pT", [128, 1024], BF16)
        PSK = [("ps", i) for i in range(7)]
        PTK = ("ps", 7)
        pT32 = pT.bitcast(F32)

        def mm(out, lhsT, rhs, start, stop, reads, writes):
            P.op("pe", lambda e: e.matmul(out, lhsT=lhsT, rhs=rhs, start=start, stop=stop), reads, writes)

        def tr(out, in_, ident, reads, writes):
            P.op("pe", lambda e: e.transpose(out=out, in_=in_, identity=ident), reads, writes)

        def act(out, in_, func, reads, writes, **kw):
            P.op("act", lambda e: e.activation(out=out, in_=in_, func=func, **kw), reads, writes)

        def tt(eng, out, in0, in1, op, reads, writes):
            P.op(eng, lambda e: e.tensor_tensor(out=out, in0=in0, in1=in1, op=op), reads, writes)

        def ts(eng, out, in0, s1, s2, op0, op1, reads, writes):
            if op1 is None:
                P.op(eng, lambda e: e.tensor_scalar(out=out, in0=in0, scalar1=s1, scalar2=None, op0=op0), reads, writes)
            else:
                P.op(eng, lambda e: e.tensor_scalar(out=out, in0=in0, scalar1=s1, scalar2=s2, op0=op0, op1=op1), reads, writes)

        def stt(out, in0, scalar, in1, op0, op1, reads, writes):
            P.op("dve", lambda e: e.scalar_tensor_tensor(out=out, in0=in0, scalar=scalar, in1=in1, op0=op0, op1=op1), reads, writes)

        def cp(eng, out, in_, reads, writes):
            if eng == "act":
                act(out, in_, AF.Copy, reads, writes)
            else:
                P.op(eng, lambda e: e.tensor_copy(out=out, in_=in_), reads, writes)

        def mset(eng, ap, val, writes):
            P.op(eng, lambda e: e.memset(ap, val), (), writes)

        def dma(out, in_, reads, writes, slot):
            P.dma(lambda e: e.dma_start(out=out, in_=in_), reads, writes, slot)

        sm_ctr = [0]

        def smcol():
            c = sm_ctr[0] % 64
            sm_ctr[0] += 1
            return c

        def rstd_from_ss(ss_ap, ss_key, dim, n):
            c1, c2 = smcol(), smcol()
            l_ap = small[0:n, c1:c1 + 1]
            r_ap = small[0:n, c2:c2 + 1]
            act(l_ap, ss_ap, AF.Ln, [ss_key], [("sm", c1)], scale=1.0 / dim, bias=EPS)
            act(r_ap, l_ap, AF.Exp, [("sm", c1)], [("sm", c2)], scale=-0.5)
            return r_ap, ("sm", c2)

        TA2 = [("tmpA", 0), ("tmpA", 1)]
        TB2 = [("tmpB", 0), ("tmpB", 1)]
        SB2 = [("sbb", 0), ("sbb", 1)]
        RC2 = [("rec", 0), ("rec", 1)]
        SQ2 = [("sq", 0), ("sq", 1)]
        sbbI = sbb.bitcast(I32)
        kpe32 = rec[0:32, 1, :]
        G = rec[0:8, :, :].rearrange("p a b -> p (a b)")
        cst = tmpA[:, :, :].rearrange("p a b -> p (a b)")
        cstb = sq[:, :, :].rearrange("p a b -> p (a b)")
        dma(gmix[:], gmix_d, [], ["gmix"], "c0")
        dma(gcq[:], gcq_d, [], ["gcq"], "c1")
        dma(ropec[:], ropec_d, [], ["ropec"], "c4")
        dma(G[:, 0:256], rb_d[:, 1:257], [], RC2, "c6")
        dma(stage[:, 0, 0:128], ident_d, [], [("stage", 0)], ("wst", 0))
        dma(gckv[:], gckv_d, [], ["gckv"], "c2")
        dma(gfin[:], gfin_d, [], ["gfin"], "c3")
        ts("dve", G[:, 256:384], G[:, 0:128], 0.0, G[:, 255:256], ALU.mult, ALU.add, RC2, RC2)

        def g2scr(e):
            g = G[:, 0:384]
            src = bass.AP(g.tensor, g.offset, [list(g.ap[0]), [0, 128], [1, 384]])
            return e.dma_start(out=scr, in_=src)
        P.dma(g2scr, RC2, ["scr"], "c7")
        cp("dve", identb[:], stage[:, 0, 0:128], [("stage", 0)], ["identb"])
        cp("dve", identf[:], stage[0:32, 0, 0:32], [("stage", 0)], ["identf"])

        mset("pool", onesb[:], 1.0, ["onesb"])
        mset("pool", KTp[:, :], 0.0, [("KTp", t) for t in range(32)])
        mset("pool", QpeT[:, :, :], 0.0, [("Qp", h) for h in range(8)])
        mset("pool", KTp_s[:, :], 0.0, ["KTp_s"])
        mset("pool", qbT[:, :, :, :], 0.0, [("qbT", p_) for p_ in range(4)])
        mset("pool", vbx[:, :, :, :], 1.0, [("vbx", t) for t in range(8)])
        mset("pool", vbx_s[:, :, :], 1.0, ["vbx_s"])
        mset("pool", maskv[:, :], 0.0, ["maskv"])
        mset("pool", maskv[64:128, :], NEGM, ["maskv"])

        NT = SEQ + TS

        def table_chunk(c0):
            n = min(1024, NT - c0)
            tA = tmpA[0:64, :, :].rearrange("p a b -> p (a b)")[:, 0:n]
            tB = tmpB[0:64, :, :].rearrange("p a b -> p (a b)")[:, 0:n]
            tI = sbbI[0:64, :, :].rearrange("p a b -> p (a b)")[:, 0:n]
            P.op("pool", lambda e, tA=tA, c0=c0, n=n: e.iota(tA, pattern=[[1, n]], base=c0, channel_multiplier=0,
                                                             allow_small_or_imprecise_dtypes=True), (), TA2)
            ts("dve", tB, tA, ropec[:, 0:1], ropec[:, 1:2], ALU.mult, ALU.add, TA2 + ["ropec"], TB2)
            cp("dve", tI, tB, TB2, SB2)
            cp("dve", tA, tI, SB2, TA2)
            tt("dve", tB, tB, tA, ALU.subtract, TA2 + TB2, TB2)
            act(tA, tB, AF.Sin, TB2 + ["ropec"], TA2, scale=ropec[:, 2:3])
            P.dma(lambda e, c0=c0, n=n, tA=tA: e.dma_start(out=tabd[:, c0:c0 + n], in_=tA), TA2, ["tabd"], "c9", eng="act")

        gatesF = gates.bitcast(F32)[:, :, :].rearrange("p a b -> p (a b)")
        xyF = xnT.bitcast(F32)[:, :, :].rearrange("p a b -> p (a b)")
        wslots = [(stage[:, 0, :], [("stage", 0)], ("wst", 0)), (stage[:, 1, :], [("stage", 1)], ("wst", 1)),
                  (gatesF[:, 0:1024], [("gate", g_) for g_ in range(0, 4)], "wst2"),
                  (gatesF[:, 1024:2048], [("gate", g_) for g_ in range(4, 8)], "wst3"),
                  (xyF[:, 0:1024], [("XY", g_, t_) for g_ in range(0, 4) for t_ in range(4)], "wst4"),
                  (xyF[:, 1024:2048], [("XY", g_, t_) for g_ in range(4, 8) for t_ in range(4)], "wst5")]
        pieces = [(0, 1024, "act"), (1024, 2048, "dve"), (2048, NCOL, "act")]
        wi = 0
        for dc in range(8):
            for pi_, (c0, c1, eng) in enumerate(pieces):
                buf, bkeys, bslot = wslots[wi % 6]
                wi += 1
                dma(buf[:, 0:c1 - c0], w_d[dc * 128:(dc + 1) * 128, c0:c1], [], bkeys, bslot)
                if eng == "act":
                    act(W[:, dc, c0:c1], buf[:, 0:c1 - c0], AF.Copy, bkeys + ["gmix"], [("W", dc, pi_)], scale=gmix[:, dc:dc + 1])
                else:
                    ts("dve", W[:, dc, c0:c1], buf[:, 0:c1 - c0], gmix[:, dc:dc + 1], None, ALU.mult, None,
                       bkeys + ["gmix"], [("W", dc, pi_)])
            if dc * 1024 < NT:
                table_chunk(dc * 1024)
        for dc in range(8):
            buf, bkeys, bslot = wslots[wi % 6]
            wi += 1
            dma(buf[:, 0:D], wout_d[dc * 128:(dc + 1) * 128, :], [], bkeys, bslot)
            cp("pool" if dc % 2 else "dve", Wout[:, dc, :], buf[:, 0:D], bkeys, [("Wout", dc)])
        WK = [("W", dc, i) for dc in range(8) for i in range(3)]
        WOK = [("Wout", dc) for dc in range(8)]

        def scr2bt(e):
            src = bass.AP(scr.tensor, scr.offset + 127, [[383, 128], [128 * 384, 8], [1, 256]])
            return e.dma_start(out=BT[:], in_=src)
        P.dma(scr2bt, ["scr"], ["BT"], "c8")
        cp("dve", cb[:, :], BT[:, :, 255], ["BT"], ["cb"])
        cp("dve", cbm[:, :], BT[:, :, 255], ["BT"], ["cbm"])
        mset("pool", cbm[0:64, :], NEGM, ["cbm"])
        mset("pool", BT[64:128, :, 0:64], NEGM, ["BT"])
        act(BT[:, :, :].rearrange("p a b -> p (a b)"), BT[:, :, :].rearrange("p a b -> p (a b)"), AF.Exp, ["BT", "cb", "cbm"], ["BT"])

        dma(stage[:, 0, 0:512], wuv_d, [], [("stage", 0)], ("wst", 0))
        cp("dve", Wuv[:], stage[:, 0, 0:512], [("stage", 0)], ["Wuv"])
        dma(stage[:, 1, 0:1024].rearrange("p (a b) -> p a b", a=2), wpe_d.rearrange("(a p) n -> p a n", p=128), [], [("stage", 1)], ("wst", 1))
        for rc in range(2):
            ts("dve", Wpe[:, rc, :, :].rearrange("p a b -> p (a b)"), stage[:, 1, rc * 512:(rc + 1) * 512], gcq[:, rc:rc + 1], None,
               ALU.mult, None, [("stage", 1), "gcq"], ["Wpe"])
        CQ2 = [("cqT", 0), ("cqT", 1)]
        ukb = sq[0:64, :, :].rearrange("p a b -> p (a b)")
        uqb = cqT[0:64, :, :].rearrange("p a b -> p (a b)")
        dma(stage[0:64, 1, 0:1024], uk_d, [], [("stage", 1)], ("wst", 1))
        cp("dve", ukb, stage[0:64, 1, 0:1024], [("stage", 1)], SQ2)
        for hg in range(2):
            dma(stage[0:64, 0, 0:1024], uqn_d[:, hg * 1024:(hg + 1) * 1024], [], [("stage", 0)], ("wst", 0))
            cp("dve", uqb, stage[0:64, 0, 0:1024], [("stage", 0)], CQ2)
            for rc in range(2):
                bank, bk = pb[rc], PSK[rc]
                for hl in range(4):
                    h = hg * 4 + hl
                    mm(bank[:, hl * 128:(hl + 1) * 128], uqb[:, hl * 256 + rc * 128: hl * 256 + (rc + 1) * 128],
                       ukb[:, h * 128:(h + 1) * 128], True, True, SQ2 + CQ2, [bk])
                ts("dve", Wlat[:, rc, hg * 4:(hg + 1) * 4, :].rearrange("p a b -> p (a b)"), bank[:, :], gcq[:, rc:rc + 1], None,
                   ALU.mult, None, [bk, "gcq"], ["Wlat"])

        bank_rr = [0]

        def nextbank(choices=(0, 1, 6, 2, 3, 4, 5)):
            i = choices[bank_rr[0] % len(choices)]
            bank_rr[0] += 1
            return pb[i], PSK[i]

        xslot = [0]

        def load_x_tile(src_ap, tsz):
            s = xslot[0] % 2
            xslot[0] += 1
            xt = stage[0:tsz, s, 0:D]
            dma(xt, src_ap, [], [("stage", s)], ("wst", s))
            return xt, ("stage", s)

        def XYt(t_):
            return [("XY", fc, t_) for fc in range(8)]

        def XYall(ntile_):
            return [("XY", fc, t_) for fc in range(8) for t_ in range(ntile_)]

        def XYfc(fc, ntile_):
            return [("XY", fc, t_) for t_ in range(ntile_)]
        tab_rr = [0]

        def a_front_tile(xt, xkeys, tsz, tti):
            c = smcol()
            ss = small[0:tsz, c:c + 1]
            jk = PT[0:tsz, 2:4, :].rearrange("p a b -> p (a b)")
            act(jk, xt, AF.Square, xkeys, [("PT", 2), ("PT", 3), ("sm", c)], accum_out=ss)
            r_ap, rk = rstd_from_ss(ss, ("sm", c), D, tsz)
            act(xnb[0:tsz, 0:512], xt[:, 0:512], AF.Copy, xkeys + [rk], [("xnb", 0)], scale=r_ap)
            ts("dve", xnb[0:tsz, 512:1024], xt[:, 512:1024], r_ap, None, ALU.mult, None, xkeys + [rk], [("xnb", 1)])
            for dc in range(8):
                tr(pT[:, dc * tsz:(dc + 1) * tsz], xnb[0:tsz, dc * 128:(dc + 1) * 128], identb[0:tsz, 0:tsz],
                   [("xnb", dc // 4), "identb"], [PTK])
            cp("dve", xnT[:, :, tti * tsz:(tti + 1) * tsz], pT[:, 0:8 * tsz].rearrange("p (a b) -> p a b", a=8),
               [PTK], XYt(tti))

        def phase_a(x_src, t0, nt, tsz, tab0, KTc_dst, KTp_dst, V_dst, kv_keys, kb_dst, kb_keys, vbx_dst, vbx_key,
                    ckv_out, kpe_out, kb_out, vb_out, front_done=False):
            ntile = nt // tsz
            tsl = 0
            TKEY = ("tab", tsl)
            dma(tabblk[:, tsl, 0:nt], tabd[:, tab0:tab0 + nt], ["tabd"], [TKEY], ("tabld", tsl))
            cosT = tabblk[0:32, tsl, 0:nt]
            sinT = tabblk[32:64, tsl, 0:nt]
            if not front_done:
                for tti in range(ntile):
                    xt, xk = load_x_tile(x_src[t0 + tti * tsz: t0 + (tti + 1) * tsz, :], tsz)
                    a_front_tile(xt, [xk], tsz, tti)
            XK = XYall(ntile)

            def fm_group(col0, m):
                bank, bk = nextbank()
                for dc in range(8):
                    mm(bank[0:m, 0:nt], W[:, dc, col0:col0 + m], xnT[:, dc, 0:nt], dc == 0, dc == 7, WK + XK, [bk])
                return bank, bk

            _sec(2)
            for rc in range(2):
                bank, bk = fm_group(C_CQ + rc * 128, 128)
                cp("dve", cqT[:, rc, 0:nt], bank[:, 0:nt], [bk], [("cqT", rc)])
                act(sq[:, rc, 0:nt], bank[:, 0:nt], AF.Square, [bk], [("sq", rc)])
            bank, bk = nextbank()
            for rc in range(2):
                mm(bank[:, 0:nt], onesb[:, :], sq[:, rc, 0:nt], rc == 0, rc == 1, ["onesb", ("sq", rc)], [bk])
            act(tmpA[:, 0, 0:nt], bank[:, 0:nt], AF.Ln, [bk], [("tmpA", 0)], scale=1.0 / 256, bias=EPS)
            act(rq[:, 0:nt], tmpA[:, 0, 0:nt], AF.Exp, [("tmpA", 0)], ["rq"], scale=-0.5)
            CQK = [("cqT", 0), ("cqT", 1)]
            _sec(3)
            for h in range(8):
                bank, bk = nextbank()
                for rc in range(2):
                    mm(bank[:, 0:nt], Wlat[:, rc, h, :], cqT[:, rc, 0:nt], rc == 0, rc == 1, ["Wlat"] + CQK, [bk])
                tt("dve", QlatT[:, h, 0:nt], bank[:, 0:nt], rq[:, 0:nt], ALU.mult, [bk, "rq"], [("QO", h, jl) for jl in range(8)])
            _sec(4)
            cosq = tmpB[0:32, 0, 0:nt]
            sinq = tmpB[32:64, 0, 0:nt]
            tt("dve", tmpB[0:64, 0, 0:nt], tabblk[0:64, tsl, 0:nt], rq[0:64, 0:nt], ALU.mult, [TKEY, "rq"], [("tmpB", 0)])
            for hp in range(4):
                bank, bk = nextbank()
                for rc in range(2):
                    mm(bank[:, 0:nt], Wpe[:, rc, 2 * hp:2 * hp + 2, :].rearrange("p a b -> p (a b)"), cqT[:, rc, 0:nt], rc == 0, rc == 1,
                       ["Wpe"] + CQK, [bk])
                for hh in range(2):
                    h = 2 * hp + hh
                    s = hh
                    r0 = 64 * hh
                    tt("dve", sbb[0:32, s, 0:nt], bank[r0:r0 + 32, 0:nt], cosq, ALU.mult, [bk, ("tmpB", 0)], [("sbb", s)])
                    tt("dve", rec[0:32, s, 0:nt], bank[r0 + 32:r0 + 64, 0:nt], sinq, ALU.mult, [bk, ("tmpB", 0)], [("rec", s)])
                    tt("dve", QpeT[0:32, h, 0:nt], sbb[0:32, s, 0:nt], rec[0:32, s, 0:nt], ALU.add, [("sbb", s), ("rec", s)], [("Qp", h)])
            _sec(5)
            for gi in range(8):
                col0 = (C_GA if gi < 4 else C_GB) + (gi % 4) * 128
                bank, bk = fm_group(col0, 128)
                s = gi % 2
                act(tmpA[:, s, 0:nt], bank[:, 0:nt], AF.Exp, [bk], [("tmpA", s)], scale=-1.0)
                act(tmpA[:, s, 0:nt], tmpA[:, s, 0:nt], AF.Ln, [("tmpA", s)], [("tmpA", s)], bias=1.0)
                act(tmpA[:, s, 0:nt], tmpA[:, s, 0:nt], AF.Exp, [("tmpA", s)], [("tmpA", s)], scale=-1.0)
                tt("dve", gates[:, gi, 0:nt], bank[:, 0:nt], tmpA[:, s, 0:nt], ALU.mult, [bk, ("tmpA", s)], [("gate", gi)])
            _sec(6)
            for p_ in range(4):
                bank, bk = fm_group(C_QB + p_ * 128, 128)
                cp("act", qbT[0:64, p_, 0, 0:nt], bank[0:64, 0:nt], [bk], [("qbT", p_)])
                cp("act", qbT[64:128, p_, 1, 0:nt], bank[64:128, 0:nt], [bk], [("qbT", p_)])
            for p_ in range(4):
                bank, bk = fm_group(C_KB + p_ * 128, 128)
                cp("dve", kb_dst(p_), bank[:, 0:nt], [bk], kb_keys(p_))
            _sec(7)
            bank, bk = fm_group(C_KR, 128)
            tt("dve", tmpB[0:32, 1, 0:nt], bank[0:32, 0:nt], cosT, ALU.mult, [bk, TKEY], [("tmpB", 1)])
            tt("dve", sbb[0:32, 0, 0:nt], bank[32:64, 0:nt], sinT, ALU.mult, [bk, TKEY], [("sbb", 0)])
            tt("dve", kpe32[:, 0:nt], tmpB[0:32, 1, 0:nt], sbb[0:32, 0, 0:nt], ALU.add, [("tmpB", 1), ("sbb", 0)], [("rec", 1)])
            cp("dve", KTp_dst, kpe32[:, 0:nt], [("rec", 1)], kv_keys[1])

            def kpe_out_tr():
                bank, bk = nextbank((6,))
                for tti in range(ntile):
                    tr(bank[0:tsz, tti * 32:(tti + 1) * 32], kpe32[0:32, tti * tsz:(tti + 1) * tsz], identf[0:32, 0:32],
                       [("rec", 1), "identf"], [bk])
                cp("act", kpst[0:tsz, 0:ntile, :], bank[0:tsz, 0:ntile * 32].rearrange("p (a b) -> p a b", a=ntile), [bk], ["kpst"])
                dma(kpe_out, kpst[0:tsz, 0:ntile, :], ["kpst"], [], "o_kpe")
            _sec(8)
            pend_tr = [kpe_out_tr]
            for tti in range(ntile):
                tok = slice(tti * tsz, (tti + 1) * tsz)
                bank, bk = nextbank()
                for dc in range(8):
                    mm(bank[0:tsz, 0:128], xnT[:, dc, tok], W[:, dc, C_CKV:C_CKV + 128], dc == 0, dc == 7, WK + XK, [bk])
                c = smcol()
                ss = small[0:tsz, c:c + 1]
                act(ckvb[0:tsz, :], bank[0:tsz, 0:128], AF.Square, [bk], ["ckvb", ("sm", c)], accum_out=ss)
                r_ap, rk = rstd_from_ss(ss, ("sm", c), 128, tsz)
                s = tti % 2
                stt(ckv32[0:tsz, s, :], bank[0:tsz, 0:128], r_ap, gckv[0:tsz, :], ALU.mult, ALU.mult, [bk, rk, "gckv"], [("ckv32", s)])
                dma(ckv_out(tti), ckv32[0:tsz, s, :], [("ckv32", s)], [], ("o_ckv", s))
                cp("act", V_dst(tti), ckv32[0:tsz, s, :], [("ckv32", s)], [kv_keys[2](tti)])

                def do_tr(tti=tti):
                    tr(pT[:, 0:tsz], V_dst(tti), identb[0:tsz, 0:tsz], [kv_keys[2](tti), "identb"], [PTK])
                    cp("dve", KTc_dst(tti), pT[:, 0:tsz], [PTK], [kv_keys[0](tti)])
                bank, bk = nextbank()
                for dc in range(8):
                    mm(bank[0:tsz, :], xnT[:, dc, tok], W[:, dc, C_VB:C_VB + 512], dc == 0, dc == 7, WK + XK, [bk])
                cp("act", vbx_dst(tti)[:, :, 0:64], bank[0:tsz, :].rearrange("p (a b) -> p a b", a=8), [bk], [vbx_key(tti)])
                if vb_out is not None:
                    cp("dve", sbb[0:tsz, 0, :], bank[0:tsz, :], [bk], [("sbb", 0)])
                    dma(vb_out(tti), sbb[0:tsz, 0, :], [("sbb", 0)], [], ("o_kv", 0))
                if kb_out is not None:
                    bank, bk = nextbank()
                    for dc in range(8):
                        mm(bank[0:tsz, :], xnT[:, dc, tok], W[:, dc, C_KB:C_KB + 512], dc == 0, dc == 7, WK + XK, [bk])
                    cp("dve", sbb[0:tsz, 1, :], bank[0:tsz, :], [bk], [("sbb", 1)])
                    dma(kb_out(tti), sbb[0:tsz, 1, :], [("sbb", 1)], [], ("o_kv", 1))
                while pend_tr:
                    pend_tr.pop(0)()
                pend_tr.append(do_tr)
            while pend_tr:
                pend_tr.pop(0)()

        pt_rr = [0]
        sbank_rr = [0]
        ol_rr = [0]
        mla_pend = {"q": []}
        PV_LAG = 2

        def mla_chunk_steps(nq, jl, tiles):
            ncol = 8 * nq
            qcols = slice(jl * nq, (jl + 1) * nq)
            par = ol_rr[0] % 2
            cid = ol_rr[0]
            ol_rr[0] += 1
            psO, kO = pb[2 + par], PSK[2 + par]
            psL, kL = pb[4], PSK[4]
            QK = [("QO", h, jl) for h in range(8)] + [("Qp", h) for h in range(8)]
            nt_ = len(tiles)
            state = mla_pend

            def issue_pv(i, pi):
                ktc, ktp, v, M, keys = tiles[i][:5]
                mm(psO[:, 0:ncol], v, PT[0:M, pi, 0:ncol], i == 0, i == nt_ - 1, [("PT", pi)] + list(keys), [kO])

            def mk(i):
                def step():
                    ktc, ktp, v, M, keys = tiles[i][:5]
                    sbk = sbank_rr[0] % 2
                    sbank_rr[0] += 1
                    psS, kS = pb[sbk], PSK[sbk]
                    out_ap = psS[0:M, 0:ncol].rearrange("p (a b) -> p a b", a=8)
                    mm(out_ap, ktc, QlatT[:, :, qcols], True, False, QK + list(keys), [kS])
                    mm(out_ap, ktp, QpeT[:, :, qcols], False, True, QK + list(keys), [kS])
                    pi = pt_rr[0] % 4
                    pt_rr[0] += 1
                    if len(tiles[i]) > 5 and tiles[i][5]:
                        act(PT[0:M, pi, 0:ncol], psS[0:M, 0:ncol], AF.Exp, [kS, "maskv"], [("PT", pi)], scale=MLA_SCALE, bias=maskv[:, 0:1])
                    else:
                        act(PT[0:M, pi, 0:ncol], psS[0:M, 0:ncol], AF.Exp, [kS], [("PT", pi)], scale=MLA_SCALE)
                    e_ = i % 2
                    if i < 2 and M == 128:
                        cp("dve", acc[:, e_, 0:ncol], PT[:, pi, 0:ncol], [("PT", pi)], [("acc", e_)])
                    else:
                        if i < 2:
                            mset("pool", acc[:, e_, 0:ncol], 0.0, [("acc", e_)])
                        tt("dve", acc[0:M, e_, 0:ncol], acc[0:M, e_, 0:ncol], PT[0:M, pi, 0:ncol], ALU.add,
                           [("acc", e_), ("PT", pi)], [("acc", e_)])
                    state["q"].append((cid, lambda i=i, pi=pi: issue_pv(i, pi)))
                    while len(state["q"]) > PV_LAG:
                        state["q"].pop(0)[1]()
                return step

            def fin_a():
                for e_ in range(min(nt_, 2)):
                    cp("dve", acch[:, e_, 0:ncol], acc[:, e_, 0:ncol], [("acc", e_)], [("acch", e_)])

            def fin_b():
                while state["q"] and state["q"][0][0] <= cid:
                    state["q"].pop(0)[1]()
                ne = min(nt_, 2)
                for e_ in range(ne):
                    mm(psL[:, 0:ncol], onesb[:, :], acch[:, e_, 0:ncol], e_ == 0, e_ == ne - 1, ["onesb", ("acch", e_)], [kL])
                act(rec[:, par, 0:ncol], psL[:, 0:ncol], AF.Ln, [kL], [("rec", par)])
                act(rec[:, par, 0:ncol], rec[:, par, 0:ncol], AF.Exp, [("rec", par)], [("rec", par)], scale=-1.0)
                tt("dve", OlatT[:, :, qcols], psO[:, 0:ncol].rearrange("p (a b) -> p a b", a=8),
                   rec[:, par, 0:ncol].rearrange("p (a b) -> p a b", a=8), ALU.mult, [kO, ("rec", par)],
                   [("QO", h, jl) for h in range(8)])

            return [mk(i) for i in range(nt_)] + [fin_a], fin_b

        def mla_steps_for(chunks):
            flat = []
            pending = []
            for (nq, jl, tiles) in chunks:
                steps, fin_b = mla_chunk_steps(nq, jl, tiles)
                for k, stp in enumerate(steps):
                    if pending and k == min(2, len(steps) - 1):
                        flat.append(pending.pop(0))
                    flat.append(stp)
                pending.append(fin_b)

            def flush():
                while mla_pend["q"]:
                    mla_pend["q"].pop(0)[1]()
            flat.append(flush)
            flat.extend(pending)
            return flat

        def merge_a(nt, njl):
            for p_ in range(4):
                bank, bk = nextbank((0, 1))
                for hh in range(2):
                    h = 2 * p_ + hh
                    mm(bank[hh * 64:(hh + 1) * 64, 0:nt], Wuv[:, h * 64:(h + 1) * 64], OlatT[:, h, 0:nt], True, True,
                       ["Wuv"] + [("QO", h, jl) for jl in range(njl)], [bk])
                tt("dve", yT[:, p_, 0:nt], bank[:, 0:nt], gates[:, p_, 0:nt], ALU.mult, [bk, ("gate", p_)], XYfc(p_, max(1, nt // 128)))

        ptb_rr = [0]

        def band_steps_for(nt, heads_tiles):
            items = []
            for (h, tiles) in heads_tiles:
                for i, tl in enumerate(tiles):
                    items.append({"h": h, "t": tl, "first": i == 0, "last": i == len(tiles) - 1})
            N = len(items)

            def stage1(it):
                h = it["h"]
                p_ = h // 2
                pbase = 64 * (h % 2)
                kb_ap, vx_ap, M, q0, q1, qr0, keys = it["t"]
                n = q1 - q0
                mm(pT32[0:M, 0:n], kb_ap, qbT[:, p_, h % 2, q0:q1], True, True, [("qbT", p_)] + list(keys), [PTK])

            def stage2(it):
                h = it["h"]
                kb_ap, vx_ap, M, q0, q1, qr0, keys = it["t"]
                n = q1 - q0
                pi = ptb_rr[0] % 2
                ptb_rr[0] += 1
                it["pi"] = pi
                psS, kS = pT32, PTK
                lo, hi = qr0, qr0 + n
                if lo < 256:
                    a1 = min(hi, 256)
                    c1_ = a1 - lo
                    act(PTb[0:M, pi, 0:c1_], psS[0:M, 0:c1_], AF.Exp, [kS], [("PTb", pi, 0)], scale=B_SCALE)
                    tt("pool", PTb[0:M, pi, 0:c1_], PTb[0:M, pi, 0:c1_], BT[0:M, h, lo:a1], ALU.mult, [("PTb", pi, 0), "BT"], [("PTb", pi, 0)])
                b0, b1 = max(lo, 256), min(hi, 576)
                if b1 > b0:
                    act(PTb[0:M, pi, b0 - lo:b1 - lo], psS[0:M, b0 - lo:b1 - lo], AF.Exp, [kS, "cb"], [("PTb", pi, 1)],
                        scale=B_SCALE, bias=cb[0:M, h:h + 1])
                e0 = max(lo, 576)
                if hi > e0:
                    act(PTb[0:M, pi, e0 - lo:hi - lo], psS[0:M, e0 - lo:hi - lo], AF.Exp, [kS, "cbm"], [("PTb", pi, 2)],
                        scale=B_SCALE, bias=cbm[0:M, h:h + 1])

            def stage3(it):
                h = it["h"]
                kb_ap, vx_ap, M, q0, q1, qr0, keys = it["t"]
                n = q1 - q0
                pi = it["pi"]
                psOb, kOb = pb[5 + h % 2], PSK[5 + h % 2]
                PK = [("PTb", pi, 0), ("PTb", pi, 1), ("PTb", pi, 2)]
                mm(psOb[:, q0:q1], vx_ap, PTb[0:M, pi, 0:n], it["first"], it["last"], PK + list(keys), [kOb])

            def finalize(h):
                p_ = h // 2
                pbase = 64 * (h % 2)
                s = h % 2
                psOb, kOb = pb[5 + s], PSK[5 + s]
                act(tmpA[64:128, s, 0:nt], psOb[64:128, 0:nt], AF.Ln, [kOb], [("tmpA", s)])
                act(tmpA[64:128, s, 0:nt], tmpA[64:128, s, 0:nt], AF.Exp, [("tmpA", s)], [("tmpA", s)], scale=-1.0)
                tt("dve", tmpB[pbase:pbase + 64, s, 0:nt], psOb[0:64, 0:nt], tmpA[64:128, s, 0:nt], ALU.mult, [kOb, ("tmpA", s)], [("tmpB", s)])
                tt("pool", yT[pbase:pbase + 64, 4 + p_, 0:nt], tmpB[pbase:pbase + 64, s, 0:nt], gates[pbase:pbase + 64, 4 + p_, 0:nt],
                   ALU.mult, [("tmpB", s), ("gate", 4 + p_)], XYfc(4 + p_, max(1, nt // 128)))

            out = []
            for k in range(N + 2):
                def pre(k=k):
                    if 0 <= k - 1 < N:
                        stage2(items[k - 1])
                    if 0 <= k - 2 < N:
                        stage3(items[k - 2])

                def post(k=k):
                    if 0 <= k - 2 < N and items[k - 2]["last"]:
                        finalize(items[k - 2]["h"])
                    if k < N:
                        stage1(items[k])
                out.append((pre, post))
            return out

        def run_interleaved(mla_flat, band_flat):
            nm, nb = len(mla_flat), len(band_flat)
            mi = 0
            for k in range(nb):
                band_flat[k][0]()
                target = ((k + 1) * nm) // nb
                while mi < target:
                    mla_flat[mi]()
                    mi += 1
                band_flat[k][1]()
            while mi < nm:
                mla_flat[mi]()
                mi += 1

        def d_buffers(tsz):
            return [(stage[0:tsz, 0, 0:D], [("stage", 0)], ("wst", 0)),
                    (stage[0:tsz, 1, 0:D], [("stage", 1)], ("wst", 1)),
                    (sbb[0:tsz, :, :].rearrange("p a b -> p (a b)"), [("sbb", 0), ("sbb", 1)], "xd2"),
                    (rec[0:tsz, :, :].rearrange("p a b -> p (a b)"), [("rec", 0), ("rec", 1)], "xd3")]

        def d_prefetch(x_src, t0, tsz, which):
            bufs = d_buffers(tsz)
            out = {}
            for i in which:
                buf, keys, slot = bufs[i]
                dma(buf, x_src[t0 + i * tsz: t0 + (i + 1) * tsz, :], [], keys, slot)
                out[i] = True
            return out

        def d_tile_a(tti, tsz):
            bufs = d_buffers(tsz)
            tok = slice(tti * tsz, (tti + 1) * tsz)
            xt, xkeys, slot = bufs[tti]
            banks = [nextbank((0, 1, 2, 3)), nextbank((0, 1, 2, 3))]
            for half in range(2):
                bank, bk = banks[half]
                for fc in range(8):
                    mm(bank[0:tsz, :], yT[:, fc, tok], Wout[:, fc, half * 512:(half + 1) * 512], fc == 0, fc == 7, XYt(tti) + WOK, [bk])
            for half in range(2):
                bank, bk = banks[half]
                tt("dve", xt[:, half * 512:(half + 1) * 512], bank[0:tsz, :], xt[:, half * 512:(half + 1) * 512], ALU.add, [bk] + xkeys, xkeys)

        def d_tile_b(tti, tsz, y_out):
            bufs = d_buffers(tsz)
            xt, xkeys, slot = bufs[tti]
            c = smcol()
            ss = small[0:tsz, c:c + 1]
            jk = PT[0:tsz, 0:2, :].rearrange("p a b -> p (a b)")
            act(jk, xt, AF.Square, xkeys, [("PT", 0), ("PT", 1), ("sm", c)], accum_out=ss)
            r_ap, rk = rstd_from_ss(ss, ("sm", c), D, tsz)
            stt(xt, xt, r_ap, gfin[0:tsz, :], ALU.mult, ALU.mult, xkeys + [rk, "gfin"], xkeys)
            dma(y_out(tti), xt, xkeys, [], ("o_y", tti))

        def d_tile(tti, tsz, y_out):
            d_tile_a(tti, tsz)
            d_tile_b(tti, tsz, y_out)

        def phase_d(x_src, t0, nt, tsz, y_out, pre=None):
            ntile = nt // tsz
            bufs = d_buffers(tsz)
            pre = pre or {}
            for tti in range(ntile):
                if tti not in pre:
                    buf, keys, slot = bufs[tti]
                    dma(buf, x_src[t0 + tti * tsz: t0 + (tti + 1) * tsz, :], [], keys, slot)
            for tti in range(ntile):
                d_tile(tti, tsz, y_out)

        if SAMPLE:
            cst3 = cst.rearrange("p (t c) -> p t c", c=128)
            for g in range(4):
                dma(cst3, cckv_d[g * 1024:(g + 1) * 1024, :].rearrange("(t p) c -> p t c", p=128), [], TA2, "cin")
                cp("pool" if g % 2 else "dve", V[:, g * 8:(g + 1) * 8, :], cst3, TA2, [("V", g * 8 + i) for i in range(8)])
                for i in range(8):
                    t = g * 8 + i
                    tr(pT[:, i * 128:(i + 1) * 128], V[:, t, :], identb[:], [("V", t), "identb"], [PTK])
                cp("act", KTc[:, g * 1024:(g + 1) * 1024], pT[:, :], [PTK], [("KTc", g * 8 + i) for i in range(8)])
            for g in range(4):
                cv = cst[:, 0:256].rearrange("p (t r) -> p t r", r=32)
                cvb_ = cstb[:, 0:256].rearrange("p (t r) -> p t r", r=32)
                dma(cv, ckpe_d[g * 1024:(g + 1) * 1024, :].rearrange("(t p) r -> p t r", p=128), [], TA2, "cin")
                cp("dve", cvb_, cv, TA2, SQ2)
                for i in range(8):
                    tr(pT[0:32, i * 128:(i + 1) * 128], cvb_[:, i, :], identb[:], SQ2 + ["identb"], [PTK])
                cp("act", KTp[0:32, g * 1024:(g + 1) * 1024], pT[0:32, :], [PTK], [("KTp", g * 8 + i) for i in range(8)])
            for g in range(2):
                cv = cst.rearrange("p (t n) -> p t n", n=512)
                cvb_ = cstb.rearrange("p (t n) -> p t n", n=512)
                dma(cv, ckb_d[g * 256:(g + 1) * 256, :].rearrange("(t p) n -> p t n", p=128), [], TA2, "cin")
                cp("dve", cvb_, cv, TA2, SQ2)
                for i in range(2):
                    tmi = g * 2 + i
                    for p_ in range(4):
                        tr(pT[:, p_ * 128:(p_ + 1) * 128], cvb_[:, i, p_ * 128:(p_ + 1) * 128], identb[:], SQ2 + ["identb"], [PTK])
                    cp("act", kbT[:, :, 1, tmi * 128:(tmi + 1) * 128], pT[:, 0:512].rearrange("p (a b) -> p a b", a=4), [PTK],
                       [("kbT", 1, p_, tmi) for p_ in range(4)])
            for g in range(2):
                cv = cst.rearrange("p (t n) -> p t n", n=512)
                dma(cv, cvb_d[g * 256:(g + 1) * 256, :].rearrange("(t p) n -> p t n", p=128), [], TA2, "cin")
                for i in range(2):
                    tmi = g * 2 + i
                    cp("dve", vbx[:, 4 + tmi, :, 0:64], cv[:, i, :].rearrange("p (a b) -> p a b", a=8), TA2, [("vbx", 4 + tmi)])

            phase_a(xs_d, 0, TS, TS, SEQ,
                    KTc_dst=lambda tti: KTc_s[:, 0:TS], KTp_dst=KTp_s[0:32, 0:TS], V_dst=lambda tti: V_s[0:TS, :],
                    kv_keys=(lambda tti: "KTc_s", ["KTp_s"], lambda tti: "V_s"),
                    kb_dst=lambda p_: kbT_s[:, p_, 0:TS], kb_keys=lambda p_: [("kbT_s", p_)],
                    vbx_dst=lambda tti: vbx_s[0:TS, :, :], vbx_key=lambda tti: "vbx_s",
                    ckv_out=lambda tti: ckvs_o, kpe_out=kpes_o.rearrange("(t p) r -> p t r", p=TS),
                    kb_out=lambda tti: kbs_o, vb_out=lambda tti: vbs_o)
            tiles = []
            for kt in range(32):
                tiles.append((KTc[:, kt * 128:(kt + 1) * 128], KTp[:, kt * 128:(kt + 1) * 128], V[:, kt, :], 128,
                              [("KTc", kt), ("KTp", kt), ("V", kt)]))
            tiles.append((KTc_s[:, 0:TS], KTp_s[:, 0:TS], V_s[0:TS, :], TS, ["KTc_s", "KTp_s", "V_s"]))
            mla_flat = mla_steps_for([(TS, 0, tiles)])
            heads_tiles = []
            for h in range(8):
                p_ = h // 2
                pbase = 64 * (h % 2)
                tiles = []
                for tmi in range(4):
                    tiles.append((kbT[:, p_, 1, tmi * 128:(tmi + 1) * 128], vbx[:, 4 + tmi, h, :], 128, 0, TS,
                                  512 - 128 * tmi, [("kbT", 1, p_, tmi), ("vbx", 4 + tmi)]))
                tiles.append((kbT_s[:, p_, 0:TS], vbx_s[0:TS, h, :], TS, 0, TS, 0, [("kbT_s", p_), "vbx_s"]))
                heads_tiles.append((h, tiles))
            run_interleaved(mla_flat, band_steps_for(TS, heads_tiles))
            merge_a(TS, 1)
            phase_d(xs_d, 0, TS, TS, lambda tti: ys_o)

        for b in range(NBLK if DBG_STAGE >= 1 else 0):
            t0 = 512 * b
            slot = b % 2
            last = (b == SEQ // 512 - 1)
            try:
              phase_a(x_d, t0, 512, 128, t0,
                    KTc_dst=lambda tti, t0=t0: KTc[:, t0 + tti * 128: t0 + (tti + 1) * 128],
                    KTp_dst=KTp[0:32, t0:t0 + 512],
                    V_dst=lambda tti, b=b: V[:, 4 * b + tti, :],
                    kv_keys=(lambda tti, b=b: ("KTc", 4 * b + tti), [("KTp", 4 * b + i) for i in range(4)], lambda tti, b=b: ("V", 4 * b + tti)),
                    kb_dst=lambda p_, slot=slot: kbT[:, p_, slot, :],
                    kb_keys=lambda p_, slot=slot: [("kbT", slot, p_, tmi) for tmi in range(4)],
                    vbx_dst=lambda tti, slot=slot: vbx[:, 4 * slot + tti, :, :],
                    vbx_key=lambda tti, slot=slot: ("vbx", 4 * slot + tti),
                    ckv_out=lambda tti, t0=t0: ckv_o[t0 + tti * 128: t0 + (tti + 1) * 128, :],
                    kpe_out=kpe_o[t0:t0 + 512, :].rearrange("(t p) r -> p t r", p=128),
                    kb_out=(lambda tti: kb_o[tti * 128:(tti + 1) * 128, :]) if last else None,
                    vb_out=(lambda tti: vb_o[tti * 128:(tti + 1) * 128, :]) if last else None,
                    front_done=(b > 0 and DBG_STAGE >= 4))
            except _Stop:
                pass
            pre = d_prefetch(x_d, t0, 128, [0, 1, 2]) if DBG_STAGE >= 4 else None
            chunks = []
            for jl in range(8 if DBG_STAGE >= 2 else 0):
                j = 8 * b + jl
                tiles = []
                for kt in range(j // 2 + 1):
                    half = (kt == j // 2 and j % 2 == 0)
                    tiles.append((KTc[:, kt * 128:(kt + 1) * 128], KTp[:, kt * 128:(kt + 1) * 128], V[:, kt, :], 128,
                                  [("KTc", kt), ("KTp", kt), ("V", kt)], half))
                chunks.append((64, jl, tiles))
            mla_flat = mla_steps_for(chunks)
            ms = [m for m in range(4 * b - 4, 4 * b + 4) if m >= 0]
            first = 4 * b - 1 if b > 0 else 0
            ms = [first] + [m for m in ms if m != first]
            heads_tiles = []
            for h in range(8 if DBG_STAGE >= 3 else 0):
                p_ = h // 2
                pbase = 64 * (h % 2)
                tiles = []
                for m in ms:
                    bm, tmi = m // 4, m % 4
                    sl = bm % 2
                    c_lo = max(2 * m, 8 * b)
                    c_hi = min(2 * m + 9, 8 * b + 7)
                    q0 = (c_lo - 8 * b) * 64
                    q1 = (c_hi - 8 * b + 1) * 64
                    qr0 = (c_lo - 2 * m) * 64
                    tiles.append((kbT[:, p_, sl, tmi * 128:(tmi + 1) * 128], vbx[:, 4 * sl + tmi, h, :], 128, q0, q1, qr0,
                                  [("kbT", sl, p_, tmi), ("vbx", 4 * sl + tmi)]))
                heads_tiles.append((h, tiles))
            run_interleaved(mla_flat, band_steps_for(512, heads_tiles))
            if DBG_STAGE >= 2:
                merge_a(512, 8)
            if DBG_STAGE >= 4:
                nxt = (b + 1 < NBLK)
                a_bufs = [(tmpA[:, :, :].rearrange("p a b -> p (a b)"), TA2, "xa0"), (tmpB[:, :, :].rearrange("p a b -> p (a b)"), TB2, "xa1"),
                          (stage[:, 0, 0:D], [("stage", 0)], ("wst", 0)), (stage[:, 1, 0:D], [("stage", 1)], ("wst", 1))]
                t1 = t0 + 512
                if 3 not in pre:
                    buf, keys, slot = d_buffers(128)[3]
                    dma(buf, x_d[t0 + 384: t0 + 512, :], [], keys, slot)
                if nxt:
                    for i_ in range(2):
                        buf, keys, slot = a_bufs[i_]
                        dma(buf, x_d[t1 + i_ * 128: t1 + (i_ + 1) * 128, :], [], keys, slot)
                yo = lambda tti, t0=t0: y_o[t0 + tti * 128: t0 + (tti + 1) * 128, :]
                for tti in range(5):
                    if tti < 4:
                        d_tile_a(tti, 128)
                    if nxt and tti >= 1:
                        buf, keys, slot = a_bufs[tti - 1]
                        a_front_tile(buf, keys, 128, tti - 1)
                    if tti < 4:
                        d_tile_b(tti, 128, yo)
                        if nxt and tti < 2:
                            buf, keys, slot = a_bufs[2 + tti]
                            dma(buf, x_d[t1 + (2 + tti) * 128: t1 + (3 + tti) * 128, :], [], keys, slot)

        n = P.emit(nc)
    return nc, n


def _prep_shared(w_in, g_mix, g_cq, w_uq, g_ckv, w_uk, w_uv, rel_bias, w_out, g_final):
    f = np.float32
    wi = np.asarray(w_in[0], f)
    perm = np.r_[16:32, 0:16]
    kr = wi[:, 384:416]
    cols = [wi[:, 0:256], wi[:, 416:928], wi[:, 2464:2976], wi[:, 928:1440], wi[:, 1440:1952], kr, kr[:, perm],
            wi[:, 256:384], wi[:, 1952:2464]]
    w_perm = np.ascontiguousarray(np.concatenate(cols, axis=1))
    assert w_perm.shape == (D, NCOL)
    wuq = np.asarray(w_uq[0], f)
    uqnT = np.ascontiguousarray(wuq[:, :, :64].transpose(2, 1, 0)).reshape(64, 8 * 256)
    ukT = np.ascontiguousarray(np.asarray(w_uk[0], f).transpose(2, 1, 0)).reshape(64, 8 * 128)
    pe = wuq[:, :, 64:96]
    wpe = np.ascontiguousarray(np.concatenate([pe, pe[:, :, perm]], axis=2)).reshape(256, 512)
    half = 16
    inv = (10000.0 ** (-np.arange(half, dtype=np.float64) / half))
    ropec = np.zeros((64, 4), np.float64)
    for r in range(32):
        ropec[r, 0] = inv[r % 16] / (2 * np.pi)
        ropec[r, 1] = 0.25
        ropec[r, 2] = 2 * np.pi
        ropec[32 + r, 0] = inv[r % 16] / (2 * np.pi)
        ropec[32 + r, 1] = 0.0
        ropec[32 + r, 2] = (-2 * np.pi) if r < 16 else (2 * np.pi)
    return {
        "w_perm": w_perm,
        "w_out": np.ascontiguousarray(np.asarray(w_out[0], f)),
        "wuv": np.ascontiguousarray(np.asarray(w_uv[0], f).reshape(128, 512)),
        "uqnT": uqnT, "ukT": ukT, "wpe": wpe,
        "gmix": np.ascontiguousarray(np.asarray(g_mix[0], f).reshape(8, 128).T),
        "gcq": np.ascontiguousarray(np.asarray(g_cq[0], f).reshape(2, 128).T),
        "gckv_bc": np.ascontiguousarray(np.broadcast_to(np.asarray(g_ckv[0], f)[None, :], (128, 128))),
        "gfin_bc": np.ascontiguousarray(np.broadcast_to(np.asarray(g_final, f)[None, :], (128, D))),
        "ropec": ropec.astype(f),
        "rb": np.ascontiguousarray(np.asarray(rel_bias[0], f)),
        "ident": np.eye(128, dtype=f),
    }


_NC_CACHE = {}


def kernel(x_prompt, x_sample, cache_ckv, cache_kpe, cache_kb, cache_vb, w_in, g_mix, g_cq, w_uq, g_ckv, w_uk, w_uv,
           rel_bias, w_out, g_final, _nblk=8, _sample=True):
    f = np.float32
    shared = _prep_shared(w_in, g_mix, g_cq, w_uq, g_ckv, w_uk, w_uv, rel_bias, w_out, g_final)
    key = (_nblk, _sample)
    if key not in _NC_CACHE:
        _NC_CACHE[key] = build_nc(_nblk, _sample)[0]
    nc = _NC_CACHE[key]
    in_maps = []
    for i in range(8):
        m = dict(shared)
        m["x"] = np.ascontiguousarray(np.asarray(x_prompt[i], f))
        m["xs"] = np.ascontiguousarray(np.asarray(x_sample[i], f))
        m["cckv"] = np.ascontiguousarray(np.asarray(cache_ckv[0, i], f))
        m["ckpe"] = np.ascontiguousarray(np.asarray(cache_kpe[0, i], f))
        m["ckb"] = np.ascontiguousarray(np.asarray(cache_kb[0, i], f).reshape(512, 512))
        m["cvb"] = np.ascontiguousarray(np.asarray(cache_vb[0, i], f).reshape(512, 512))
        in_maps.append(m)
    res = run_bass_kernel_spmd(nc, in_maps, core_ids=list(range(8)))
    R = res.results
    st = lambda k: np.stack([np.asarray(R[i][k], f) for i in range(8)], axis=0)
    y = st("y")
    ys = st("ys")
    return (y, ys,
            st("ckv_o")[None], st("kpe_o")[None],
            st("kb_o").reshape(8, 512, 8, 64)[None], st("vb_o").reshape(8, 512, 8, 64)[None],
            st("ckvs_o")[None], st("kpes_o")[None],
            st("kbs_o").reshape(8, TS, 8, 64)[None], st("vbs_o").reshape(8, TS, 8, 64)[None])
```
